# Optimizing a Trainium2 kernel written in Bass

```python
import math
import jax, jax.numpy as jnp
from jax import lax
import numpy as np

D_MODEL = 1024
BATCH = 16
SEQ = 2048
DEPTH = 1

MEM_LEN = 256
EPS = 1e-6
ROPE_THETA = 10000.0
NEG = -1e30
ATT_HEAD_DIM = 64
ATT_SLOTS = 8
DIL_GROUPS = ((128, 1), (512, 4), (2048, 16))
N_GROUPS = 3
ATT_HEADS = ATT_SLOTS * N_GROUPS
ATT_QKV = ATT_HEADS * ATT_HEAD_DIM
ATT_OUT = ATT_SLOTS * ATT_HEAD_DIM
ML_HEADS = 4
ML_HEAD_DIM = 128
ML_WIDTH = ML_HEADS * ML_HEAD_DIM
ML_GATES = 4 * ML_HEADS
ML_CHUNK = 128
ML_CONV = 5
CX_HEADS = 4
CX_HEAD_DIM = 128
CX_WIDTH = CX_HEADS * CX_HEAD_DIM
N_BRANCH = 3
N_IN = 3 * ATT_QKV + 4 * ML_WIDTH + ML_GATES + CX_WIDTH + N_BRANCH * D_MODEL
D_FF = 2816

kernel_name = "hybrid_dilated_mlstm_memxattn_block"


def rmsnorm(x, g):
    xf = x.astype(jnp.float32)
    y = xf * lax.rsqrt(jnp.mean(xf * xf, axis=-1, keepdims=True) + EPS)
    return (y * g.astype(jnp.float32)).astype(x.dtype)


def swiglu(h, w_in, w_out):
    gate, up = jnp.split(h @ w_in, 2, axis=-1)
    return (jax.nn.silu(gate) * up) @ w_out


def rope(x):
    s, d = x.shape[1], x.shape[-1]
    inv = ROPE_THETA ** (-jnp.arange(0, d, 2, dtype=jnp.float32) / d)
    ang = jnp.arange(s, dtype=jnp.float32)[:, None] * inv[None, :]
    cos = jnp.cos(ang)[None, :, None, :]
    sin = jnp.sin(ang)[None, :, None, :]
    xf = x.astype(jnp.float32)
    x1, x2 = jnp.split(xf, 2, axis=-1)
    return jnp.concatenate([x1 * cos - x2 * sin, x2 * cos + x1 * sin], axis=-1).astype(x.dtype)


def banded_attention(q, k, v, radius):
    blk = radius
    n, d = q.shape[-2], q.shape[-1]
    lead = q.shape[:-2]
    nb = -(-n // blk)
    npad = nb * blk
    zl = [(0, 0)] * len(lead)
    qb = jnp.pad(q, zl + [(0, npad - n), (0, 0)]).reshape(*lead, nb, blk, d)
    kp = jnp.pad(k, zl + [(blk, npad - n + blk), (0, 0)]).reshape(*lead, nb + 2, blk, d)
    vp = jnp.pad(v, zl + [(blk, npad - n + blk), (0, 0)]).reshape(*lead, nb + 2, blk, d)
    kw = jnp.concatenate([kp[..., 0:nb, :, :], kp[..., 1:nb + 1, :, :], kp[..., 2:nb + 2, :, :]], axis=-2)
    vw = jnp.concatenate([vp[..., 0:nb, :, :], vp[..., 1:nb + 1, :, :], vp[..., 2:nb + 2, :, :]], axis=-2)
    s = jnp.einsum('...nqd,...nkd->...nqk', qb, kw).astype(jnp.float32)
    qpos = jnp.arange(nb)[:, None] * blk + jnp.arange(blk)[None, :]
    kpos = jnp.arange(nb)[:, None] * blk - blk + jnp.arange(3 * blk)[None, :]
    valid = ((jnp.abs(qpos[:, :, None] - kpos[:, None, :]) <= radius)
             & (kpos[:, None, :] >= 0) & (kpos[:, None, :] < n))
    s = jnp.where(valid, s, NEG)
    m = jnp.max(s, axis=-1, keepdims=True)
    p = jnp.exp(s - m)
    l = jnp.sum(p, axis=-1)
    out = jnp.einsum('...nqk,...nkd->...nqd', p.astype(v.dtype), vw).astype(jnp.float32) / l[..., None]
    lse = m[..., 0] + jnp.log(l)
    out = out.reshape(*lead, npad, d)[..., :n, :]
    lse = lse.reshape(*lead, npad)[..., :n]
    return out, lse


def dilated_attention(q, k, v):
    b, s, _, d = q.shape
    outs, lses = [], []
    for g, (w, r) in enumerate(DIL_GROUPS):
        sl = slice(g * ATT_SLOTS, (g + 1) * ATT_SLOTS)

        def to_res(t):
            return t[:, :, sl].reshape(b, s // r, r, ATT_SLOTS, d).transpose(0, 3, 2, 1, 4)

        o, lse = banded_attention(to_res(q), to_res(k), to_res(v), (w // 2) // r)
        outs.append(o.transpose(0, 3, 2, 1, 4).reshape(b, s, ATT_SLOTS, d))
        lses.append(lse.transpose(0, 3, 2, 1).reshape(b, s, ATT_SLOTS))
    alpha = jax.nn.softmax(jnp.stack(lses, axis=0), axis=0)
    out = jnp.sum(alpha[..., None] * jnp.stack(outs, axis=0), axis=0)
    return out.reshape(b, s, ATT_OUT)


def centred_dwconv(x, w, bias):
    c = x.shape[-1]
    y = lax.conv_general_dilated(x, w[:, None, :].astype(x.dtype), window_strides=(1,), padding='SAME',
                                 dimension_numbers=('NWC', 'WIO', 'NWC'), feature_group_count=c)
    return y + bias.astype(x.dtype)


def mlstm_chunkwise(q, k, v, log_i, log_f):
    b, h, s, d = q.shape
    L = ML_CHUNK
    nc = s // L
    chunk = lambda t: jnp.moveaxis(t.reshape(b, h, nc, L, *t.shape[3:]), 2, 0)
    tril = jnp.tril(jnp.ones((L, L), dtype=bool))

    def step(carry, xs):
        C, n, m = carry
        qc, kc, vc, li, lf = xs
        bcum = jnp.cumsum(lf, axis=-1)
        gtot = bcum[..., -1]
        dmat = jnp.where(tril, bcum[..., :, None] - bcum[..., None, :] + li[..., None, :], NEG)
        inter = bcum + m[..., None]
        mt = jnp.maximum(jnp.max(dmat, axis=-1), inter)
        wts = jnp.exp(dmat - mt[..., None])
        w_inter = jnp.exp(inter - mt)
        sc = jnp.einsum('bhtd,bhsd->bhts', qc, kc) * wts
        num = jnp.einsum('bhts,bhsd->bhtd', sc, vc) + w_inter[..., None] * jnp.einsum('bhvk,bhtk->bhtv', C, qc)
        den = jnp.sum(sc, axis=-1) + w_inter * jnp.einsum('bhk,bhtk->bht', n, qc)
        hout = num / jnp.maximum(jnp.abs(den), jnp.exp(-mt))[..., None]
        a = gtot[..., None] - bcum + li
        m_new = jnp.maximum(gtot + m, jnp.max(a, axis=-1))
        decay = jnp.exp(gtot + m - m_new)
        wk = jnp.exp(a - m_new[..., None])
        C_new = decay[..., None, None] * C + jnp.einsum('bhs,bhsv,bhsk->bhvk', wk, vc, kc)
        n_new = decay[..., None] * n + jnp.einsum('bhs,bhsk->bhk', wk, kc)
        return (C_new, n_new, m_new), hout

    init = (jnp.zeros((b, h, d, d), jnp.float32), jnp.zeros((b, h, d), jnp.float32),
            jnp.zeros((b, h), jnp.float32))
    _, hs = lax.scan(step, init, (chunk(q), chunk(k), chunk(v), chunk(log_i), chunk(log_f)))
    return jnp.moveaxis(hs, 0, 2).reshape(b, h, s, d)


def setup_inputs(seed: int = 0) -> dict:
    key = jax.random.key(seed)
    ks = iter(jax.random.split(key, 32))
    f32 = jnp.float32
    nrm = lambda shape, fan_in: jax.random.normal(next(ks), shape, f32) * (fan_in ** -0.5)
    gain = lambda shape: 1.0 + 0.02 * jax.random.normal(next(ks), shape, f32)
    small = lambda shape: 0.02 * jax.random.normal(next(ks), shape, f32)
    x = jax.random.normal(next(ks), (BATCH, SEQ, D_MODEL), f32)
    mem = jax.random.normal(next(ks), (BATCH, MEM_LEN, D_MODEL), f32)
    norm_ffn1 = gain((DEPTH, D_MODEL))
    w_ffn1_in = nrm((DEPTH, D_MODEL, 2 * D_FF), D_MODEL)
    w_ffn1_out = nrm((DEPTH, D_FF, D_MODEL), D_FF)
    norm_mix = gain((DEPTH, D_MODEL))
    norm_mem = gain((DEPTH, D_MODEL))
    w_in = nrm((DEPTH, D_MODEL, N_IN), D_MODEL)
    att_q_gain = gain((DEPTH, ATT_HEAD_DIM))
    att_k_gain = gain((DEPTH, ATT_HEAD_DIM))
    ml_conv_w = nrm((DEPTH, ML_CONV, 2 * ML_WIDTH), ML_CONV)
    ml_conv_b = small((DEPTH, 2 * ML_WIDTH))
    forget_offset = (jnp.array([0.0, 1.0, 0.0, 1.0], f32)[:, None]
                     * jnp.linspace(3.0, 6.0, ML_HEADS, dtype=f32)[None, :])
    ml_gate_b = (0.1 * jax.random.normal(next(ks), (DEPTH, 4, ML_HEADS), f32)
                 + forget_offset[None]).reshape(DEPTH, ML_GATES)
    ml_out_gain = gain((DEPTH, ML_WIDTH))
    cx_q_gain = gain((DEPTH, CX_HEAD_DIM))
    cx_k_gain = gain((DEPTH, CX_HEAD_DIM))
    w_mem_kv = nrm((DEPTH, D_MODEL, 2 * CX_WIDTH), D_MODEL)
    mix_gate_b = small((DEPTH, N_BRANCH * D_MODEL))
    w_br_att = nrm((DEPTH, ATT_OUT, D_MODEL), ATT_OUT)
    w_br_ml = nrm((DEPTH, ML_WIDTH, D_MODEL), ML_WIDTH)
    w_br_cx = nrm((DEPTH, CX_WIDTH, D_MODEL), CX_WIDTH)
    w_out = nrm((DEPTH, D_MODEL, D_MODEL), D_MODEL)
    norm_ffn2 = gain((DEPTH, D_MODEL))
    w_ffn2_in = nrm((DEPTH, D_MODEL, 2 * D_FF), D_MODEL)
    w_ffn2_out = nrm((DEPTH, D_FF, D_MODEL), D_FF)
    norm_final = gain((DEPTH, D_MODEL))
    return {"x": x, "mem": mem, "norm_ffn1": norm_ffn1, "w_ffn1_in": w_ffn1_in, "w_ffn1_out": w_ffn1_out,
            "norm_mix": norm_mix, "norm_mem": norm_mem, "w_in": w_in, "att_q_gain": att_q_gain,
            "att_k_gain": att_k_gain, "ml_conv_w": ml_conv_w, "ml_conv_b": ml_conv_b, "ml_gate_b": ml_gate_b,
            "ml_out_gain": ml_out_gain, "cx_q_gain": cx_q_gain, "cx_k_gain": cx_k_gain, "w_mem_kv": w_mem_kv,
            "mix_gate_b": mix_gate_b, "w_br_att": w_br_att, "w_br_ml": w_br_ml, "w_br_cx": w_br_cx,
            "w_out": w_out, "norm_ffn2": norm_ffn2, "w_ffn2_in": w_ffn2_in, "w_ffn2_out": w_ffn2_out,
            "norm_final": norm_final}


def reference(x, mem, norm_ffn1, w_ffn1_in, w_ffn1_out, norm_mix, norm_mem, w_in, att_q_gain, att_k_gain,
              ml_conv_w, ml_conv_b, ml_gate_b, ml_out_gain, cx_q_gain, cx_k_gain, w_mem_kv, mix_gate_b,
              w_br_att, w_br_ml, w_br_cx, w_out, norm_ffn2, w_ffn2_in, w_ffn2_out, norm_final):
    b, s, dm = x.shape
    mlen = mem.shape[1]
    sizes = (ATT_QKV, ATT_QKV, ATT_QKV, ML_WIDTH, ML_WIDTH, ML_WIDTH, ML_WIDTH, ML_GATES, CX_WIDTH,
             N_BRANCH * D_MODEL)
    split_pts = np.cumsum(np.array(sizes))[:-1].tolist()
    for l in range(DEPTH):
        x = x + 0.5 * swiglu(rmsnorm(x, norm_ffn1[l]), w_ffn1_in[l], w_ffn1_out[l])
        h = rmsnorm(x, norm_mix[l])
        aq, ak, av, mq, mk, mv, mo, mif, cq, gts = jnp.split(h @ w_in[l], split_pts, axis=-1)

        ah = lambda t: t.reshape(b, s, ATT_HEADS, ATT_HEAD_DIM)
        aq = rope(rmsnorm(ah(aq), att_q_gain[l])) * (ATT_HEAD_DIM ** -0.5)
        ak = rope(rmsnorm(ah(ak), att_k_gain[l]))
        y_att = dilated_attention(aq, ak, ah(av)).astype(x.dtype)

        mqk = jax.nn.silu(centred_dwconv(jnp.concatenate([mq, mk], axis=-1), ml_conv_w[l], ml_conv_b[l]))
        mq, mk = jnp.split(mqk, 2, axis=-1)
        mh = lambda t: t.reshape(b, s, ML_HEADS, ML_HEAD_DIM).transpose(0, 2, 1, 3).astype(jnp.float32)
        q_m = mh(mq) * (ML_HEAD_DIM ** -0.5)
        k_m, v_m = mh(mk), mh(mv)
        gates = (mif + ml_gate_b[l]).astype(jnp.float32).reshape(b, s, 4, ML_HEADS).transpose(2, 0, 3, 1)
        li_f, fp_f, li_b, fp_b = gates[0], gates[1], gates[2], gates[3]
        h_f = mlstm_chunkwise(q_m, k_m, v_m, li_f, jax.nn.log_sigmoid(fp_f))
        fl = lambda t: jnp.flip(t, axis=2)
        h_b = fl(mlstm_chunkwise(fl(q_m), fl(k_m), fl(v_m), fl(li_b), fl(jax.nn.log_sigmoid(fp_b))))
        hm = (h_f + h_b).transpose(0, 2, 1, 3)
        hm = hm * lax.rsqrt(jnp.mean(hm * hm, axis=-1, keepdims=True) + EPS)
        hm = hm.reshape(b, s, ML_WIDTH) * ml_out_gain[l].astype(jnp.float32)
        y_ml = (hm * jax.nn.sigmoid(mo.astype(jnp.float32))).astype(x.dtype)

        cq = rmsnorm(cq.reshape(b, s, CX_HEADS, CX_HEAD_DIM), cx_q_gain[l]) * (CX_HEAD_DIM ** -0.5)
        ck, cv = jnp.split(rmsnorm(mem, norm_mem[l]) @ w_mem_kv[l], 2, axis=-1)
        ck = rmsnorm(ck.reshape(b, mlen, CX_HEADS, CX_HEAD_DIM), cx_k_gain[l])
        cv = cv.reshape(b, mlen, CX_HEADS, CX_HEAD_DIM)
        p = jax.nn.softmax(jnp.einsum('bqhd,bkhd->bhqk', cq, ck).astype(jnp.float32), axis=-1)
        y_cx = jnp.einsum('bhqk,bkhd->bqhd', p.astype(cv.dtype), cv).reshape(b, s, CX_WIDTH).astype(x.dtype)

        g = jax.nn.sigmoid((gts + mix_gate_b[l]).astype(jnp.float32)).astype(x.dtype).reshape(b, s, N_BRANCH, dm)
        merged = (g[:, :, 0] * (y_att @ w_br_att[l]) + g[:, :, 1] * (y_ml @ w_br_ml[l])
                  + g[:, :, 2] * (y_cx @ w_br_cx[l]))
        x = x + merged @ w_out[l]

        x = x + 0.5 * swiglu(rmsnorm(x, norm_ffn2[l]), w_ffn2_in[l], w_ffn2_out[l])
        x = rmsnorm(x, norm_final[l])
    return x
```

```python
import numpy as np
import concourse.bass as bass
import concourse.mybir as mybir
from concourse.bass_utils import run_bass_kernel_spmd

F32 = mybir.dt.float32
BF16 = mybir.dt.bfloat16
AF = mybir.ActivationFunctionType
ALU = mybir.AluOpType
AX = mybir.AxisListType

_WRITE_KEYS = ("out", "accum_out", "ap")
_DT_SIZE = {}


def _dsize(dt):
    s = _DT_SIZE.get(dt)
    if s is None:
        s = mybir.dt.size(dt)
        _DT_SIZE[dt] = s
    return s


def _region(ap):
    sp = str(ap.space)
    if sp == "DRAM":
        return None
    pairs = ap.ap
    pstep, pcnt = pairs[0]
    off = ap.offset
    es = _dsize(ap.dtype)
    if pstep == 0:
        p0 = 0
        fo = off
    else:
        p0 = off // pstep
        fo = off - p0 * pstep
    lo = fo
    hi = fo
    for st, cnt in pairs[1:]:
        if cnt <= 0:
            continue
        ext = st * (cnt - 1)
        if ext >= 0:
            hi += ext
        else:
            lo += ext
    return (ap.tensor.name, p0, p0 + pcnt, lo * es, (hi + 1) * es)


def _overlap(a, b):
    return a[1] < b[2] and b[1] < a[2] and a[3] < b[4] and b[3] < a[4]


def _covers(a, b):
    return a[1] <= b[1] and a[2] >= b[2] and a[3] <= b[3] and a[4] >= b[4]


class _Rec:
    __slots__ = ("eng", "name", "kwargs", "deps", "inc", "dma", "dsem", "dval", "seq", "cnt", "pre", "label")

    def __init__(self, eng, name, kwargs, dma):
        self.eng = eng
        self.name = name
        self.kwargs = kwargs
        self.deps = []
        self.inc = False
        self.dma = dma
        self.dsem = None
        self.dval = 0
        self.cnt = 0
        self.pre = None


class _Proxy:
    def __init__(self, sched, key):
        self._s = sched
        self._k = key

    def __getattr__(self, name):
        def f(**kwargs):
            return self._s._add(self._k, name, kwargs)
        return f


class Sched:
    ENGS = ("pe", "act", "dve", "pool", "sp")

    def __init__(self, nc, n_dma_sems=24, same_engine_sync=True):
        self.nc = nc
        self.q = {e: [] for e in self.ENGS}
        self.tr = {}
        self.same = same_engine_sync
        self.n_dma_sems = n_dma_sems
        self.n_dma = 0
        self.n_dma_q = [0, 0]
        self.dma_last = [None] * n_dma_sems
        self.pe = _Proxy(self, "pe")
        self.act = _Proxy(self, "act")
        self.dve = _Proxy(self, "dve")
        self.pool = _Proxy(self, "pool")
        self.n = 0
        self.out_dmas = []
        self.bank_last = {}
        self.label = None
        self.annotate = False
        self.strict = False

    def dma(self, out, in_, queue="sp", is_output=False, **kw):
        kwargs = dict(out=out, in_=in_)
        kwargs.update(kw)
        rec = self._add(queue, "dma_start", kwargs, dma=True)
        if is_output:
            self.out_dmas.append(rec)
        return rec

    def _add(self, key, name, kwargs, dma=False):
        rec = _Rec(key, name, kwargs, dma)
        rec.seq = self.n
        rec.label = self.label
        self.n += 1
        deps = {}
        reads, writes = [], []
        pbanks = set()
        for k, v in kwargs.items():
            if isinstance(v, bass.AP):
                r = _region(v)
                if r is None:
                    continue
                (writes if k in _WRITE_KEYS else reads).append(r)
                if str(v.space) == "PSUM":
                    for bnk in range(r[3] // 2048, (r[4] - 1) // 2048 + 1):
                        pbanks.add((r[0], bnk))
        for bnk in pbanks:
            bl = self.bank_last.setdefault(bnk, {})
            for e2, r2 in bl.items():
                if e2 != key:
                    deps[id(r2)] = r2
            bl[key] = rec
        if name == "matmul" and kwargs.get("start") is False:
            pass
        safe_same = {}
        def _dep(drec, kind, preg, creg):
            deps[id(drec)] = drec
            if drec.eng == key and not drec.dma and self.same == 2:
                ok = not self.strict
                if kind == "raw":
                    big = (preg[4] - preg[3]) >= 512 and (creg[4] - creg[3]) >= 512
                    ok = big and preg[3] == creg[3] and preg[1] == creg[1] and not self.strict
                if not ok:
                    safe_same[id(drec)] = False
                else:
                    safe_same.setdefault(id(drec), True)
        for r in reads:
            t = self.tr.get(r[0])
            if t is None:
                t = self.tr[r[0]] = [[], {}]
            for wr, wrec in t[0]:
                if _overlap(wr, r):
                    _dep(wrec, "raw", wr, r)
        for w in writes:
            t = self.tr.get(w[0])
            if t is None:
                t = self.tr[w[0]] = [[], {}]
            for wr, wrec in t[0]:
                if _overlap(wr, w):
                    _dep(wrec, "waw", wr, w)
            for (rk, rr, _sq), rrec in t[1].items():
                if _overlap(rr, w):
                    _dep(rrec, "war", rr, w)
        for r in reads:
            t = self.tr[r[0]]
            t[1][(key, r, rec.seq if dma else -1)] = rec
        for w in writes:
            t = self.tr[w[0]]
            t[0] = [(wr, wrec) for (wr, wrec) in t[0] if not _covers(w, wr)]
            t[0].append((w, rec))
            for kk in [kk for kk in t[1] if _covers(w, kk[1])]:
                if t[1][kk] is not rec:
                    del t[1][kk]
        for d in deps.values():
            if d is rec:
                continue
            if d.eng == key and not d.dma:
                if key == "pe" or key == "sp":
                    continue
                if not self.same:
                    continue
                if self.same == 2 and safe_same.get(id(d), False):
                    continue
            rec.deps.append(d)
            if not d.dma:
                d.inc = True
        if dma:
            npool = 6
            qi = 1 if key == "pool" else 0
            k = self.n_dma_q[qi]
            self.n_dma_q[qi] += 1
            if qi:
                nsl = npool
                slot = (self.n_dma_sems - npool) + (k % nsl)
            else:
                nsl = self.n_dma_sems - npool
                slot = k % nsl
            rec.dsem = slot
            rec.dval = 16 * (k // nsl + 1)
            rec.pre = self.dma_last[slot]
            self.dma_last[slot] = rec
            self.n_dma += 1
        self.q[key].append(rec)
        return rec

    def emit(self, sems):
        nc = self.nc
        for e in self.ENGS:
            c = 0
            for rec in self.q[e]:
                if rec.inc and not rec.dma:
                    c += 1
                    rec.cnt = c
        self.stats = {e: [len(self.q[e]), 0, 0] for e in self.ENGS}
        engsem = sems

        def replay(key, eng):
            waited = {}
            for rec in self.q[key]:
                need = {}
                deps = list(rec.deps)
                if rec.dma and rec.pre is not None:
                    deps.append(rec.pre)
                for d in deps:
                    if d.dma:
                        sk = ("dma", d.dsem)
                        v = d.dval
                    else:
                        sk = d.eng
                        v = d.cnt
                    if need.get(sk, 0) < v:
                        need[sk] = v
                for sk, v in need.items():
                    if waited.get(sk, 0) >= v:
                        continue
                    waited[sk] = v
                    sem = engsem["dma"][sk[1]] if isinstance(sk, tuple) else engsem[sk]
                    eng.wait_ge(sem, v)
                    self.stats[key][1] += 1
                ins = getattr(eng, rec.name)(**rec.kwargs)
                if self.annotate and rec.label:
                    ins.annotate(rec.label)
                if rec.dma:
                    ins.then_inc(engsem["dma"][rec.dsem], 16)
                elif rec.inc:
                    ins.then_inc(engsem[key], 1)
                    self.stats[key][2] += 1
            if key == "sp":
                for slot, last in enumerate(self.dma_last):
                    if last is not None:
                        eng.wait_ge(engsem["dma"][slot], last.dval)
                for e in ("pe", "act", "dve", "pool"):
                    c = max([r.cnt for r in self.q[e]] + [0])
                    if c:
                        eng.wait_ge(engsem[e], c)

        return replay


def run_sched(nc, S):
    from contextlib import ExitStack
    with ExitStack() as st:
        sems = {}
        for e in ("pe", "act", "dve", "pool"):
            sems[e] = st.enter_context(nc.semaphore("s_" + e))
        sems["dma"] = [st.enter_context(nc.semaphore("s_dma%d" % i)) for i in range(S.n_dma_sems)]
        block = st.enter_context(nc.Block())
        replay = S.emit(sems)

        @block.tensor
        def _(eng):
            replay("pe", eng)

        @block.scalar
        def _(eng):
            replay("act", eng)

        @block.vector
        def _(eng):
            replay("dve", eng)

        @block.gpsimd
        def _(eng):
            replay("pool", eng)

        @block.sync
        def _(eng):
            replay("sp", eng)

D = 1024
SEQ = 2048
DFF = 2816
NIN = 10256
MEMLEN = 256
NCORES = 8
EPS = 1e-6
OFF_AQ, OFF_AK, OFF_AV = 0, 1536, 3072
OFF_MQ, OFF_MK, OFF_MV, OFF_MO = 4608, 5120, 5632, 6144
OFF_MIF, OFF_CQ, OFF_G = 6656, 6672, 7184
NEGM = -30000.0

V_NF1, V_NMIX, V_NF2, V_NFIN, V_NMEM = 0, 8, 16, 24, 32
V_GQ, V_GK = 40, 41
V_CW = 42
V_CB = 82
V_MLG = 90
V_CXQ, V_CXK = 94, 95
V_GB = 96
V_MGB = 120
NV = 136


def _st_chunks(W):
    K, N = W.shape
    return np.ascontiguousarray(W.reshape(K // 128, 128, N // 128, 128).transpose(2, 1, 0, 3))


def _mv_layout(W):
    K, N = W.shape
    return np.ascontiguousarray(W.reshape(K // 128, 128, N).transpose(1, 0, 2))


def _pp(v, n):
    return np.ascontiguousarray(np.asarray(v, np.float32).reshape(n, 128).T)


def host_consts():
    c = {}
    idn = np.eye(128, dtype=np.float32)
    a = np.arange(128)
    A, B = a[:, None], a[None, :]
    tri_le = (A <= B).astype(np.float32)
    tri_ge = (A >= B).astype(np.float32)
    blk = ((A // 64) == (B // 64)).astype(np.float32)
    partner = np.where((a % 64) < 32, a + 32, a - 32)
    psw = np.zeros((128, 128), np.float32)
    psw[partner, a] = 1.0
    sel = np.zeros((128, 64), np.float32)
    sel[64, :] = 1.0
    selp = np.zeros((128, 128), np.float32)
    selp[:, :64] = sel
    c["cmat"] = np.ascontiguousarray(np.stack([idn, tri_le, tri_ge, psw, np.ones((128, 128), np.float32), selp], axis=1))
    c["cbf"] = np.ascontiguousarray(np.stack([idn, np.ones((128, 128), np.float32), blk, tri_le, tri_ge], axis=1))
    mprev = np.where(A - B >= 64, 0.0, NEGM).astype(np.float32)
    mself = np.where(np.abs(A - B) <= 64, 0.0, NEGM).astype(np.float32)
    mnext = np.where(B - A >= 64, 0.0, NEGM).astype(np.float32)
    m01 = np.concatenate([mprev, mself, mnext, mprev, mself, mnext], axis=1)
    m2 = []
    for w in range(4):
        bq = 32 * w + np.arange(32)[None, :]
        blkm = np.where(np.abs(A - bq) <= 64, 0.0, NEGM).astype(np.float32)
        m2.append(np.tile(blkm, (1, 16)))
    c["amask"] = np.ascontiguousarray(np.concatenate([m01] + m2, axis=1))
    inv = 10000.0 ** (-np.arange(0, 64, 2, dtype=np.float64) / 64.0)
    ang = np.arange(SEQ, dtype=np.float64)[:, None] * inv[None, :]
    j = (a % 64) % 32
    cosT = np.cos(ang)[:, j].T.astype(np.float32)
    sinT = np.sin(ang)[:, j].T.astype(np.float32)
    sgn = np.where((a % 64) < 32, -1.0, 1.0).astype(np.float32)[:, None]
    c["rope"] = np.ascontiguousarray(np.stack([cosT, sinT * sgn], axis=1))
    return c


def host_layout(inp):
    L = {}
    f = lambda k: np.asarray(inp[k], np.float32)[0]
    L["w1a"] = _st_chunks(f("w_ffn1_in"))
    L["w1b"] = _st_chunks(f("w_ffn1_out"))
    L["w2a"] = _st_chunks(f("w_ffn2_in"))
    L["w2b"] = _st_chunks(f("w_ffn2_out"))
    win = f("w_in")
    st_cols = np.concatenate([np.arange(OFF_AQ, OFF_AV), np.arange(OFF_MQ, OFF_MV),
                              np.arange(OFF_CQ, OFF_G), np.arange(OFF_G, NIN)])
    L["wst"] = _st_chunks(win[:, st_cols])
    av = []
    for p in range(4):
        cols = np.concatenate([np.arange(OFF_AV + (g * 4 + p) * 128, OFF_AV + (g * 4 + p + 1) * 128) for g in range(3)])
        av.append(_mv_layout(win[:, cols]))
    L["wav"] = np.ascontiguousarray(np.stack(av))
    mvo = []
    for h in range(4):
        cols = np.concatenate([np.arange(OFF_MV + h * 128, OFF_MV + (h + 1) * 128),
                               np.arange(OFF_MO + h * 128, OFF_MO + (h + 1) * 128)])
        mvo.append(_mv_layout(win[:, cols]))
    L["wmvo"] = np.ascontiguousarray(np.stack(mvo))
    L["wmif"] = _mv_layout(win[:, OFF_MIF:OFF_MIF + 16])
    wkv = f("w_mem_kv")
    L["wck"] = _st_chunks(wkv[:, :512])
    L["wcv"] = _mv_layout(wkv[:, 512:])
    L["wbr"] = np.ascontiguousarray(np.stack([_st_chunks(f("w_br_att")), _st_chunks(f("w_br_ml")),
                                              _st_chunks(f("w_br_cx"))]))
    L["wout"] = _st_chunks(f("w_out"))
    vec = np.zeros((128, NV), np.float32)
    vec[:, V_NF1:V_NF1 + 8] = _pp(f("norm_ffn1"), 8)
    vec[:, V_NMIX:V_NMIX + 8] = _pp(f("norm_mix"), 8)
    vec[:, V_NF2:V_NF2 + 8] = _pp(f("norm_ffn2"), 8)
    vec[:, V_NFIN:V_NFIN + 8] = _pp(f("norm_final"), 8)
    vec[:, V_NMEM:V_NMEM + 8] = _pp(f("norm_mem"), 8)
    vec[:, V_GQ] = np.tile(f("att_q_gain"), 2)
    vec[:, V_GK] = np.tile(f("att_k_gain"), 2)
    cw = f("ml_conv_w")
    for ch in range(8):
        vec[:, V_CW + ch * 5:V_CW + ch * 5 + 5] = cw[:, ch * 128:(ch + 1) * 128].T
    vec[:, V_CB:V_CB + 8] = _pp(f("ml_conv_b"), 8)
    vec[:, V_MLG:V_MLG + 4] = _pp(f("ml_out_gain"), 4)
    vec[:, V_CXQ] = f("cx_q_gain")
    vec[:, V_CXK] = f("cx_k_gain")
    vec[:, V_GB:V_GB + 24] = _pp(f("mix_gate_b"), 24)
    vec[:, V_MGB:V_MGB + 16] = np.broadcast_to(f("ml_gate_b")[None, :], (128, 16))
    L["vecs"] = vec
    return L


class Arena:
    def __init__(self, t, nbytes):
        self.t = t
        self.cap = nbytes
        self.top = 0

    def alloc(self, nbytes, align=64):
        off = (self.top + align - 1) // align * align
        self.top = off + nbytes
        assert self.top <= self.cap, ("arena overflow", self.top, self.cap)
        return off

    def view(self, off, dt, shape, p0=0, p1=128):
        es = _dsize(dt)
        n = 1
        for s in shape:
            n *= s
        nb = n * es
        assert off % 4 == 0 and nb % 4 == 0
        ap = self.t[p0:p1, off // 4:(off + nb) // 4]
        if es != 4:
            ap = ap.bitcast(dt)
        if len(shape) > 1:
            names = ["a%d" % i for i in range(len(shape))]
            ap = ap.rearrange("p (%s) -> p %s" % (" ".join(names), " ".join(names)),
                              **{nm: s for nm, s in zip(names, shape)})
        return ap

    def new(self, dt, shape, p0=0, p1=128):
        n = 1
        for s in shape:
            n *= s
        off = self.alloc(n * _dsize(dt))
        return self.view(off, dt, shape, p0, p1)


class Rot:
    def __init__(self, items):
        self.items = list(items)
        self.i = 0

    def next(self):
        v = self.items[self.i % len(self.items)]
        self.i += 1
        return v


def build(nseq=2, stages=("ffn1", "att", "ml", "cx", "ffn2"), same_engine_sync=True):
    from contextlib import ExitStack
    nc = bass.Bass("TRN2", target_bir_lowering=False)

    def dr(name, shape, kind="ExternalInput"):
        return nc.dram_tensor(name, list(shape), F32, kind=kind).ap()

    x_d = dr("x", [nseq, SEQ, D])
    mem_d = dr("mem", [nseq, MEMLEN, D])
    out_d = dr("out", [nseq, SEQ, D], kind="ExternalOutput")
    w1a_d, w1b_d = dr("w1a", [44, 128, 8, 128]), dr("w1b", [8, 128, 22, 128])
    w2a_d, w2b_d = dr("w2a", [44, 128, 8, 128]), dr("w2b", [8, 128, 22, 128])
    wst_d = dr("wst", [60, 128, 8, 128])
    wav_d = dr("wav", [4, 128, 8, 384])
    wmvo_d = dr("wmvo", [4, 128, 8, 256])
    wmif_d = dr("wmif", [128, 8, 16])
    wck_d = dr("wck", [4, 128, 8, 128])
    wcv_d = dr("wcv", [128, 8, 512])
    wbr_d = dr("wbr", [3, 8, 128, 4, 128])
    wout_d = dr("wout", [8, 128, 8, 128])
    vecs_d = dr("vecs", [128, NV])
    cmat_d = dr("cmat", [128, 6, 128])
    cbf_d = dr("cbf", [128, 5, 128])
    amask_d = dr("amask", [128, 2816])
    rope_d = dr("rope", [128, 2, SEQ])

    ARENA_BYTES = 212480
    with ExitStack() as st:
        arena_t = st.enter_context(nc.sbuf_tensor("arena", [128, ARENA_BYTES // 4], F32))
        ps = st.enter_context(nc.psum_tensor("ps", [128, 8, 512], F32))
        A = Arena(arena_t, ARENA_BYTES)
        S = Sched(nc, same_engine_sync=same_engine_sync)
        import os as _os2
        S.annotate = bool(_os2.environ.get('MK_ANNOTATE'))

        def bank(b, n=512, p0=0, p1=128):
            return ps[p0:p1, b, 0:n]

        def banks(b0, nb, p0=0, p1=128):
            return ps[p0:p1, b0:b0 + nb, :].rearrange("p a b -> p (a b)")

        cmat = A.new(F32, [6, 128])
        cbf = A.new(BF16, [5, 128])
        amask = A.new(BF16, [2816])
        vecs = A.new(F32, [NV])
        dvec = A.new(F32, [8])
        S.dma(out=cmat, in_=cmat_d)
        S.dma(out=vecs, in_=vecs_d)
        S.dma(out=cbf, in_=cbf_d, queue="pool")
        S.dma(out=amask, in_=amask_d, queue="pool")
        ident_f = cmat[:, 0, :]
        tri_le_f, tri_ge_f, psw_f, ones_f = cmat[:, 1, :], cmat[:, 2, :], cmat[:, 3, :], cmat[:, 4, :]
        sel_f = cmat[:, 5, 0:64]
        ident_b, ones_b, blk_b, tri_le_b, tri_ge_b = (cbf[:, i, :] for i in range(5))
        S.dve.tensor_scalar(out=dvec[:, 0:1], in0=vecs[:, V_GQ:V_GQ + 1], scalar1=0.125, scalar2=None, op0=ALU.mult)
        S.dve.tensor_scalar(out=dvec[:, 1:2], in0=vecs[:, V_CXQ:V_CXQ + 1], scalar1=float(128 ** -0.5), scalar2=None, op0=ALU.mult)

        xT = A.new(F32, [8, SEQ])
        hT = A.new(BF16, [8, SEQ])
        phase_mark = A.top

        def vcol(c, n=1):
            return vecs[:, c:c + n]

        flip = [0]

        def evac(out, in_):
            flip[0] ^= 1
            if flip[0]:
                S.act.activation(out=out, in_=in_, func=AF.Copy)
            else:
                S.dve.tensor_copy(out=out, in_=in_)

        def rms_rstd(tb, sqrot, r1, rstd, pbank, nfeat_inv=1.0 / D):
            for kc in range(8):
                sq = sqrot.next()
                xs_ = xT[:, kc, tb * 512:(tb + 1) * 512]
                if kc % 4 == 1:
                    S.dve.tensor_tensor(out=sq, in0=xs_, in1=xs_, op=ALU.mult)
                elif kc % 4 == 3:
                    S.pool.tensor_tensor(out=sq, in0=xs_, in1=xs_, op=ALU.mult)
                else:
                    S.act.activation(out=sq, in_=xs_, func=AF.Square)
                S.pe.matmul(out=bank(pbank), lhsT=ones_b, rhs=sq, start=(kc == 0), stop=(kc == 7))
            S.act.activation(out=r1, in_=bank(pbank), func=AF.Ln, bias=EPS, scale=nfeat_inv)
            S.act.activation(out=rstd, in_=r1, func=AF.Exp, scale=-0.5)

        def make_hT(gcol):
            S.label = 'norm'
            m = A.top
            sqrot = Rot([A.new(BF16, [512]) for _ in range(4)])
            r1 = A.new(F32, [512])
            rstdr = Rot([A.new(F32, [512]) for _ in range(2)])
            for tb in range(4):
                rstd = rstdr.next()
                rms_rstd(tb, sqrot, r1, rstd, 7)
                for kc in range(8):
                    S.dve.scalar_tensor_tensor(out=hT[:, kc, tb * 512:(tb + 1) * 512],
                                               in0=xT[:, kc, tb * 512:(tb + 1) * 512],
                                               scalar=vcol(gcol + kc), in1=rstd, op0=ALU.mult, op1=ALU.mult)
            A.top = m

        def ffn(wa_d, wb_d, gcol):
            make_hT(gcol)
            S.label = 'ffn'
            m = A.top
            act = A.new(BF16, [11, SEQ])
            wrot = Rot([A.new(BF16, [2, 8, 128]) for _ in range(3)])
            w2rot = Rot([A.new(BF16, [11, 128]) for _ in range(2)])
            sgrot = Rot([A.new(BF16, [1024]) for _ in range(2)])
            setflip = 0
            for half in range(2):
                for jj in range(11):
                    j = half * 11 + jj
                    wb = wrot.next()
                    S.dma(out=wb[:, 0], in_=wa_d[j], queue="pool")
                    S.dma(out=wb[:, 1], in_=wa_d[22 + j], queue="pool")
                    for th in range(2):
                        b0 = 4 * setflip
                        setflip ^= 1
                        for kc in range(8):
                            for gu in range(2):
                                for n in range(2):
                                    t0 = th * 1024 + n * 512
                                    S.pe.matmul(out=bank(b0 + gu * 2 + n), lhsT=wb[:, gu, kc, :],
                                                rhs=hT[:, kc, t0:t0 + 512], start=(kc == 0), stop=(kc == 7))
                        sg = sgrot.next()
                        S.act.activation(out=sg, in_=banks(b0, 2), func=AF.Silu)
                        S.dve.tensor_tensor(out=act[:, jj, th * 1024:(th + 1) * 1024], in0=banks(b0 + 2, 2),
                                            in1=sg, op=ALU.mult)
                for d in range(8):
                    w2 = w2rot.next()
                    S.dma(out=w2, in_=wb_d[d, :, half * 11:(half + 1) * 11, :], queue="pool")
                    b0 = 4 * setflip
                    setflip ^= 1
                    for jj in range(11):
                        for tb in range(4):
                            S.pe.matmul(out=bank(b0 + tb), lhsT=w2[:, jj, :], rhs=act[:, jj, tb * 512:(tb + 1) * 512],
                                        start=(jj == 0), stop=(jj == 10))
                    for tb in range(4):
                        xs = xT[:, d, tb * 512:(tb + 1) * 512]
                        S.dve.scalar_tensor_tensor(out=xs, in0=bank(b0 + tb), scalar=0.5, in1=xs,
                                                   op0=ALU.mult, op1=ALU.add)
            A.top = m

        def load_x(b):
            S.label = 'load'
            m = A.top
            xin = Rot([A.new(F32, [D]) for _ in range(2)])
            for tt in range(16):
                xi = xin.next()
                S.dma(out=xi, in_=x_d[b, tt * 128:(tt + 1) * 128, :])
                for hb in range(2):
                    bk = (tt * 2 + hb) % 4
                    for q in range(4):
                        kc = hb * 4 + q
                        S.pe.transpose(out=ps[:, bk, q * 128:(q + 1) * 128], in_=xi[:, kc * 128:(kc + 1) * 128],
                                       identity=ident_f)
                    evac(xT[:, hb * 4:hb * 4 + 4, tt * 128:(tt + 1) * 128],
                         ps[:, bk, :].rearrange("p (a b) -> p a b", a=4))
            A.top = m

        def final_out(b):
            S.label = 'final'
            m = A.top
            sqrot = Rot([A.new(BF16, [512]) for _ in range(3)])
            r1 = A.new(F32, [512])
            rstd = A.new(F32, [512])
            xn = A.new(F32, [8, 512])
            orot = Rot([A.new(F32, [D]) for _ in range(2)])
            for tb in range(4):
                rms_rstd(tb, sqrot, r1, rstd, 7)
                for kc in range(8):
                    S.dve.scalar_tensor_tensor(out=xn[:, kc, :], in0=xT[:, kc, tb * 512:(tb + 1) * 512],
                                               scalar=vcol(V_NFIN + kc), in1=rstd, op0=ALU.mult, op1=ALU.mult)
                for q in range(4):
                    ot = orot.next()
                    for hb in range(2):
                        bk = (q * 2 + hb) % 4
                        for r in range(4):
                            kc = hb * 4 + r
                            S.pe.transpose(out=ps[:, bk, r * 128:(r + 1) * 128], in_=xn[:, kc, q * 128:(q + 1) * 128],
                                           identity=ident_f)
                        evac(ot[:, hb * 512:(hb + 1) * 512], ps[:, bk, :])
                    t0 = tb * 512 + q * 128
                    S.dma(out=out_d[b, t0:t0 + 128, :], in_=ot, is_output=True)
            A.top = m

        P2 = Rot([0, 2])
        P1 = Rot([4, 5, 6, 7])

        def bankb(bk):
            return ps[:, bk, :].bitcast(BF16)

        def branch_contrib(bi, yT):
            S.label = "contrib"
            m = A.top
            cT = A.new(BF16, [8, SEQ])
            wbr = Rot([A.new(BF16, [4, 128]) for _ in range(2)])
            wgr = Rot([A.new(BF16, [8, 128]) for _ in range(2)])
            wor = Rot([A.new(BF16, [8, 128]) for _ in range(2)])
            grot = Rot([A.new(F32, [1024]) for _ in range(2)])
            sf = 0
            for d in range(8):
                wb = wbr.next()
                S.dma(out=wb, in_=wbr_d[bi, d], queue="pool")
                wg = wgr.next()
                S.dma(out=wg, in_=wst_d[36 + bi * 8 + d], queue="pool")
                for th in range(2):
                    b0 = 4 * sf
                    sf ^= 1
                    for kc in range(4):
                        for n in range(2):
                            t0 = th * 1024 + n * 512
                            S.pe.matmul(out=bank(b0 + n), lhsT=wb[:, kc, :], rhs=yT[:, kc, t0:t0 + 512],
                                        start=(kc == 0), stop=(kc == 3))
                    for kc in range(8):
                        for n in range(2):
                            t0 = th * 1024 + n * 512
                            S.pe.matmul(out=bank(b0 + 2 + n), lhsT=wg[:, kc, :], rhs=hT[:, kc, t0:t0 + 512],
                                        start=(kc == 0), stop=(kc == 7))
                    g = grot.next()
                    S.act.activation(out=g, in_=banks(b0 + 2, 2), func=AF.Sigmoid,
                                     bias=vcol(V_GB + bi * 8 + d), scale=1.0)
                    S.dve.tensor_tensor(out=cT[:, d, th * 1024:(th + 1) * 1024], in0=banks(b0, 2), in1=g, op=ALU.mult)
            for d2 in range(8):
                w = wor.next()
                S.dma(out=w, in_=wout_d[d2], queue="pool")
                b0 = 4 * sf
                sf ^= 1
                for kc in range(8):
                    for tb in range(4):
                        S.pe.matmul(out=bank(b0 + tb), lhsT=w[:, kc, :], rhs=cT[:, kc, tb * 512:(tb + 1) * 512],
                                    start=(kc == 0), stop=(kc == 7))
                for tb in range(4):
                    xs = xT[:, d2, tb * 512:(tb + 1) * 512]
                    S.dve.tensor_tensor(out=xs, in0=bank(b0 + tb), in1=xs, op=ALU.add)
            A.top = m

        def cross_branch(b):
            S.label = 'cx'
            m = A.top
            yT = A.new(BF16, [4, SEQ])
            m2 = A.top
            memt = A.new(F32, [2, D])
            memn = A.new(BF16, [2, D])
            junk = A.new(BF16, [D])
            memnT = A.new(BF16, [8, 256])
            ckT = A.new(BF16, [4, 256])
            cv = A.new(BF16, [2, 512])
            ssm = A.new(F32, [4])
            wcv = A.new(BF16, [8, 512])
            wrot = Rot([A.new(BF16, [8, 128]) for _ in range(2)])
            sqr = Rot([A.new(BF16, [512]) for _ in range(2)])
            r1 = A.new(F32, [512])
            rstd = A.new(F32, [512])
            cqr = Rot([A.new(BF16, [512]) for _ in range(2)])
            ptr = Rot([A.new(BF16, [2, 512]) for _ in range(2)])
            rinv = A.new(F32, [512])
            S.dve.memset(ap=ssm, constant=0.0)
            S.dma(out=wcv, in_=wcv_d, queue="pool")
            for kt in range(2):
                S.dma(out=memt[:, kt, :], in_=mem_d[b, kt * 128:(kt + 1) * 128, :])
            for kt in range(2):
                S.act.activation(out=junk, in_=memt[:, kt, :], func=AF.Square, accum_out=ssm[:, kt:kt + 1])
                S.act.activation(out=ssm[:, 2 + kt:3 + kt], in_=ssm[:, kt:kt + 1], func=AF.Ln, bias=EPS, scale=1.0 / D)
                S.act.activation(out=ssm[:, 2 + kt:3 + kt], in_=ssm[:, 2 + kt:3 + kt], func=AF.Exp, scale=-0.5)
                S.dve.tensor_scalar(out=memn[:, kt, :], in0=memt[:, kt, :], scalar1=ssm[:, 2 + kt:3 + kt], scalar2=None,
                                    op0=ALU.mult)
                for hb in range(2):
                    bk = P1.next()
                    for q in range(4):
                        kc = hb * 4 + q
                        S.pe.transpose(out=bankb(bk)[:, q * 128:(q + 1) * 128], in_=memn[:, kt, kc * 128:(kc + 1) * 128],
                                       identity=ident_b)
                    for q in range(4):
                        kc = hb * 4 + q
                        S.act.activation(out=memnT[:, kc, kt * 128:(kt + 1) * 128], in_=bankb(bk)[:, q * 128:(q + 1) * 128],
                                         func=AF.Copy, scale=vcol(V_NMEM + kc))
            for hd in range(4):
                w = wrot.next()
                S.dma(out=w, in_=wck_d[hd], queue="pool")
                braw, bss = P1.next(), P1.next()
                for kc in range(8):
                    S.pe.matmul(out=bank(braw, 256), lhsT=w[:, kc, :], rhs=memnT[:, kc, :], start=(kc == 0), stop=(kc == 7))
                sq = sqr.next()
                S.act.activation(out=sq[:, 0:256], in_=bank(braw, 256), func=AF.Square)
                S.pe.matmul(out=bank(bss, 256), lhsT=ones_b, rhs=sq[:, 0:256], start=True, stop=True)
                S.act.activation(out=r1[:, 0:256], in_=bank(bss, 256), func=AF.Ln, bias=EPS, scale=1.0 / 128)
                S.act.activation(out=rstd[:, 0:256], in_=r1[:, 0:256], func=AF.Exp, scale=-0.5)
                S.dve.scalar_tensor_tensor(out=ckT[:, hd, :], in0=bank(braw, 256), scalar=vcol(V_CXK), in1=rstd[:, 0:256],
                                           op0=ALU.mult, op1=ALU.mult)
            for kt in range(2):
                bk = P1.next()
                for kc in range(8):
                    S.pe.matmul(out=bank(bk), lhsT=memnT[:, kc, kt * 128:(kt + 1) * 128], rhs=wcv[:, kc, :],
                                start=(kc == 0), stop=(kc == 7))
                evac(cv[:, kt, :], bank(bk))
            its = [(hd, tb) for hd in range(4) for tb in range(4)]
            RB, SB_ = Rot([0, 1]), Rot([2, 3])
            r1r = Rot([r1, A.new(F32, [512])])
            rstdr = Rot([rstd, A.new(F32, [512])])
            rinvr = Rot([rinv, A.new(F32, [512])])
            hold = {}

            def s1(i):
                hd, tb = its[i]
                if tb == 0:
                    w = wrot.next()
                    S.dma(out=w, in_=wst_d[32 + hd], queue="pool")
                    hold["w"] = w
                w = hold["w"]
                braw = RB.next()
                for kc in range(8):
                    S.pe.matmul(out=bank(braw), lhsT=w[:, kc, :], rhs=hT[:, kc, tb * 512:(tb + 1) * 512],
                                start=(kc == 0), stop=(kc == 7))
                sq = sqr.next()
                S.act.activation(out=sq, in_=bank(braw), func=AF.Square)
                hold[i] = {"braw": braw, "sq": sq}

            def s2(i):
                h_ = hold[i]
                bss = SB_.next()
                S.pe.matmul(out=bank(bss), lhsT=ones_b, rhs=h_["sq"], start=True, stop=True)
                r1_, rstd_ = r1r.next(), rstdr.next()
                S.act.activation(out=r1_, in_=bank(bss), func=AF.Ln, bias=EPS, scale=1.0 / 128)
                S.act.activation(out=rstd_, in_=r1_, func=AF.Exp, scale=-0.5)
                cq = cqr.next()
                S.dve.scalar_tensor_tensor(out=cq, in0=bank(h_["braw"]), scalar=dvec[:, 1:2], in1=rstd_,
                                           op0=ALU.mult, op1=ALU.mult)
                h_["cq"] = cq

            def s3(i):
                hd, tb = its[i]
                h_ = hold[i]
                for kt in range(2):
                    S.pe.matmul(out=bank(4 + kt), lhsT=ckT[:, hd, kt * 128:(kt + 1) * 128], rhs=h_["cq"], start=True, stop=True)
                pt = ptr.next()
                S.act.activation(out=pt.rearrange("p a b -> p (a b)"), in_=banks(4, 2), func=AF.Exp)
                h_["pt"] = pt

            def s4(i):
                hd, tb = its[i]
                h_ = hold.pop(i)
                pt = h_["pt"]
                for kt in range(2):
                    S.pe.matmul(out=bank(6), lhsT=cv[:, kt, hd * 128:(hd + 1) * 128], rhs=pt[:, kt, :],
                                start=(kt == 0), stop=(kt == 1))
                for kt in range(2):
                    S.pe.matmul(out=bank(7), lhsT=ones_b, rhs=pt[:, kt, :], start=(kt == 0), stop=(kt == 1))
                r1_, rinv_ = r1r.next(), rinvr.next()
                S.act.activation(out=r1_, in_=bank(7), func=AF.Ln)
                S.act.activation(out=rinv_, in_=r1_, func=AF.Exp, scale=-1.0)
                S.dve.tensor_tensor(out=yT[:, hd, tb * 512:(tb + 1) * 512], in0=bank(6), in1=rinv_, op=ALU.mult)

            n_it = len(its)
            for step in range(n_it + 3):
                if step < n_it:
                    s1(step)
                if 0 <= step - 1 < n_it:
                    s2(step - 1)
                if 0 <= step - 2 < n_it:
                    s3(step - 2)
                if 0 <= step - 3 < n_it:
                    s4(step - 3)
            A.top = m2
            branch_contrib(2, yT)
            A.top = m

        def attention_branch(b):
            m = A.top
            yT = A.new(BF16, [4, SEQ])
            m2 = A.top
            kT = A.new(BF16, [3, SEQ])
            qT = A.new(BF16, [3, SEQ])
            Vt = [A.new(BF16, [16, 2, 66]) for _ in range(3)]
            wav = A.new(BF16, [8, 384])
            wrot = Rot([A.new(BF16, [8, 128]) for _ in range(2)])
            ropet = Rot([A.new(F32, [2, 512]) for _ in range(2)])
            sqr = Rot([A.new(BF16, [512]) for _ in range(2)])
            rgr = Rot([A.new(F32, [512]) for _ in range(2)])
            r1r = Rot([A.new(F32, [512]) for _ in range(2)])
            rstdr = Rot([A.new(F32, [512]) for _ in range(2)])
            t1r = Rot([A.new(F32, [512]) for _ in range(2)])
            t2r = Rot([A.new(F32, [512]) for _ in range(2)])
            ptr = Rot([A.new(BF16, [1024]) for _ in range(2)])
            ytr = Rot([A.new(BF16, [512]) for _ in range(2)])

            def res_view(ap2d, r):
                return ap2d.rearrange("p (i r) -> p r i", r=r)

            def tok_ap(kc, g, tile):
                if g == 0:
                    return hT[:, kc, tile * 128:(tile + 1) * 128]
                if g == 1:
                    c, u = divmod(tile, 4)
                    return res_view(hT[:, kc, :], 4)[:, c, u * 128:(u + 1) * 128]
                return res_view(hT[:, kc, :], 16)[:, tile, :]

            for p in range(4):
                S.label = 'att_v'
                S.dma(out=wav, in_=wav_d[p], queue="pool")
                vfill = []
                for g in range(3):
                    S.dve.memset(ap=Vt[g], constant=1.0)
                    for t2_ in range(8):
                        def vf(g=g, t2_=t2_):
                            lab = S.label
                            S.label = 'att_v'
                            bk = PA.next()
                            for ti in range(2):
                                tile = t2_ * 2 + ti
                                for kc in range(8):
                                    S.pe.matmul(out=ps[:, bk, ti * 128:(ti + 1) * 128], lhsT=tok_ap(kc, g, tile),
                                                rhs=wav[:, kc, g * 128:(g + 1) * 128], start=(kc == 0), stop=(kc == 7))
                            evac(Vt[g][:, t2_ * 2:(t2_ + 1) * 2, :, 0:64],
                                 ps[:, bk, 0:256].rearrange("p (a s d) -> p a s d", a=2, s=2))
                            S.label = lab
                        vfill.append(vf)
                S.label = 'att_qk'
                PA = Rot(list(range(8)))
                its = [(isq, g, tb) for isq in (0, 1) for g in range(3) for tb in range(4)]
                state = {}

                def stage_a(it):
                    isq, g, tb = it
                    if tb == 0:
                        w = wrot.next()
                        S.dma(out=w, in_=wst_d[(0 if isq else 12) + g * 4 + p], queue="pool")
                        state["w"] = w
                    w = state["w"]
                    rt = ropet.next()
                    S.dma(out=rt, in_=rope_d[:, :, tb * 512:(tb + 1) * 512])
                    braw = PA.next()
                    for kc in range(8):
                        S.pe.matmul(out=bank(braw), lhsT=w[:, kc, :], rhs=hT[:, kc, tb * 512:(tb + 1) * 512],
                                    start=(kc == 0), stop=(kc == 7))
                    sq = sqr.next()
                    rg = rgr.next()
                    gcol = dvec[:, 0:1] if isq else vcol(V_GK)
                    S.act.activation(out=sq, in_=bank(braw), func=AF.Square)
                    S.act.activation(out=rg, in_=bank(braw), func=AF.Copy, scale=gcol)
                    return (it, rt, sq, rg)

                def stage_b(st_):
                    (isq, g, tb), rt, sq, rg = st_
                    dstT = qT if isq else kT
                    bss, bsw = PA.next(), PA.next()
                    S.pe.matmul(out=bank(bss), lhsT=blk_b, rhs=sq, start=True, stop=True)
                    S.pe.matmul(out=bank(bsw), lhsT=psw_f, rhs=rg, start=True, stop=True)
                    r1 = r1r.next()
                    rstd = rstdr.next()
                    t2 = t2r.next()
                    S.act.activation(out=r1, in_=bank(bss), func=AF.Ln, bias=EPS, scale=1.0 / 64)
                    S.act.activation(out=rstd, in_=r1, func=AF.Exp, scale=-0.5)
                    t1 = t1r.next()
                    S.pool.tensor_tensor(out=t1, in0=rg, in1=rt[:, 0, :], op=ALU.mult)
                    S.dve.tensor_tensor(out=t2, in0=bank(bsw), in1=rt[:, 1, :], op=ALU.mult)
                    S.dve.tensor_tensor(out=t2, in0=t2, in1=t1, op=ALU.add)
                    S.dve.tensor_tensor(out=dstT[:, g, tb * 512:(tb + 1) * 512], in0=t2, in1=rstd, op=ALU.mult)

                prev = None
                for it in its:
                    cur = stage_a(it)
                    if prev is not None:
                        stage_b(prev)
                        if vfill:
                            vfill.pop(0)()
                    prev = cur
                stage_b(prev)
                while vfill:
                    vfill.pop(0)()
                S.label = 'att_core'
                batches = []
                for sl in range(2):
                    for w_ in range(4):
                        ctx = {"first": True, "OB": None}

                        def mk_pv(ctx):
                            def pv(out_fn, vt_ap, pt_ap):
                                if ctx["OB"] is None:
                                    ctx["OB"] = P1.next()
                                S.pe.matmul(out=out_fn(ctx["OB"]), lhsT=vt_ap, rhs=pt_ap, start=ctx["first"], stop=False,
                                            skip_group_check=True)
                                ctx["first"] = False
                            return pv

                        pv = mk_pv(ctx)
                        pb = 64 * sl
                        kq = (lambda pb: (lambda T, g: T[pb:pb + 64, g, :]))(pb)
                        for g in range(2):
                            for bi in range(2):
                                def qk_fn(g=g, bi=bi, sl=sl, w_=w_, kq=kq):
                                    b2 = P2.next()
                                    pt = ptr.next()
                                    st2 = banks(b2, 2)
                                    S.pe.matmul(out=bank(b2), lhsT=ident_b, rhs=amask[:, 0:512], start=True, stop=False,
                                                skip_group_check=True)
                                    S.pe.matmul(out=bank(b2 + 1, 256), lhsT=ident_b, rhs=amask[:, 512:768], start=True,
                                                stop=False, skip_group_check=True)
                                    blocks = []
                                    for qi in range(2):
                                        if g == 0:
                                            t = 4 * w_ + 2 * bi + qi
                                            q_ap = kq(qT, 0)[:, t * 128:(t + 1) * 128]
                                            ntile = 16
                                            out_fn = (lambda t: (lambda OB: ps[0:65, OB, (t % 4) * 128:(t % 4 + 1) * 128]))(t)
                                        else:
                                            c = 2 * bi + qi
                                            t = w_
                                            q_ap = res_view(kq(qT, 1), 4)[:, c, t * 128:(t + 1) * 128]
                                            ntile = 4
                                            out_fn = (lambda c: (lambda OB: res_view(ps[0:65, OB, :], 4)[:, c, :]))(c)
                                        for ki, kt in enumerate((t - 1, t, t + 1)):
                                            if not (0 <= kt < ntile):
                                                continue
                                            col = (qi * 3 + ki) * 128
                                            if g == 0:
                                                k_ap = kq(kT, 0)[:, kt * 128:(kt + 1) * 128]
                                                v_ap = Vt[0][:, kt, sl, 0:65]
                                            else:
                                                k_ap = res_view(kq(kT, 1), 4)[:, c, kt * 128:(kt + 1) * 128]
                                                v_ap = Vt[1][:, c * 4 + kt, sl, 0:65]
                                            S.pe.matmul(out=st2[:, col:col + 128], lhsT=k_ap, rhs=q_ap, start=False, stop=False,
                                                        skip_group_check=True)
                                            blocks.append((col, v_ap, out_fn))
                                    S.act.activation(out=pt[:, 0:768], in_=st2[:, 0:768], func=AF.Exp)
                                    return (pt, blocks)

                                def pv_fn(res, pv=pv):
                                    pt, blocks = res
                                    for col, v_ap, out_fn in blocks:
                                        pv(out_fn, v_ap, pt[:, col:col + 128])

                                batches.append((qk_fn, pv_fn, None))

                        def qk2_fn(sl=sl, w_=w_, kq=kq):
                            b2 = P2.next()
                            pt = ptr.next()
                            S.pe.matmul(out=bank(b2), lhsT=ident_b, rhs=amask[:, 768 + w_ * 512:768 + (w_ + 1) * 512],
                                        start=True, stop=False, skip_group_check=True)
                            for c in range(16):
                                S.pe.matmul(out=ps[:, b2, c * 32:(c + 1) * 32], lhsT=res_view(kq(kT, 2), 16)[:, c, :],
                                            rhs=res_view(kq(qT, 2), 16)[:, c, w_ * 32:(w_ + 1) * 32], start=False, stop=False,
                                            skip_group_check=True)
                            S.act.activation(out=pt[:, 0:512], in_=bank(b2), func=AF.Exp)
                            return pt

                        def pv2_fn(pt, sl=sl, pv=pv):
                            for c in range(16):
                                pv((lambda c: (lambda OB: res_view(ps[0:65, OB, :], 16)[:, c, :]))(c), Vt[2][:, c, sl, 0:65],
                                   pt[:, c * 32:(c + 1) * 32])

                        def tail_fn(sl=sl, w_=w_, ctx=ctx):
                            OB = ctx["OB"]
                            acc = t2r.next()
                            S.act.activation(out=acc[0:65, :], in_=ps[0:65, OB, :], func=AF.Copy)

                            def tail_b():
                                bl = P1.next()
                                S.pe.matmul(out=ps[0:64, bl, :], lhsT=sel_f[0:65, :], rhs=acc[0:65, :], start=True, stop=True)
                                r1 = r1r.next()
                                rinv = rstdr.next()
                                S.act.activation(out=r1[0:64, :], in_=ps[0:64, bl, :], func=AF.Ln)
                                S.act.activation(out=rinv[0:64, :], in_=r1[0:64, :], func=AF.Exp, scale=-1.0)
                                if sl == 0:
                                    S.dve.tensor_tensor(out=yT[0:64, p, w_ * 512:(w_ + 1) * 512], in0=acc[0:64, :],
                                                        in1=rinv[0:64, :], op=ALU.mult)
                                else:
                                    yt = ytr.next()
                                    S.dve.tensor_tensor(out=yt[0:64, :], in0=acc[0:64, :], in1=rinv[0:64, :], op=ALU.mult)
                                    S.dma(out=yT[64:128, p, w_ * 512:(w_ + 1) * 512], in_=yt[0:64, :])
                            return tail_b

                        batches.append((qk2_fn, pv2_fn, tail_fn))
                prev = None
                pend = None
                for qk_fn, pv_fn, tail_fn in batches:
                    res = qk_fn()
                    if pend is not None:
                        pend()
                        pend = None
                    if prev is not None:
                        prev[0](prev[1])
                        if prev[2] is not None:
                            pend = prev[2]()
                    prev = (pv_fn, res, tail_fn)
                if pend is not None:
                    pend()
                prev[0](prev[1])
                prev[2]()()
            A.top = m2
            branch_contrib(0, yT)
            A.top = m

        LNA = float(-0.5 * np.log(128.0))

        def mlstm_branch(b):
            m = A.top
            yT = A.new(BF16, [4, SEQ])
            m2 = A.top
            G = A.new(F32, [16, 16])
            SP = A.new(F32, [16, 8])
            CUM = A.new(F32, [16, 8])
            EU = A.new(F32, [16, 8])
            EB = A.new(F32, [16, 8])
            EG = A.new(F32, [16, 8])
            wmif = A.new(BF16, [8, 16])
            S.label = 'ml_gates'
            S.dma(out=wmif, in_=wmif_d, queue="pool")
            bk = P1.next()
            for tt in range(16):
                for kc in range(8):
                    S.pe.matmul(out=ps[:, bk, tt * 16:(tt + 1) * 16], lhsT=hT[:, kc, tt * 128:(tt + 1) * 128],
                                rhs=wmif[:, kc, :], start=(kc == 0), stop=(kc == 7))
            S.dve.tensor_tensor(out=G, in0=ps[:, bk, 0:256].rearrange("p (t g) -> p t g", g=16),
                                in1=vecs[:, V_MGB:V_MGB + 16].unsqueeze(1).to_broadcast([128, 16, 16]), op=ALU.add)
            for d_ in range(2):
                S.act.activation(out=SP[:, :, d_ * 4:d_ * 4 + 4], in_=G[:, :, 4 + d_ * 8:8 + d_ * 8], func=AF.Exp, scale=-1.0)
            S.act.activation(out=SP, in_=SP, func=AF.Ln, bias=1.0)
            bkc, bkc2, bkg = P1.next(), P1.next(), P1.next()
            SPf = SP.rearrange("p t g -> p (t g)")
            S.pe.matmul(out=ps[:, bkc, 0:128], lhsT=tri_le_f, rhs=SPf, start=True, stop=True)
            S.pe.matmul(out=ps[:, bkc2, 0:128], lhsT=tri_ge_f, rhs=SPf, start=True, stop=True)
            S.pe.matmul(out=ps[:, bkg, 0:128], lhsT=ones_f, rhs=SPf, start=True, stop=True)
            pcf = ps[:, bkc, 0:128].rearrange("p (t g) -> p t g", g=8)
            pcb = ps[:, bkc2, 0:128].rearrange("p (t g) -> p t g", g=8)
            pg = ps[:, bkg, 0:128].rearrange("p (t g) -> p t g", g=8)
            S.dve.tensor_copy(out=CUM[:, :, 0:4], in_=pcf[:, :, 0:4])
            S.dve.tensor_copy(out=CUM[:, :, 4:8], in_=pcb[:, :, 4:8])
            S.act.activation(out=EB, in_=CUM, func=AF.Exp, scale=-1.0)
            S.act.activation(out=EG, in_=pg, func=AF.Exp, scale=-1.0)
            for d_ in range(2):
                S.dve.tensor_tensor(out=EU[:, :, d_ * 4:d_ * 4 + 4], in0=G[:, :, d_ * 8:d_ * 8 + 4],
                                    in1=CUM[:, :, d_ * 4:d_ * 4 + 4], op=ALU.add)
            S.act.activation(out=EU, in_=EU, func=AF.Exp, bias=LNA)
            qkbuf = [(A.new(BF16, [SEQ]), A.new(BF16, [SEQ])) for _ in range(2)]
            ktok = A.new(BF16, [16, 128])
            vdir = [A.new(BF16, [16, 130]) for _ in range(2)]
            gs = A.new(BF16, [16, 128])
            roff = A.alloc(24960)
            STf = [A.view(roff + d_ * 8320, F32, [16, 130]) for d_ in range(2)]
            STb = [A.view(roff + 16640 + d_ * 4160, BF16, [16, 130]) for d_ in range(2)]
            hmall = A.view(roff, F32, [16, 128])
            ytall = A.view(roff + 8320, BF16, [16, 128])
            cbuf = A.new(BF16, [SEQ + 4])
            dg = A.new(BF16, [10, 128])
            soff = A.alloc(8192)
            smb = A.view(soff, BF16, [2, 16, 128])
            hsq = A.view(soff, F32, [16, 128])
            wq = A.new(BF16, [8, 128])
            wmv = A.new(BF16, [8, 256])
            tmpr = Rot([A.new(F32, [128]) for _ in range(2)])
            SC = A.new(F32, [2, 16])
            sct = A.new(F32, [4, 16])
            stat = A.new(F32, [32])
            PB = Rot(list(range(8)))
            S.dve.memset(ap=cbuf[:, 0:2], constant=0.0)
            S.dve.memset(ap=cbuf[:, SEQ + 2:SEQ + 4], constant=0.0)

            def gen_qk(hd):
                S.label = 'ml_qk'
                qTm, kTm = qkbuf[hd % 2]
                for isk in (0, 1):
                    ch = isk * 4 + hd
                    for j in range(5):
                        S.dve.tensor_scalar(out=dg[:, isk * 5 + j, :], in0=ident_f, scalar1=vcol(V_CW + ch * 5 + j), scalar2=None,
                                            op0=ALU.mult)
                for isk in (0, 1):
                    ch = isk * 4 + hd
                    S.dma(out=wq, in_=wst_d[24 + isk * 4 + hd], queue="pool")
                    for tb in range(4):
                        bk = PB.next()
                        for kc in range(8):
                            S.pe.matmul(out=bank(bk), lhsT=wq[:, kc, :], rhs=hT[:, kc, tb * 512:(tb + 1) * 512],
                                        start=(kc == 0), stop=(kc == 7))
                        evac(cbuf[:, 2 + tb * 512:2 + (tb + 1) * 512], bank(bk))
                    for tb in range(4):
                        bk = PB.next()
                        for j in range(5):
                            S.pe.matmul(out=bank(bk), lhsT=dg[:, isk * 5 + j, :], rhs=cbuf[:, tb * 512 + j:tb * 512 + j + 512],
                                        start=(j == 0), stop=(j == 4))
                        S.act.activation(out=(kTm if isk else qTm)[:, tb * 512:(tb + 1) * 512], in_=bank(bk), func=AF.Silu,
                                         bias=vcol(V_CB + ch), scale=1.0)

            gen_qk(0)
            for hd in range(4):
                qTm, kTm = qkbuf[hd % 2]
                S.label = 'ml_v'
                for t4 in range(4):
                    bk = PB.next()
                    for ti in range(4):
                        tile = t4 * 4 + ti
                        S.pe.transpose(out=bankb(bk)[:, ti * 128:(ti + 1) * 128], in_=kTm[:, tile * 128:(tile + 1) * 128],
                                       identity=ident_b)
                    evac(ktok[:, t4 * 4:(t4 + 1) * 4, :], bankb(bk)[:, 0:512].rearrange("p (a d) -> p a d", a=4))
                S.dma(out=wmv, in_=wmvo_d[hd], queue="pool")
                for t2 in range(8):
                    bk = PB.next()
                    for ti in range(2):
                        tt = t2 * 2 + ti
                        for kc in range(8):
                            S.pe.matmul(out=ps[:, bk, ti * 256:(ti + 1) * 256], lhsT=hT[:, kc, tt * 128:(tt + 1) * 128],
                                        rhs=wmv[:, kc, :], start=(kc == 0), stop=(kc == 7))
                    for ti in range(2):
                        tt = t2 * 2 + ti
                        S.act.activation(out=gs[:, tt, :], in_=ps[:, bk, ti * 256 + 128:(ti + 1) * 256], func=AF.Sigmoid)
                        S.dve.tensor_scalar(out=vdir[0][:, tt, 0:128], in0=ps[:, bk, ti * 256:ti * 256 + 128],
                                            scalar1=EU[:, tt, hd:hd + 1], scalar2=None, op0=ALU.mult)
                        S.act.activation(out=vdir[1][:, tt, 0:128], in_=ps[:, bk, ti * 256:ti * 256 + 128], func=AF.Copy,
                                         scale=EU[:, tt, 4 + hd:5 + hd])
                for d_ in range(2):
                    S.dve.tensor_copy(out=vdir[d_][:, :, 128:129], in_=EU[:, :, d_ * 4 + hd:d_ * 4 + hd + 1])
                S.label = 'ml_rec'
                for d_ in range(2):
                    col = d_ * 4 + hd
                    for c in range(16):
                        if (d_ == 0 and c == 15) or (d_ == 1 and c == 0):
                            continue
                        bU = PB.next()
                        S.pe.matmul(out=ps[:, bU, 0:129], lhsT=ktok[:, c, :], rhs=vdir[d_][:, c, 0:129], start=True, stop=True)
                        S.act.activation(out=STf[d_][:, c, 0:129], in_=ps[:, bU, 0:129], func=AF.Copy,
                                         scale=EG[:, c, col:col + 1])
                for c in range(16):
                    cs = slice(c * 128, (c + 1) * 128)
                    bST = PB.next()
                    S.pe.matmul(out=ps[:, bST, 0:128], lhsT=kTm[:, cs], rhs=qTm[:, cs], start=True, stop=True)
                    S.dve.tensor_tensor(out=smb[:, 0, c, :], in0=ps[:, bST, 0:128], in1=tri_le_b, op=ALU.mult)
                    S.dve.tensor_tensor(out=smb[:, 1, c, :], in0=ps[:, bST, 0:128], in1=tri_ge_b, op=ALU.mult)
                if hd + 1 < 4:
                    gen_qk(hd + 1)
                    S.label = 'ml_rec'
                S.strict = True
                for c in range(1, 15):
                    S.dve.scalar_tensor_tensor(out=STf[0][:, c, 0:129], in0=STf[0][:, c - 1, 0:129], scalar=EG[:, c, hd:hd + 1],
                                               in1=STf[0][:, c, 0:129], op0=ALU.mult, op1=ALU.add)
                    cb = 15 - c
                    S.dve.scalar_tensor_tensor(out=STf[1][:, cb, 0:129], in0=STf[1][:, cb + 1, 0:129],
                                               scalar=EG[:, cb, 4 + hd:5 + hd], in1=STf[1][:, cb, 0:129],
                                               op0=ALU.mult, op1=ALU.add)
                S.strict = False
                S.act.activation(out=STb[0][:, 0:15, 0:129], in_=STf[0][:, 0:15, 0:129], func=AF.Copy)
                S.act.activation(out=STb[1][:, 1:16, 0:129], in_=STf[1][:, 1:16, 0:129], func=AF.Copy)
                bD = PB.next()
                for c in range(16):
                    cs = slice(c * 128, (c + 1) * 128)
                    for d_ in range(2):
                        cp = c - 1 if d_ == 0 else c + 1
                        has = 0 <= cp < 16
                        j = d_ * 16 + c
                        S.pe.matmul(out=ps[:, bD, j:j + 1], lhsT=smb[:, d_, c, :], rhs=vdir[d_][:, c, 128:129],
                                    start=True, stop=(not has))
                        if has:
                            S.pe.matmul(out=ps[:, bD, j:j + 1], lhsT=qTm[:, cs], rhs=STb[d_][:, cp, 128:129],
                                        start=False, stop=True)
                for d_ in range(2):
                    col = d_ * 4 + hd
                    ebv = EB[:, :, col]
                    S.dve.tensor_tensor(out=sct[:, 0, :], in0=ps[:, bD, d_ * 16:(d_ + 1) * 16], in1=ebv, op=ALU.mult)
                    S.dve.scalar_tensor_tensor(out=sct[:, 1, :], in0=sct[:, 0, :], scalar=-1.0, in1=sct[:, 0, :],
                                               op0=ALU.mult, op1=ALU.max)
                    S.dve.tensor_scalar(out=sct[:, 1, :], in0=sct[:, 1, :], scalar1=1.0, scalar2=None, op0=ALU.max)
                    S.dve.reciprocal(out=sct[:, 2, :], in_=sct[:, 1, :])
                    S.dve.tensor_tensor(out=SC[:, d_, :], in0=sct[:, 2, :], in1=ebv, op=ALU.mult)
                for c in range(16):
                    cs = slice(c * 128, (c + 1) * 128)
                    bN = [PB.next(), PB.next()]
                    if bD in bN:
                        bN = [PB.next(), PB.next()]
                    for d_ in range(2):
                        cp = c - 1 if d_ == 0 else c + 1
                        has = 0 <= cp < 16
                        S.pe.matmul(out=ps[:, bN[d_], 0:128], lhsT=smb[:, d_, c, :], rhs=vdir[d_][:, c, 0:128],
                                    start=True, stop=(not has))
                        if has:
                            S.pe.matmul(out=ps[:, bN[d_], 0:128], lhsT=qTm[:, cs], rhs=STb[d_][:, cp, 0:128],
                                        start=False, stop=True)
                    tmp = tmpr.next()
                    S.act.activation(out=tmp, in_=ps[:, bN[0], 0:128], func=AF.Copy, scale=SC[:, 0, c:c + 1])
                    S.dve.scalar_tensor_tensor(out=hmall[:, c, :], in0=ps[:, bN[1], 0:128], scalar=SC[:, 1, c:c + 1],
                                               in1=tmp, op0=ALU.mult, op1=ALU.add)
                S.dve.tensor_tensor(out=hsq, in0=hmall, in1=hmall, op=ALU.mult)
                S.dve.tensor_reduce(out=stat[:, 0:16], in_=hsq, axis=AX.X, op=ALU.add)
                S.act.activation(out=stat[:, 16:32], in_=stat[:, 0:16], func=AF.Ln, bias=EPS, scale=1.0 / 128)
                S.act.activation(out=stat[:, 16:32], in_=stat[:, 16:32], func=AF.Exp, scale=-0.5)
                S.dve.tensor_tensor(out=hmall, in0=hmall, in1=stat[:, 16:32].unsqueeze(2).to_broadcast([128, 16, 128]),
                                    op=ALU.mult)
                S.dve.tensor_tensor(out=ytall, in0=hmall, in1=gs, op=ALU.mult)
                for t4 in range(4):
                    bk = PB.next()
                    for ti in range(4):
                        c = t4 * 4 + ti
                        S.pe.transpose(out=bankb(bk)[:, ti * 128:(ti + 1) * 128], in_=ytall[:, c, :], identity=ident_b)
                    S.act.activation(out=yT[:, hd, t4 * 512:(t4 + 1) * 512], in_=bankb(bk)[:, 0:512], func=AF.Copy,
                                     scale=vcol(V_MLG + hd))
            A.top = m2
            branch_contrib(1, yT)
            A.top = m

        for b in range(nseq):
            load_x(b)
            if "ffn1" in stages:
                ffn(w1a_d, w1b_d, V_NF1)
            if any(s in stages for s in ("att", "ml", "cx")):
                make_hT(V_NMIX)
                if "att" in stages:
                    attention_branch(b)
                if "ml" in stages:
                    mlstm_branch(b)
                if "cx" in stages:
                    cross_branch(b)
            if "ffn2" in stages:
                ffn(w2a_d, w2b_d, V_NF2)
            final_out(b)
        run_sched(nc, S)
        build.last_sched = S
    return nc


_CACHE = {}


def _get_nc(nseq, stages, same=True):
    key = (nseq, tuple(stages), same)
    if key not in _CACHE:
        _CACHE[key] = build(nseq, stages, same)
    return _CACHE[key]


ALL_STAGES = ("ffn1", "att", "ml", "cx", "ffn2")


def run_cores(inputs, cores, nseq=2, stages=ALL_STAGES, same=True, trace=False):
    L = host_layout(inputs)
    L.update(host_consts())
    x = np.asarray(inputs["x"], np.float32)
    mem = np.asarray(inputs["mem"], np.float32)
    nc = _get_nc(nseq, stages, same)
    in_maps = []
    for c in cores:
        m = dict(L)
        m["x"] = np.ascontiguousarray(x[c * 2:c * 2 + nseq])
        m["mem"] = np.ascontiguousarray(mem[c * 2:c * 2 + nseq])
        in_maps.append(m)
    res = run_bass_kernel_spmd(nc, in_maps, core_ids=list(range(len(cores))), trace=trace)
    return res


def kernel(**inputs):
    res = run_cores(inputs, list(range(NCORES)))
    out = np.concatenate([np.asarray(r["out"], np.float32) for r in res.results], axis=0)
    return out
```

```python
import numpy as np
import concourse.bass as bass
import concourse.mybir as mybir
from concourse.bass_utils import run_bass_kernel_spmd

F32 = mybir.dt.float32
BF16 = mybir.dt.bfloat16
AF = mybir.ActivationFunctionType
ALU = mybir.AluOpType
AX = mybir.AxisListType

_WRITE_KEYS = ("out", "accum_out", "ap")
_DT_SIZE = {}


def _dsize(dt):
    s = _DT_SIZE.get(dt)
    if s is None:
        s = mybir.dt.size(dt)
        _DT_SIZE[dt] = s
    return s


def _region(ap):
    sp = str(ap.space)
    if sp == "DRAM":
        return None
    pairs = ap.ap
    pstep, pcnt = pairs[0]
    off = ap.offset
    es = _dsize(ap.dtype)
    if pstep == 0:
        p0 = 0
        fo = off
    else:
        p0 = off // pstep
        fo = off - p0 * pstep
    lo = fo
    hi = fo
    for st, cnt in pairs[1:]:
        if cnt <= 0:
            continue
        ext = st * (cnt - 1)
        if ext >= 0:
            hi += ext
        else:
            lo += ext
    return (ap.tensor.name, p0, p0 + pcnt, lo * es, (hi + 1) * es)


def _overlap(a, b):
    return a[1] < b[2] and b[1] < a[2] and a[3] < b[4] and b[3] < a[4]


def _covers(a, b):
    return a[1] <= b[1] and a[2] >= b[2] and a[3] <= b[3] and a[4] >= b[4]


class _Rec:
    __slots__ = ("eng", "name", "kwargs", "deps", "inc", "dma", "dsem", "dval", "seq", "cnt", "pre", "label")

    def __init__(self, eng, name, kwargs, dma):
        self.eng = eng
        self.name = name
        self.kwargs = kwargs
        self.deps = []
        self.inc = False
        self.dma = dma
        self.dsem = None
        self.dval = 0
        self.cnt = 0
        self.pre = None


class _Proxy:
    def __init__(self, sched, key):
        self._s = sched
        self._k = key

    def __getattr__(self, name):
        def f(**kwargs):
            return self._s._add(self._k, name, kwargs)
        return f


class Sched:
    ENGS = ("pe", "act", "dve", "pool", "sp")

    def __init__(self, nc, n_dma_sems=24, same_engine_sync=True):
        self.nc = nc
        self.q = {e: [] for e in self.ENGS}
        self.tr = {}
        self.same = same_engine_sync
        self.n_dma_sems = n_dma_sems
        self.n_dma = 0
        self.n_dma_q = [0, 0]
        self.dma_last = [None] * n_dma_sems
        self.pe = _Proxy(self, "pe")
        self.act = _Proxy(self, "act")
        self.dve = _Proxy(self, "dve")
        self.pool = _Proxy(self, "pool")
        self.n = 0
        self.out_dmas = []
        self.bank_last = {}
        self.label = None
        self.annotate = False
        self.strict = False

    def dma(self, out, in_, queue="sp", is_output=False, **kw):
        kwargs = dict(out=out, in_=in_)
        kwargs.update(kw)
        rec = self._add(queue, "dma_start", kwargs, dma=True)
        if is_output:
            self.out_dmas.append(rec)
        return rec

    def _add(self, key, name, kwargs, dma=False):
        rec = _Rec(key, name, kwargs, dma)
        rec.seq = self.n
        rec.label = self.label
        self.n += 1
        deps = {}
        reads, writes = [], []
        pbanks = set()
        for k, v in kwargs.items():
            if isinstance(v, bass.AP):
                r = _region(v)
                if r is None:
                    continue
                (writes if k in _WRITE_KEYS else reads).append(r)
                if str(v.space) == "PSUM":
                    for bnk in range(r[3] // 2048, (r[4] - 1) // 2048 + 1):
                        pbanks.add((r[0], bnk))
        for bnk in pbanks:
            bl = self.bank_last.setdefault(bnk, {})
            for e2, r2 in bl.items():
                if e2 != key:
                    deps[id(r2)] = r2
            bl[key] = rec
        if name == "matmul" and kwargs.get("start") is False:
            pass
        safe_same = {}
        def _dep(drec, kind, preg, creg):
            deps[id(drec)] = drec
            if drec.eng == key and not drec.dma and self.same == 2:
                ok = not self.strict
                if kind == "raw":
                    big = (preg[4] - preg[3]) >= 512 and (creg[4] - creg[3]) >= 512
                    ok = big and preg[3] == creg[3] and preg[1] == creg[1] and not self.strict
                if not ok:
                    safe_same[id(drec)] = False
                else:
                    safe_same.setdefault(id(drec), True)
        for r in reads:
            t = self.tr.get(r[0])
            if t is None:
                t = self.tr[r[0]] = [[], {}]
            for wr, wrec in t[0]:
                if _overlap(wr, r):
                    _dep(wrec, "raw", wr, r)
        for w in writes:
            t = self.tr.get(w[0])
            if t is None:
                t = self.tr[w[0]] = [[], {}]
            for wr, wrec in t[0]:
                if _overlap(wr, w):
                    _dep(wrec, "waw", wr, w)
            for (rk, rr, _sq), rrec in t[1].items():
                if _overlap(rr, w):
                    _dep(rrec, "war", rr, w)
        for r in reads:
            t = self.tr[r[0]]
            t[1][(key, r, rec.seq if dma else -1)] = rec
        for w in writes:
            t = self.tr[w[0]]
            t[0] = [(wr, wrec) for (wr, wrec) in t[0] if not _covers(w, wr)]
            t[0].append((w, rec))
            for kk in [kk for kk in t[1] if _covers(w, kk[1])]:
                if t[1][kk] is not rec:
                    del t[1][kk]
        for d in deps.values():
            if d is rec:
                continue
            if d.eng == key and not d.dma:
                if key == "pe" or key == "sp":
                    continue
                if not self.same:
                    continue
                if self.same == 2 and safe_same.get(id(d), False):
                    continue
            rec.deps.append(d)
            if not d.dma:
                d.inc = True
        if dma:
            npool = 6
            qi = 1 if key == "pool" else 0
            k = self.n_dma_q[qi]
            self.n_dma_q[qi] += 1
            if qi:
                nsl = npool
                slot = (self.n_dma_sems - npool) + (k % nsl)
            else:
                nsl = self.n_dma_sems - npool
                slot = k % nsl
            rec.dsem = slot
            rec.dval = 16 * (k // nsl + 1)
            rec.pre = self.dma_last[slot]
            self.dma_last[slot] = rec
            self.n_dma += 1
        self.q[key].append(rec)
        return rec

    def emit(self, sems):
        nc = self.nc
        for e in self.ENGS:
            c = 0
            for rec in self.q[e]:
                if rec.inc and not rec.dma:
                    c += 1
                    rec.cnt = c
        self.stats = {e: [len(self.q[e]), 0, 0] for e in self.ENGS}
        engsem = sems

        def replay(key, eng):
            waited = {}
            for rec in self.q[key]:
                need = {}
                deps = list(rec.deps)
                if rec.dma and rec.pre is not None:
                    deps.append(rec.pre)
                for d in deps:
                    if d.dma:
                        sk = ("dma", d.dsem)
                        v = d.dval
                    else:
                        sk = d.eng
                        v = d.cnt
                    if need.get(sk, 0) < v:
                        need[sk] = v
                for sk, v in need.items():
                    if waited.get(sk, 0) >= v:
                        continue
                    waited[sk] = v
                    sem = engsem["dma"][sk[1]] if isinstance(sk, tuple) else engsem[sk]
                    eng.wait_ge(sem, v)
                    self.stats[key][1] += 1
                ins = getattr(eng, rec.name)(**rec.kwargs)
                if self.annotate and rec.label:
                    ins.annotate(rec.label)
                if rec.dma:
                    ins.then_inc(engsem["dma"][rec.dsem], 16)
                elif rec.inc:
                    ins.then_inc(engsem[key], 1)
                    self.stats[key][2] += 1
            if key == "sp":
                for slot, last in enumerate(self.dma_last):
                    if last is not None:
                        eng.wait_ge(engsem["dma"][slot], last.dval)
                for e in ("pe", "act", "dve", "pool"):
                    c = max([r.cnt for r in self.q[e]] + [0])
                    if c:
                        eng.wait_ge(engsem[e], c)

        return replay


def run_sched(nc, S):
    from contextlib import ExitStack
    with ExitStack() as st:
        sems = {}
        for e in ("pe", "act", "dve", "pool"):
            sems[e] = st.enter_context(nc.semaphore("s_" + e))
        sems["dma"] = [st.enter_context(nc.semaphore("s_dma%d" % i)) for i in range(S.n_dma_sems)]
        block = st.enter_context(nc.Block())
        replay = S.emit(sems)

        @block.tensor
        def _(eng):
            replay("pe", eng)

        @block.scalar
        def _(eng):
            replay("act", eng)

        @block.vector
        def _(eng):
            replay("dve", eng)

        @block.gpsimd
        def _(eng):
            replay("pool", eng)

        @block.sync
        def _(eng):
            replay("sp", eng)

D = 1024
SEQ = 2048
DFF = 2816
NIN = 10256
MEMLEN = 256
NCORES = 8
EPS = 1e-6
OFF_AQ, OFF_AK, OFF_AV = 0, 1536, 3072
OFF_MQ, OFF_MK, OFF_MV, OFF_MO = 4608, 5120, 5632, 6144
OFF_MIF, OFF_CQ, OFF_G = 6656, 6672, 7184
NEGM = -30000.0

V_NF1, V_NMIX, V_NF2, V_NFIN, V_NMEM = 0, 8, 16, 24, 32
V_GQ, V_GK = 40, 41
V_CW = 42
V_CB = 82
V_MLG = 90
V_CXQ, V_CXK = 94, 95
V_GB = 96
V_MGB = 120
NV = 136


def _st_chunks(W):
    K, N = W.shape
    return np.ascontiguousarray(W.reshape(K // 128, 128, N // 128, 128).transpose(2, 1, 0, 3))


def _mv_layout(W):
    K, N = W.shape
    return np.ascontiguousarray(W.reshape(K // 128, 128, N).transpose(1, 0, 2))


def _pp(v, n):
    return np.ascontiguousarray(np.asarray(v, np.float32).reshape(n, 128).T)


def host_consts():
    c = {}
    idn = np.eye(128, dtype=np.float32)
    a = np.arange(128)
    A, B = a[:, None], a[None, :]
    tri_le = (A <= B).astype(np.float32)
    tri_ge = (A >= B).astype(np.float32)
    blk = ((A // 64) == (B // 64)).astype(np.float32)
    partner = np.where((a % 64) < 32, a + 32, a - 32)
    psw = np.zeros((128, 128), np.float32)
    psw[partner, a] = 1.0
    sel = np.zeros((128, 64), np.float32)
    sel[64, :] = 1.0
    selp = np.zeros((128, 128), np.float32)
    selp[:, :64] = sel
    c["cmat"] = np.ascontiguousarray(np.stack([idn, tri_le, tri_ge, psw, np.ones((128, 128), np.float32), selp], axis=1))
    c["cbf"] = np.ascontiguousarray(np.stack([idn, np.ones((128, 128), np.float32), blk, tri_le, tri_ge], axis=1))
    mprev = np.where(A - B >= 64, 0.0, NEGM).astype(np.float32)
    mself = np.where(np.abs(A - B) <= 64, 0.0, NEGM).astype(np.float32)
    mnext = np.where(B - A >= 64, 0.0, NEGM).astype(np.float32)
    m01 = np.concatenate([mprev, mself, mnext, mprev, mself, mnext], axis=1)
    m2 = []
    for w in range(4):
        bq = 32 * w + np.arange(32)[None, :]
        blkm = np.where(np.abs(A - bq) <= 64, 0.0, NEGM).astype(np.float32)
        m2.append(np.tile(blkm, (1, 16)))
    c["amask"] = np.ascontiguousarray(np.concatenate([m01] + m2, axis=1))
    inv = 10000.0 ** (-np.arange(0, 64, 2, dtype=np.float64) / 64.0)
    ang = np.arange(SEQ, dtype=np.float64)[:, None] * inv[None, :]
    j = (a % 64) % 32
    cosT = np.cos(ang)[:, j].T.astype(np.float32)
    sinT = np.sin(ang)[:, j].T.astype(np.float32)
    sgn = np.where((a % 64) < 32, -1.0, 1.0).astype(np.float32)[:, None]
    c["rope"] = np.ascontiguousarray(np.stack([cosT, sinT * sgn], axis=1))
    return c


def host_layout(inp):
    L = {}
    f = lambda k: np.asarray(inp[k], np.float32)[0]
    L["w1a"] = _st_chunks(f("w_ffn1_in"))
    L["w1b"] = _st_chunks(f("w_ffn1_out"))
    L["w2a"] = _st_chunks(f("w_ffn2_in"))
    L["w2b"] = _st_chunks(f("w_ffn2_out"))
    win = f("w_in")
    st_cols = np.concatenate([np.arange(OFF_AQ, OFF_AV), np.arange(OFF_MQ, OFF_MV),
                              np.arange(OFF_CQ, OFF_G), np.arange(OFF_G, NIN)])
    L["wst"] = _st_chunks(win[:, st_cols])
    av = []
    for p in range(4):
        cols = np.concatenate([np.arange(OFF_AV + (g * 4 + p) * 128, OFF_AV + (g * 4 + p + 1) * 128) for g in range(3)])
        av.append(_mv_layout(win[:, cols]))
    L["wav"] = np.ascontiguousarray(np.stack(av))
    mvo = []
    for h in range(4):
        cols = np.concatenate([np.arange(OFF_MV + h * 128, OFF_MV + (h + 1) * 128),
                               np.arange(OFF_MO + h * 128, OFF_MO + (h + 1) * 128)])
        mvo.append(_mv_layout(win[:, cols]))
    L["wmvo"] = np.ascontiguousarray(np.stack(mvo))
    L["wmif"] = _mv_layout(win[:, OFF_MIF:OFF_MIF + 16])
    wkv = f("w_mem_kv")
    L["wck"] = _st_chunks(wkv[:, :512])
    L["wcv"] = _mv_layout(wkv[:, 512:])
    L["wbr"] = np.ascontiguousarray(np.stack([_st_chunks(f("w_br_att")), _st_chunks(f("w_br_ml")),
                                              _st_chunks(f("w_br_cx"))]))
    L["wout"] = _st_chunks(f("w_out"))
    vec = np.zeros((128, NV), np.float32)
    vec[:, V_NF1:V_NF1 + 8] = _pp(f("norm_ffn1"), 8)
    vec[:, V_NMIX:V_NMIX + 8] = _pp(f("norm_mix"), 8)
    vec[:, V_NF2:V_NF2 + 8] = _pp(f("norm_ffn2"), 8)
    vec[:, V_NFIN:V_NFIN + 8] = _pp(f("norm_final"), 8)
    vec[:, V_NMEM:V_NMEM + 8] = _pp(f("norm_mem"), 8)
    vec[:, V_GQ] = np.tile(f("att_q_gain"), 2)
    vec[:, V_GK] = np.tile(f("att_k_gain"), 2)
    cw = f("ml_conv_w")
    for ch in range(8):
        vec[:, V_CW + ch * 5:V_CW + ch * 5 + 5] = cw[:, ch * 128:(ch + 1) * 128].T
    vec[:, V_CB:V_CB + 8] = _pp(f("ml_conv_b"), 8)
    vec[:, V_MLG:V_MLG + 4] = _pp(f("ml_out_gain"), 4)
    vec[:, V_CXQ] = f("cx_q_gain")
    vec[:, V_CXK] = f("cx_k_gain")
    vec[:, V_GB:V_GB + 24] = _pp(f("mix_gate_b"), 24)
    vec[:, V_MGB:V_MGB + 16] = np.broadcast_to(f("ml_gate_b")[None, :], (128, 16))
    L["vecs"] = vec
    return L


class Arena:
    def __init__(self, t, nbytes):
        self.t = t
        self.cap = nbytes
        self.top = 0

    def alloc(self, nbytes, align=64):
        off = (self.top + align - 1) // align * align
        self.top = off + nbytes
        assert self.top <= self.cap, ("arena overflow", self.top, self.cap)
        return off

    def view(self, off, dt, shape, p0=0, p1=128):
        es = _dsize(dt)
        n = 1
        for s in shape:
            n *= s
        nb = n * es
        assert off % 4 == 0 and nb % 4 == 0
        ap = self.t[p0:p1, off // 4:(off + nb) // 4]
        if es != 4:
            ap = ap.bitcast(dt)
        if len(shape) > 1:
            names = ["a%d" % i for i in range(len(shape))]
            ap = ap.rearrange("p (%s) -> p %s" % (" ".join(names), " ".join(names)),
                              **{nm: s for nm, s in zip(names, shape)})
        return ap

    def new(self, dt, shape, p0=0, p1=128):
        n = 1
        for s in shape:
            n *= s
        off = self.alloc(n * _dsize(dt))
        return self.view(off, dt, shape, p0, p1)


class Rot:
    def __init__(self, items):
        self.items = list(items)
        self.i = 0

    def next(self):
        v = self.items[self.i % len(self.items)]
        self.i += 1
        return v


def build(nseq=2, stages=("ffn1", "att", "ml", "cx", "ffn2"), same_engine_sync=True):
    from contextlib import ExitStack
    nc = bass.Bass("TRN2", target_bir_lowering=False)

    def dr(name, shape, kind="ExternalInput"):
        return nc.dram_tensor(name, list(shape), F32, kind=kind).ap()

    x_d = dr("x", [nseq, SEQ, D])
    mem_d = dr("mem", [nseq, MEMLEN, D])
    out_d = dr("out", [nseq, SEQ, D], kind="ExternalOutput")
    w1a_d, w1b_d = dr("w1a", [44, 128, 8, 128]), dr("w1b", [8, 128, 22, 128])
    w2a_d, w2b_d = dr("w2a", [44, 128, 8, 128]), dr("w2b", [8, 128, 22, 128])
    wst_d = dr("wst", [60, 128, 8, 128])
    wav_d = dr("wav", [4, 128, 8, 384])
    wmvo_d = dr("wmvo", [4, 128, 8, 256])
    wmif_d = dr("wmif", [128, 8, 16])
    wck_d = dr("wck", [4, 128, 8, 128])
    wcv_d = dr("wcv", [128, 8, 512])
    wbr_d = dr("wbr", [3, 8, 128, 4, 128])
    wout_d = dr("wout", [8, 128, 8, 128])
    vecs_d = dr("vecs", [128, NV])
    cmat_d = dr("cmat", [128, 6, 128])
    cbf_d = dr("cbf", [128, 5, 128])
    amask_d = dr("amask", [128, 2816])
    rope_d = dr("rope", [128, 2, SEQ])

    ARENA_BYTES = 212480
    with ExitStack() as st:
        arena_t = st.enter_context(nc.sbuf_tensor("arena", [128, ARENA_BYTES // 4], F32))
        ps = st.enter_context(nc.psum_tensor("ps", [128, 8, 512], F32))
        A = Arena(arena_t, ARENA_BYTES)
        S = Sched(nc, same_engine_sync=same_engine_sync)
        import os as _os2
        S.annotate = bool(_os2.environ.get('MK_ANNOTATE'))

        def bank(b, n=512, p0=0, p1=128):
            return ps[p0:p1, b, 0:n]

        def banks(b0, nb, p0=0, p1=128):
            return ps[p0:p1, b0:b0 + nb, :].rearrange("p a b -> p (a b)")

        cmat = A.new(F32, [6, 128])
        cbf = A.new(BF16, [5, 128])
        amask = A.new(BF16, [2816])
        vecs = A.new(F32, [NV])
        dvec = A.new(F32, [8])
        S.dma(out=cmat, in_=cmat_d)
        S.dma(out=vecs, in_=vecs_d)
        S.dma(out=cbf, in_=cbf_d, queue="pool")
        S.dma(out=amask, in_=amask_d, queue="pool")
        ident_f = cmat[:, 0, :]
        tri_le_f, tri_ge_f, psw_f, ones_f = cmat[:, 1, :], cmat[:, 2, :], cmat[:, 3, :], cmat[:, 4, :]
        sel_f = cmat[:, 5, 0:64]
        ident_b, ones_b, blk_b, tri_le_b, tri_ge_b = (cbf[:, i, :] for i in range(5))
        S.dve.tensor_scalar(out=dvec[:, 0:1], in0=vecs[:, V_GQ:V_GQ + 1], scalar1=0.125, scalar2=None, op0=ALU.mult)
        S.dve.tensor_scalar(out=dvec[:, 1:2], in0=vecs[:, V_CXQ:V_CXQ + 1], scalar1=float(128 ** -0.5), scalar2=None, op0=ALU.mult)

        xT = A.new(F32, [8, SEQ])
        hT = A.new(BF16, [8, SEQ])
        phase_mark = A.top

        def vcol(c, n=1):
            return vecs[:, c:c + n]

        flip = [0]

        def evac(out, in_):
            flip[0] ^= 1
            if flip[0]:
                S.act.activation(out=out, in_=in_, func=AF.Copy)
            else:
                S.dve.tensor_copy(out=out, in_=in_)

        def rms_rstd(tb, sqrot, r1, rstd, pbank, nfeat_inv=1.0 / D):
            for kc in range(8):
                sq = sqrot.next()
                xs_ = xT[:, kc, tb * 512:(tb + 1) * 512]
                if kc % 4 == 1:
                    S.dve.tensor_tensor(out=sq, in0=xs_, in1=xs_, op=ALU.mult)
                elif kc % 4 == 3:
                    S.pool.tensor_tensor(out=sq, in0=xs_, in1=xs_, op=ALU.mult)
                else:
                    S.act.activation(out=sq, in_=xs_, func=AF.Square)
                S.pe.matmul(out=bank(pbank), lhsT=ones_b, rhs=sq, start=(kc == 0), stop=(kc == 7))
            S.act.activation(out=r1, in_=bank(pbank), func=AF.Ln, bias=EPS, scale=nfeat_inv)
            S.act.activation(out=rstd, in_=r1, func=AF.Exp, scale=-0.5)

        def make_hT(gcol):
            S.label = 'norm'
            m = A.top
            sqrot = Rot([A.new(BF16, [512]) for _ in range(4)])
            r1 = A.new(F32, [512])
            rstdr = Rot([A.new(F32, [512]) for _ in range(2)])
            for tb in range(4):
                rstd = rstdr.next()
                rms_rstd(tb, sqrot, r1, rstd, 7)
                for kc in range(8):
                    S.dve.scalar_tensor_tensor(out=hT[:, kc, tb * 512:(tb + 1) * 512],
                                               in0=xT[:, kc, tb * 512:(tb + 1) * 512],
                                               scalar=vcol(gcol + kc), in1=rstd, op0=ALU.mult, op1=ALU.mult)
            A.top = m

        def ffn(wa_d, wb_d, gcol):
            make_hT(gcol)
            S.label = 'ffn'
            m = A.top
            act = A.new(BF16, [11, SEQ])
            wrot = Rot([A.new(BF16, [2, 8, 128]) for _ in range(3)])
            w2rot = Rot([A.new(BF16, [11, 128]) for _ in range(2)])
            sgrot = Rot([A.new(BF16, [1024]) for _ in range(2)])
            setflip = 0
            for half in range(2):
                for jj in range(11):
                    j = half * 11 + jj
                    wb = wrot.next()
                    S.dma(out=wb[:, 0], in_=wa_d[j], queue="pool")
                    S.dma(out=wb[:, 1], in_=wa_d[22 + j], queue="pool")
                    for th in range(2):
                        b0 = 4 * setflip
                        setflip ^= 1
                        for kc in range(8):
                            for gu in range(2):
                                for n in range(2):
                                    t0 = th * 1024 + n * 512
                                    S.pe.matmul(out=bank(b0 + gu * 2 + n), lhsT=wb[:, gu, kc, :],
                                                rhs=hT[:, kc, t0:t0 + 512], start=(kc == 0), stop=(kc == 7))
                        sg = sgrot.next()
                        S.act.activation(out=sg, in_=banks(b0, 2), func=AF.Silu)
                        S.dve.tensor_tensor(out=act[:, jj, th * 1024:(th + 1) * 1024], in0=banks(b0 + 2, 2),
                                            in1=sg, op=ALU.mult)
                for d in range(8):
                    w2 = w2rot.next()
                    S.dma(out=w2, in_=wb_d[d, :, half * 11:(half + 1) * 11, :], queue="pool")
                    b0 = 4 * setflip
                    setflip ^= 1
                    for jj in range(11):
                        for tb in range(4):
                            S.pe.matmul(out=bank(b0 + tb), lhsT=w2[:, jj, :], rhs=act[:, jj, tb * 512:(tb + 1) * 512],
                                        start=(jj == 0), stop=(jj == 10))
                    for tb in range(4):
                        xs = xT[:, d, tb * 512:(tb + 1) * 512]
                        S.dve.scalar_tensor_tensor(out=xs, in0=bank(b0 + tb), scalar=0.5, in1=xs,
                                                   op0=ALU.mult, op1=ALU.add)
            A.top = m

        def load_x(b):
            S.label = 'load'
            m = A.top
            xin = Rot([A.new(F32, [D]) for _ in range(4)])
            for tt in range(16):
                xi = xin.next()
                S.dma(out=xi, in_=x_d[b, tt * 128:(tt + 1) * 128, :])
                for hb in range(2):
                    bk = (tt * 2 + hb) % 4
                    for q in range(4):
                        kc = hb * 4 + q
                        S.pe.transpose(out=ps[:, bk, q * 128:(q + 1) * 128], in_=xi[:, kc * 128:(kc + 1) * 128],
                                       identity=ident_f)
                    evac(xT[:, hb * 4:hb * 4 + 4, tt * 128:(tt + 1) * 128],
                         ps[:, bk, :].rearrange("p (a b) -> p a b", a=4))
            A.top = m

        def final_out(b):
            S.label = 'final'
            m = A.top
            sqrot = Rot([A.new(BF16, [512]) for _ in range(3)])
            r1r_ = Rot([A.new(F32, [512]) for _ in range(2)])
            rstdr_ = Rot([A.new(F32, [512]) for _ in range(2)])
            xnr = Rot([A.new(F32, [8, 512]) for _ in range(2)])
            orot = Rot([A.new(F32, [D]) for _ in range(3)])
            for tb in range(4):
                r1, rstd, xn = r1r_.next(), rstdr_.next(), xnr.next()
                rms_rstd(tb, sqrot, r1, rstd, 6 + tb % 2)
                for kc in range(8):
                    S.dve.scalar_tensor_tensor(out=xn[:, kc, :], in0=xT[:, kc, tb * 512:(tb + 1) * 512],
                                               scalar=vcol(V_NFIN + kc), in1=rstd, op0=ALU.mult, op1=ALU.mult)
                for q in range(4):
                    ot = orot.next()
                    for hb in range(2):
                        bk = (q * 2 + hb) % 4
                        for r in range(4):
                            kc = hb * 4 + r
                            S.pe.transpose(out=ps[:, bk, r * 128:(r + 1) * 128], in_=xn[:, kc, q * 128:(q + 1) * 128],
                                           identity=ident_f)
                        evac(ot[:, hb * 512:(hb + 1) * 512], ps[:, bk, :])
                    t0 = tb * 512 + q * 128
                    S.dma(out=out_d[b, t0:t0 + 128, :], in_=ot, is_output=True)
            A.top = m

        P2 = Rot([0, 2])
        P1 = Rot([4, 5, 6, 7])

        def bankb(bk):
            return ps[:, bk, :].bitcast(BF16)

        def branch_contrib(bi, yT):
            S.label = "contrib"
            m = A.top
            cT = A.new(BF16, [8, SEQ])
            wbr = Rot([A.new(BF16, [4, 128]) for _ in range(2)])
            wgr = Rot([A.new(BF16, [8, 128]) for _ in range(2)])
            wor = Rot([A.new(BF16, [8, 128]) for _ in range(2)])
            grot = Rot([A.new(F32, [1024]) for _ in range(2)])
            sf = 0
            for d in range(8):
                wb = wbr.next()
                S.dma(out=wb, in_=wbr_d[bi, d], queue="pool")
                wg = wgr.next()
                S.dma(out=wg, in_=wst_d[36 + bi * 8 + d], queue="pool")
                for th in range(2):
                    b0 = 4 * sf
                    sf ^= 1
                    for kc in range(4):
                        for n in range(2):
                            t0 = th * 1024 + n * 512
                            S.pe.matmul(out=bank(b0 + n), lhsT=wb[:, kc, :], rhs=yT[:, kc, t0:t0 + 512],
                                        start=(kc == 0), stop=(kc == 3))
                    for kc in range(8):
                        for n in range(2):
                            t0 = th * 1024 + n * 512
                            S.pe.matmul(out=bank(b0 + 2 + n), lhsT=wg[:, kc, :], rhs=hT[:, kc, t0:t0 + 512],
                                        start=(kc == 0), stop=(kc == 7))
                    g = grot.next()
                    S.act.activation(out=g, in_=banks(b0 + 2, 2), func=AF.Sigmoid,
                                     bias=vcol(V_GB + bi * 8 + d), scale=1.0)
                    S.dve.tensor_tensor(out=cT[:, d, th * 1024:(th + 1) * 1024], in0=banks(b0, 2), in1=g, op=ALU.mult)
            for d2 in range(8):
                w = wor.next()
                S.dma(out=w, in_=wout_d[d2], queue="pool")
                b0 = 4 * sf
                sf ^= 1
                for kc in range(8):
                    for tb in range(4):
                        S.pe.matmul(out=bank(b0 + tb), lhsT=w[:, kc, :], rhs=cT[:, kc, tb * 512:(tb + 1) * 512],
                                    start=(kc == 0), stop=(kc == 7))
                for tb in range(4):
                    xs = xT[:, d2, tb * 512:(tb + 1) * 512]
                    S.dve.tensor_tensor(out=xs, in0=bank(b0 + tb), in1=xs, op=ALU.add)
            A.top = m

        def cross_branch(b):
            S.label = 'cx'
            m = A.top
            yT = A.new(BF16, [4, SEQ])
            m2 = A.top
            memt = A.new(F32, [2, D])
            memn = A.new(BF16, [2, D])
            junk = A.new(BF16, [D])
            memnT = A.new(BF16, [8, 256])
            ckT = A.new(BF16, [4, 256])
            cv = A.new(BF16, [2, 512])
            ssm = A.new(F32, [4])
            wcv = A.new(BF16, [8, 512])
            wrot = Rot([A.new(BF16, [8, 128]) for _ in range(2)])
            sqr = Rot([A.new(BF16, [512]) for _ in range(2)])
            r1 = A.new(F32, [512])
            rstd = A.new(F32, [512])
            cqr = Rot([A.new(BF16, [512]) for _ in range(2)])
            ptr = Rot([A.new(BF16, [2, 512]) for _ in range(2)])
            rinv = A.new(F32, [512])
            S.dve.memset(ap=ssm, constant=0.0)
            S.dma(out=wcv, in_=wcv_d, queue="pool")
            for kt in range(2):
                S.dma(out=memt[:, kt, :], in_=mem_d[b, kt * 128:(kt + 1) * 128, :])
            for kt in range(2):
                S.act.activation(out=junk, in_=memt[:, kt, :], func=AF.Square, accum_out=ssm[:, kt:kt + 1])
                S.act.activation(out=ssm[:, 2 + kt:3 + kt], in_=ssm[:, kt:kt + 1], func=AF.Ln, bias=EPS, scale=1.0 / D)
                S.act.activation(out=ssm[:, 2 + kt:3 + kt], in_=ssm[:, 2 + kt:3 + kt], func=AF.Exp, scale=-0.5)
                S.dve.tensor_scalar(out=memn[:, kt, :], in0=memt[:, kt, :], scalar1=ssm[:, 2 + kt:3 + kt], scalar2=None,
                                    op0=ALU.mult)
                for hb in range(2):
                    bk = P1.next()
                    for q in range(4):
                        kc = hb * 4 + q
                        S.pe.transpose(out=bankb(bk)[:, q * 128:(q + 1) * 128], in_=memn[:, kt, kc * 128:(kc + 1) * 128],
                                       identity=ident_b)
                    for q in range(4):
                        kc = hb * 4 + q
                        S.act.activation(out=memnT[:, kc, kt * 128:(kt + 1) * 128], in_=bankb(bk)[:, q * 128:(q + 1) * 128],
                                         func=AF.Copy, scale=vcol(V_NMEM + kc))
            for hd in range(4):
                w = wrot.next()
                S.dma(out=w, in_=wck_d[hd], queue="pool")
                braw, bss = P1.next(), P1.next()
                for kc in range(8):
                    S.pe.matmul(out=bank(braw, 256), lhsT=w[:, kc, :], rhs=memnT[:, kc, :], start=(kc == 0), stop=(kc == 7))
                sq = sqr.next()
                S.act.activation(out=sq[:, 0:256], in_=bank(braw, 256), func=AF.Square)
                S.pe.matmul(out=bank(bss, 256), lhsT=ones_b, rhs=sq[:, 0:256], start=True, stop=True)
                S.act.activation(out=r1[:, 0:256], in_=bank(bss, 256), func=AF.Ln, bias=EPS, scale=1.0 / 128)
                S.act.activation(out=rstd[:, 0:256], in_=r1[:, 0:256], func=AF.Exp, scale=-0.5)
                S.dve.scalar_tensor_tensor(out=ckT[:, hd, :], in0=bank(braw, 256), scalar=vcol(V_CXK), in1=rstd[:, 0:256],
                                           op0=ALU.mult, op1=ALU.mult)
            for kt in range(2):
                bk = P1.next()
                for kc in range(8):
                    S.pe.matmul(out=bank(bk), lhsT=memnT[:, kc, kt * 128:(kt + 1) * 128], rhs=wcv[:, kc, :],
                                start=(kc == 0), stop=(kc == 7))
                evac(cv[:, kt, :], bank(bk))
            its = [(hd, tb) for hd in range(4) for tb in range(4)]
            RB, SB_ = Rot([0, 1]), Rot([2, 3])
            r1r = Rot([r1, A.new(F32, [512])])
            rstdr = Rot([rstd, A.new(F32, [512])])
            rinvr = Rot([rinv, A.new(F32, [512])])
            hold = {}

            def s1(i):
                hd, tb = its[i]
                if tb == 0:
                    w = wrot.next()
                    S.dma(out=w, in_=wst_d[32 + hd], queue="pool")
                    hold["w"] = w
                w = hold["w"]
                braw = RB.next()
                for kc in range(8):
                    S.pe.matmul(out=bank(braw), lhsT=w[:, kc, :], rhs=hT[:, kc, tb * 512:(tb + 1) * 512],
                                start=(kc == 0), stop=(kc == 7))
                sq = sqr.next()
                S.act.activation(out=sq, in_=bank(braw), func=AF.Square)
                hold[i] = {"braw": braw, "sq": sq}

            def s2(i):
                h_ = hold[i]
                bss = SB_.next()
                S.pe.matmul(out=bank(bss), lhsT=ones_b, rhs=h_["sq"], start=True, stop=True)
                r1_, rstd_ = r1r.next(), rstdr.next()
                S.act.activation(out=r1_, in_=bank(bss), func=AF.Ln, bias=EPS, scale=1.0 / 128)
                S.act.activation(out=rstd_, in_=r1_, func=AF.Exp, scale=-0.5)
                cq = cqr.next()
                S.dve.scalar_tensor_tensor(out=cq, in0=bank(h_["braw"]), scalar=dvec[:, 1:2], in1=rstd_,
                                           op0=ALU.mult, op1=ALU.mult)
                h_["cq"] = cq

            def s3(i):
                hd, tb = its[i]
                h_ = hold[i]
                for kt in range(2):
                    S.pe.matmul(out=bank(4 + kt), lhsT=ckT[:, hd, kt * 128:(kt + 1) * 128], rhs=h_["cq"], start=True, stop=True)
                pt = ptr.next()
                S.act.activation(out=pt.rearrange("p a b -> p (a b)"), in_=banks(4, 2), func=AF.Exp)
                h_["pt"] = pt

            def s4(i):
                hd, tb = its[i]
                h_ = hold.pop(i)
                pt = h_["pt"]
                for kt in range(2):
                    S.pe.matmul(out=bank(6), lhsT=cv[:, kt, hd * 128:(hd + 1) * 128], rhs=pt[:, kt, :],
                                start=(kt == 0), stop=(kt == 1))
                for kt in range(2):
                    S.pe.matmul(out=bank(7), lhsT=ones_b, rhs=pt[:, kt, :], start=(kt == 0), stop=(kt == 1))
                r1_, rinv_ = r1r.next(), rinvr.next()
                S.act.activation(out=r1_, in_=bank(7), func=AF.Ln)
                S.act.activation(out=rinv_, in_=r1_, func=AF.Exp, scale=-1.0)
                S.dve.tensor_tensor(out=yT[:, hd, tb * 512:(tb + 1) * 512], in0=bank(6), in1=rinv_, op=ALU.mult)

            n_it = len(its)
            for step in range(n_it + 3):
                if step < n_it:
                    s1(step)
                if 0 <= step - 1 < n_it:
                    s2(step - 1)
                if 0 <= step - 2 < n_it:
                    s3(step - 2)
                if 0 <= step - 3 < n_it:
                    s4(step - 3)
            A.top = m2
            branch_contrib(2, yT)
            A.top = m

        def attention_branch(b):
            m = A.top
            yT = A.new(BF16, [4, SEQ])
            m2 = A.top
            kT = A.new(BF16, [3, SEQ])
            qT = A.new(BF16, [3, SEQ])
            Vt = [A.new(BF16, [16, 2, 66]) for _ in range(3)]
            wav = A.new(BF16, [8, 384])
            wrot = Rot([A.new(BF16, [8, 128]) for _ in range(2)])
            ropet = Rot([A.new(F32, [2, 512]) for _ in range(2)])
            sqr = Rot([A.new(BF16, [512]) for _ in range(2)])
            rgr = Rot([A.new(F32, [512]) for _ in range(2)])
            r1r = Rot([A.new(F32, [512]) for _ in range(2)])
            rstdr = Rot([A.new(F32, [512]) for _ in range(2)])
            t1r = Rot([A.new(F32, [512]) for _ in range(2)])
            t2r = Rot([A.new(F32, [512]) for _ in range(2)])
            ptr = Rot([A.new(BF16, [1024]) for _ in range(2)])
            ytr = Rot([A.new(BF16, [512]) for _ in range(2)])

            def res_view(ap2d, r):
                return ap2d.rearrange("p (i r) -> p r i", r=r)

            def tok_ap(kc, g, tile):
                if g == 0:
                    return hT[:, kc, tile * 128:(tile + 1) * 128]
                if g == 1:
                    c, u = divmod(tile, 4)
                    return res_view(hT[:, kc, :], 4)[:, c, u * 128:(u + 1) * 128]
                return res_view(hT[:, kc, :], 16)[:, tile, :]

            for p in range(4):
                S.label = 'att_v'
                S.dma(out=wav, in_=wav_d[p], queue="pool")
                vfill = []
                for g in range(3):
                    S.dve.memset(ap=Vt[g], constant=1.0)
                    for t2_ in range(8):
                        def vf(g=g, t2_=t2_):
                            lab = S.label
                            S.label = 'att_v'
                            bk = PA.next()
                            for ti in range(2):
                                tile = t2_ * 2 + ti
                                for kc in range(8):
                                    S.pe.matmul(out=ps[:, bk, ti * 128:(ti + 1) * 128], lhsT=tok_ap(kc, g, tile),
                                                rhs=wav[:, kc, g * 128:(g + 1) * 128], start=(kc == 0), stop=(kc == 7))
                            evac(Vt[g][:, t2_ * 2:(t2_ + 1) * 2, :, 0:64],
                                 ps[:, bk, 0:256].rearrange("p (a s d) -> p a s d", a=2, s=2))
                            S.label = lab
                        vfill.append(vf)
                S.label = 'att_qk'
                PA = Rot(list(range(8)))
                its = [(isq, g, tb) for isq in (0, 1) for g in range(3) for tb in range(4)]
                state = {}

                def stage_a(it):
                    isq, g, tb = it
                    if tb == 0:
                        w = wrot.next()
                        S.dma(out=w, in_=wst_d[(0 if isq else 12) + g * 4 + p], queue="pool")
                        state["w"] = w
                    w = state["w"]
                    rt = ropet.next()
                    S.dma(out=rt, in_=rope_d[:, :, tb * 512:(tb + 1) * 512])
                    braw = PA.next()
                    for kc in range(8):
                        S.pe.matmul(out=bank(braw), lhsT=w[:, kc, :], rhs=hT[:, kc, tb * 512:(tb + 1) * 512],
                                    start=(kc == 0), stop=(kc == 7))
                    sq = sqr.next()
                    rg = rgr.next()
                    gcol = dvec[:, 0:1] if isq else vcol(V_GK)
                    S.act.activation(out=rg, in_=bank(braw), func=AF.Copy, scale=gcol)
                    S.act.activation(out=sq, in_=bank(braw), func=AF.Square)
                    return (it, rt, sq, rg)

                def stage_b(st_):
                    (isq, g, tb), rt, sq, rg = st_
                    dstT = qT if isq else kT
                    bss, bsw = PA.next(), PA.next()
                    S.pe.matmul(out=bank(bss), lhsT=blk_b, rhs=sq, start=True, stop=True)
                    S.pe.matmul(out=bank(bsw), lhsT=psw_f, rhs=rg, start=True, stop=True)
                    r1 = r1r.next()
                    rstd = rstdr.next()
                    t2 = t2r.next()
                    S.act.activation(out=r1, in_=bank(bss), func=AF.Ln, bias=EPS, scale=1.0 / 64)
                    S.act.activation(out=rstd, in_=r1, func=AF.Exp, scale=-0.5)
                    t1 = t1r.next()
                    S.pool.tensor_tensor(out=t1, in0=rg, in1=rt[:, 0, :], op=ALU.mult)
                    S.dve.tensor_tensor(out=t2, in0=bank(bsw), in1=rt[:, 1, :], op=ALU.mult)
                    S.dve.tensor_tensor(out=t2, in0=t2, in1=t1, op=ALU.add)
                    S.dve.tensor_tensor(out=dstT[:, g, tb * 512:(tb + 1) * 512], in0=t2, in1=rstd, op=ALU.mult)

                prev = None
                for it in its:
                    cur = stage_a(it)
                    if prev is not None:
                        stage_b(prev)
                        if vfill:
                            vfill.pop(0)()
                    prev = cur
                stage_b(prev)
                while vfill:
                    vfill.pop(0)()
                S.label = 'att_core'
                batches = []
                for sl in range(2):
                    for w_ in range(4):
                        ctx = {"first": True, "OB": None}

                        def mk_pv(ctx):
                            def pv(out_fn, vt_ap, pt_ap):
                                if ctx["OB"] is None:
                                    ctx["OB"] = P1.next()
                                S.pe.matmul(out=out_fn(ctx["OB"]), lhsT=vt_ap, rhs=pt_ap, start=ctx["first"], stop=False,
                                            skip_group_check=True)
                                ctx["first"] = False
                            return pv

                        pv = mk_pv(ctx)
                        pb = 64 * sl
                        kq = (lambda pb: (lambda T, g: T[pb:pb + 64, g, :]))(pb)
                        for g in range(2):
                            for bi in range(2):
                                def qk_fn(g=g, bi=bi, sl=sl, w_=w_, kq=kq):
                                    b2 = P2.next()
                                    pt = ptr.next()
                                    st2 = banks(b2, 2)
                                    S.pe.matmul(out=bank(b2), lhsT=ident_b, rhs=amask[:, 0:512], start=True, stop=False,
                                                skip_group_check=True)
                                    S.pe.matmul(out=bank(b2 + 1, 256), lhsT=ident_b, rhs=amask[:, 512:768], start=True,
                                                stop=False, skip_group_check=True)
                                    blocks = []
                                    for qi in range(2):
                                        if g == 0:
                                            t = 4 * w_ + 2 * bi + qi
                                            q_ap = kq(qT, 0)[:, t * 128:(t + 1) * 128]
                                            ntile = 16
                                            out_fn = (lambda t: (lambda OB: ps[0:65, OB, (t % 4) * 128:(t % 4 + 1) * 128]))(t)
                                        else:
                                            c = 2 * bi + qi
                                            t = w_
                                            q_ap = res_view(kq(qT, 1), 4)[:, c, t * 128:(t + 1) * 128]
                                            ntile = 4
                                            out_fn = (lambda c: (lambda OB: res_view(ps[0:65, OB, :], 4)[:, c, :]))(c)
                                        for ki, kt in enumerate((t - 1, t, t + 1)):
                                            if not (0 <= kt < ntile):
                                                continue
                                            col = (qi * 3 + ki) * 128
                                            if g == 0:
                                                k_ap = kq(kT, 0)[:, kt * 128:(kt + 1) * 128]
                                                v_ap = Vt[0][:, kt, sl, 0:65]
                                            else:
                                                k_ap = res_view(kq(kT, 1), 4)[:, c, kt * 128:(kt + 1) * 128]
                                                v_ap = Vt[1][:, c * 4 + kt, sl, 0:65]
                                            S.pe.matmul(out=st2[:, col:col + 128], lhsT=k_ap, rhs=q_ap, start=False, stop=False,
                                                        skip_group_check=True)
                                            blocks.append((col, v_ap, out_fn))
                                    S.act.activation(out=pt[:, 0:768], in_=st2[:, 0:768], func=AF.Exp)
                                    return (pt, blocks)

                                def pv_fn(res, pv=pv):
                                    pt, blocks = res
                                    for col, v_ap, out_fn in blocks:
                                        pv(out_fn, v_ap, pt[:, col:col + 128])

                                batches.append((qk_fn, pv_fn, None))

                        def qk2_fn(sl=sl, w_=w_, kq=kq):
                            b2 = P2.next()
                            pt = ptr.next()
                            S.pe.matmul(out=bank(b2), lhsT=ident_b, rhs=amask[:, 768 + w_ * 512:768 + (w_ + 1) * 512],
                                        start=True, stop=False, skip_group_check=True)
                            for c in range(16):
                                S.pe.matmul(out=ps[:, b2, c * 32:(c + 1) * 32], lhsT=res_view(kq(kT, 2), 16)[:, c, :],
                                            rhs=res_view(kq(qT, 2), 16)[:, c, w_ * 32:(w_ + 1) * 32], start=False, stop=False,
                                            skip_group_check=True)
                            S.act.activation(out=pt[:, 0:512], in_=bank(b2), func=AF.Exp)
                            return pt

                        def pv2_fn(pt, sl=sl, pv=pv):
                            for c in range(16):
                                pv((lambda c: (lambda OB: res_view(ps[0:65, OB, :], 16)[:, c, :]))(c), Vt[2][:, c, sl, 0:65],
                                   pt[:, c * 32:(c + 1) * 32])

                        def tail_fn(sl=sl, w_=w_, ctx=ctx):
                            OB = ctx["OB"]
                            acc = t2r.next()
                            S.act.activation(out=acc[0:65, :], in_=ps[0:65, OB, :], func=AF.Copy)

                            def tail_b():
                                bl = P1.next()
                                S.pe.matmul(out=ps[0:64, bl, :], lhsT=sel_f[0:65, :], rhs=acc[0:65, :], start=True, stop=True)
                                r1 = r1r.next()
                                rinv = rstdr.next()
                                S.act.activation(out=r1[0:64, :], in_=ps[0:64, bl, :], func=AF.Ln)
                                S.act.activation(out=rinv[0:64, :], in_=r1[0:64, :], func=AF.Exp, scale=-1.0)
                                if sl == 0:
                                    S.dve.tensor_tensor(out=yT[0:64, p, w_ * 512:(w_ + 1) * 512], in0=acc[0:64, :],
                                                        in1=rinv[0:64, :], op=ALU.mult)
                                else:
                                    yt = ytr.next()
                                    S.dve.tensor_tensor(out=yt[0:64, :], in0=acc[0:64, :], in1=rinv[0:64, :], op=ALU.mult)
                                    S.dma(out=yT[64:128, p, w_ * 512:(w_ + 1) * 512], in_=yt[0:64, :])
                            return tail_b

                        batches.append((qk2_fn, pv2_fn, tail_fn))
                prev = None
                pend = None
                for qk_fn, pv_fn, tail_fn in batches:
                    res = qk_fn()
                    if pend is not None:
                        pend()
                        pend = None
                    if prev is not None:
                        prev[0](prev[1])
                        if prev[2] is not None:
                            pend = prev[2]()
                    prev = (pv_fn, res, tail_fn)
                if pend is not None:
                    pend()
                prev[0](prev[1])
                prev[2]()()
            A.top = m2
            branch_contrib(0, yT)
            A.top = m

        LNA = float(-0.5 * np.log(128.0))

        def mlstm_branch(b):
            m = A.top
            yT = A.new(BF16, [4, SEQ])
            m2 = A.top
            G = A.new(F32, [16, 16])
            SP = A.new(F32, [16, 8])
            CUM = A.new(F32, [16, 8])
            EU = A.new(F32, [16, 8])
            EB = A.new(F32, [16, 8])
            EG = A.new(F32, [16, 8])
            wmif = A.new(BF16, [8, 16])
            S.label = 'ml_gates'
            S.dma(out=wmif, in_=wmif_d, queue="pool")
            bk = P1.next()
            for tt in range(16):
                for kc in range(8):
                    S.pe.matmul(out=ps[:, bk, tt * 16:(tt + 1) * 16], lhsT=hT[:, kc, tt * 128:(tt + 1) * 128],
                                rhs=wmif[:, kc, :], start=(kc == 0), stop=(kc == 7))
            S.dve.tensor_tensor(out=G, in0=ps[:, bk, 0:256].rearrange("p (t g) -> p t g", g=16),
                                in1=vecs[:, V_MGB:V_MGB + 16].unsqueeze(1).to_broadcast([128, 16, 16]), op=ALU.add)
            for d_ in range(2):
                S.act.activation(out=SP[:, :, d_ * 4:d_ * 4 + 4], in_=G[:, :, 4 + d_ * 8:8 + d_ * 8], func=AF.Exp, scale=-1.0)
            S.act.activation(out=SP, in_=SP, func=AF.Ln, bias=1.0)
            bkc, bkc2, bkg = P1.next(), P1.next(), P1.next()
            SPf = SP.rearrange("p t g -> p (t g)")
            S.pe.matmul(out=ps[:, bkc, 0:128], lhsT=tri_le_f, rhs=SPf, start=True, stop=True)
            S.pe.matmul(out=ps[:, bkc2, 0:128], lhsT=tri_ge_f, rhs=SPf, start=True, stop=True)
            S.pe.matmul(out=ps[:, bkg, 0:128], lhsT=ones_f, rhs=SPf, start=True, stop=True)
            pcf = ps[:, bkc, 0:128].rearrange("p (t g) -> p t g", g=8)
            pcb = ps[:, bkc2, 0:128].rearrange("p (t g) -> p t g", g=8)
            pg = ps[:, bkg, 0:128].rearrange("p (t g) -> p t g", g=8)
            S.dve.tensor_copy(out=CUM[:, :, 0:4], in_=pcf[:, :, 0:4])
            S.dve.tensor_copy(out=CUM[:, :, 4:8], in_=pcb[:, :, 4:8])
            S.act.activation(out=EB, in_=CUM, func=AF.Exp, scale=-1.0)
            S.act.activation(out=EG, in_=pg, func=AF.Exp, scale=-1.0)
            for d_ in range(2):
                S.dve.tensor_tensor(out=EU[:, :, d_ * 4:d_ * 4 + 4], in0=G[:, :, d_ * 8:d_ * 8 + 4],
                                    in1=CUM[:, :, d_ * 4:d_ * 4 + 4], op=ALU.add)
            S.act.activation(out=EU, in_=EU, func=AF.Exp, bias=LNA)
            qkbuf = [(A.new(BF16, [SEQ]), A.new(BF16, [SEQ])) for _ in range(2)]
            ktok = A.new(BF16, [16, 128])
            vdir = [A.new(BF16, [16, 130]) for _ in range(2)]
            gs = A.new(BF16, [16, 128])
            roff = A.alloc(24960)
            STf = [A.view(roff + d_ * 8320, F32, [16, 130]) for d_ in range(2)]
            STb = [A.view(roff + 16640 + d_ * 4160, BF16, [16, 130]) for d_ in range(2)]
            hmall = A.view(roff, F32, [16, 128])
            ytall = A.view(roff + 8320, BF16, [16, 128])
            cbuf = A.new(BF16, [SEQ + 4])
            dg = A.new(BF16, [10, 128])
            soff = A.alloc(8192)
            smb = A.view(soff, BF16, [2, 16, 128])
            hsq = A.view(soff, F32, [16, 128])
            wq = A.new(BF16, [8, 128])
            wmv = A.new(BF16, [8, 256])
            tmpr = Rot([A.new(F32, [128]) for _ in range(2)])
            SC = A.new(F32, [2, 16])
            sct = A.new(F32, [4, 16])
            stat = A.new(F32, [32])
            PB = Rot(list(range(8)))
            S.dve.memset(ap=cbuf[:, 0:2], constant=0.0)
            S.dve.memset(ap=cbuf[:, SEQ + 2:SEQ + 4], constant=0.0)

            def gen_qk(hd):
                S.label = 'ml_qk'
                qTm, kTm = qkbuf[hd % 2]
                for isk in (0, 1):
                    ch = isk * 4 + hd
                    for j in range(5):
                        S.dve.tensor_scalar(out=dg[:, isk * 5 + j, :], in0=ident_f, scalar1=vcol(V_CW + ch * 5 + j), scalar2=None,
                                            op0=ALU.mult)
                for isk in (0, 1):
                    ch = isk * 4 + hd
                    S.dma(out=wq, in_=wst_d[24 + isk * 4 + hd], queue="pool")
                    for tb in range(4):
                        bk = PB.next()
                        for kc in range(8):
                            S.pe.matmul(out=bank(bk), lhsT=wq[:, kc, :], rhs=hT[:, kc, tb * 512:(tb + 1) * 512],
                                        start=(kc == 0), stop=(kc == 7))
                        evac(cbuf[:, 2 + tb * 512:2 + (tb + 1) * 512], bank(bk))
                    for tb in range(4):
                        bk = PB.next()
                        for j in range(5):
                            S.pe.matmul(out=bank(bk), lhsT=dg[:, isk * 5 + j, :], rhs=cbuf[:, tb * 512 + j:tb * 512 + j + 512],
                                        start=(j == 0), stop=(j == 4))
                        S.act.activation(out=(kTm if isk else qTm)[:, tb * 512:(tb + 1) * 512], in_=bank(bk), func=AF.Silu,
                                         bias=vcol(V_CB + ch), scale=1.0)

            gen_qk(0)
            for hd in range(4):
                qTm, kTm = qkbuf[hd % 2]
                S.label = 'ml_v'
                for t4 in range(4):
                    bk = PB.next()
                    for ti in range(4):
                        tile = t4 * 4 + ti
                        S.pe.transpose(out=bankb(bk)[:, ti * 128:(ti + 1) * 128], in_=kTm[:, tile * 128:(tile + 1) * 128],
                                       identity=ident_b)
                    evac(ktok[:, t4 * 4:(t4 + 1) * 4, :], bankb(bk)[:, 0:512].rearrange("p (a d) -> p a d", a=4))
                S.dma(out=wmv, in_=wmvo_d[hd], queue="pool")
                for t2 in range(8):
                    bk = PB.next()
                    for ti in range(2):
                        tt = t2 * 2 + ti
                        for kc in range(8):
                            S.pe.matmul(out=ps[:, bk, ti * 256:(ti + 1) * 256], lhsT=hT[:, kc, tt * 128:(tt + 1) * 128],
                                        rhs=wmv[:, kc, :], start=(kc == 0), stop=(kc == 7))
                    for ti in range(2):
                        tt = t2 * 2 + ti
                        S.act.activation(out=gs[:, tt, :], in_=ps[:, bk, ti * 256 + 128:(ti + 1) * 256], func=AF.Sigmoid)
                        S.dve.tensor_scalar(out=vdir[0][:, tt, 0:128], in0=ps[:, bk, ti * 256:ti * 256 + 128],
                                            scalar1=EU[:, tt, hd:hd + 1], scalar2=None, op0=ALU.mult)
                        S.act.activation(out=vdir[1][:, tt, 0:128], in_=ps[:, bk, ti * 256:ti * 256 + 128], func=AF.Copy,
                                         scale=EU[:, tt, 4 + hd:5 + hd])
                for d_ in range(2):
                    S.dve.tensor_copy(out=vdir[d_][:, :, 128:129], in_=EU[:, :, d_ * 4 + hd:d_ * 4 + hd + 1])
                S.label = 'ml_rec'
                for d_ in range(2):
                    col = d_ * 4 + hd
                    for c in range(16):
                        if (d_ == 0 and c == 15) or (d_ == 1 and c == 0):
                            continue
                        bU = PB.next()
                        S.pe.matmul(out=ps[:, bU, 0:129], lhsT=ktok[:, c, :], rhs=vdir[d_][:, c, 0:129], start=True, stop=True)
                        S.act.activation(out=STf[d_][:, c, 0:129], in_=ps[:, bU, 0:129], func=AF.Copy,
                                         scale=EG[:, c, col:col + 1])
                for c in range(16):
                    cs = slice(c * 128, (c + 1) * 128)
                    bST = PB.next()
                    S.pe.matmul(out=ps[:, bST, 0:128], lhsT=kTm[:, cs], rhs=qTm[:, cs], start=True, stop=True)
                    S.dve.tensor_tensor(out=smb[:, 0, c, :], in0=ps[:, bST, 0:128], in1=tri_le_b, op=ALU.mult)
                    S.dve.tensor_tensor(out=smb[:, 1, c, :], in0=ps[:, bST, 0:128], in1=tri_ge_b, op=ALU.mult)
                if hd + 1 < 4:
                    gen_qk(hd + 1)
                    S.label = 'ml_rec'
                S.strict = True
                for c in range(1, 15):
                    S.dve.scalar_tensor_tensor(out=STf[0][:, c, 0:129], in0=STf[0][:, c - 1, 0:129], scalar=EG[:, c, hd:hd + 1],
                                               in1=STf[0][:, c, 0:129], op0=ALU.mult, op1=ALU.add)
                    cb = 15 - c
                    S.dve.scalar_tensor_tensor(out=STf[1][:, cb, 0:129], in0=STf[1][:, cb + 1, 0:129],
                                               scalar=EG[:, cb, 4 + hd:5 + hd], in1=STf[1][:, cb, 0:129],
                                               op0=ALU.mult, op1=ALU.add)
                S.strict = False
                S.act.activation(out=STb[0][:, 0:15, 0:129], in_=STf[0][:, 0:15, 0:129], func=AF.Copy)
                S.act.activation(out=STb[1][:, 1:16, 0:129], in_=STf[1][:, 1:16, 0:129], func=AF.Copy)
                bD = PB.next()
                for c in range(16):
                    cs = slice(c * 128, (c + 1) * 128)
                    for d_ in range(2):
                        cp = c - 1 if d_ == 0 else c + 1
                        has = 0 <= cp < 16
                        j = d_ * 16 + c
                        S.pe.matmul(out=ps[:, bD, j:j + 1], lhsT=smb[:, d_, c, :], rhs=vdir[d_][:, c, 128:129],
                                    start=True, stop=(not has))
                        if has:
                            S.pe.matmul(out=ps[:, bD, j:j + 1], lhsT=qTm[:, cs], rhs=STb[d_][:, cp, 128:129],
                                        start=False, stop=True)
                for d_ in range(2):
                    col = d_ * 4 + hd
                    ebv = EB[:, :, col]
                    S.dve.tensor_tensor(out=sct[:, 0, :], in0=ps[:, bD, d_ * 16:(d_ + 1) * 16], in1=ebv, op=ALU.mult)
                    S.dve.scalar_tensor_tensor(out=sct[:, 1, :], in0=sct[:, 0, :], scalar=-1.0, in1=sct[:, 0, :],
                                               op0=ALU.mult, op1=ALU.max)
                    S.dve.tensor_scalar(out=sct[:, 1, :], in0=sct[:, 1, :], scalar1=1.0, scalar2=None, op0=ALU.max)
                    S.dve.reciprocal(out=sct[:, 2, :], in_=sct[:, 1, :])
                    S.dve.tensor_tensor(out=SC[:, d_, :], in0=sct[:, 2, :], in1=ebv, op=ALU.mult)
                for c in range(16):
                    cs = slice(c * 128, (c + 1) * 128)
                    bN = [PB.next(), PB.next()]
                    if bD in bN:
                        bN = [PB.next(), PB.next()]
                    for d_ in range(2):
                        cp = c - 1 if d_ == 0 else c + 1
                        has = 0 <= cp < 16
                        S.pe.matmul(out=ps[:, bN[d_], 0:128], lhsT=smb[:, d_, c, :], rhs=vdir[d_][:, c, 0:128],
                                    start=True, stop=(not has))
                        if has:
                            S.pe.matmul(out=ps[:, bN[d_], 0:128], lhsT=qTm[:, cs], rhs=STb[d_][:, cp, 0:128],
                                        start=False, stop=True)
                    tmp = tmpr.next()
                    S.act.activation(out=tmp, in_=ps[:, bN[0], 0:128], func=AF.Copy, scale=SC[:, 0, c:c + 1])
                    S.dve.scalar_tensor_tensor(out=hmall[:, c, :], in0=ps[:, bN[1], 0:128], scalar=SC[:, 1, c:c + 1],
                                               in1=tmp, op0=ALU.mult, op1=ALU.add)
                S.dve.tensor_tensor(out=hsq, in0=hmall, in1=hmall, op=ALU.mult)
                S.dve.tensor_reduce(out=stat[:, 0:16], in_=hsq, axis=AX.X, op=ALU.add)
                S.act.activation(out=stat[:, 16:32], in_=stat[:, 0:16], func=AF.Ln, bias=EPS, scale=1.0 / 128)
                S.act.activation(out=stat[:, 16:32], in_=stat[:, 16:32], func=AF.Exp, scale=-0.5)
                S.dve.tensor_tensor(out=hmall, in0=hmall, in1=stat[:, 16:32].unsqueeze(2).to_broadcast([128, 16, 128]),
                                    op=ALU.mult)
                S.dve.tensor_tensor(out=ytall, in0=hmall, in1=gs, op=ALU.mult)
                for t4 in range(4):
                    bk = PB.next()
                    for ti in range(4):
                        c = t4 * 4 + ti
                        S.pe.transpose(out=bankb(bk)[:, ti * 128:(ti + 1) * 128], in_=ytall[:, c, :], identity=ident_b)
                    S.act.activation(out=yT[:, hd, t4 * 512:(t4 + 1) * 512], in_=bankb(bk)[:, 0:512], func=AF.Copy,
                                     scale=vcol(V_MLG + hd))
            A.top = m2
            branch_contrib(1, yT)
            A.top = m

        for b in range(nseq):
            load_x(b)
            if "ffn1" in stages:
                ffn(w1a_d, w1b_d, V_NF1)
            if any(s in stages for s in ("att", "ml", "cx")):
                make_hT(V_NMIX)
                if "att" in stages:
                    attention_branch(b)
                if "ml" in stages:
                    mlstm_branch(b)
                if "cx" in stages:
                    cross_branch(b)
            if "ffn2" in stages:
                ffn(w2a_d, w2b_d, V_NF2)
            final_out(b)
        run_sched(nc, S)
        build.last_sched = S
    return nc


_CACHE = {}


def _get_nc(nseq, stages, same=True):
    key = (nseq, tuple(stages), same)
    if key not in _CACHE:
        _CACHE[key] = build(nseq, stages, same)
    return _CACHE[key]


ALL_STAGES = ("ffn1", "att", "ml", "cx", "ffn2")


def run_cores(inputs, cores, nseq=2, stages=ALL_STAGES, same=True, trace=False):
    L = host_layout(inputs)
    L.update(host_consts())
    x = np.asarray(inputs["x"], np.float32)
    mem = np.asarray(inputs["mem"], np.float32)
    nc = _get_nc(nseq, stages, same)
    in_maps = []
    for c in cores:
        m = dict(L)
        m["x"] = np.ascontiguousarray(x[c * 2:c * 2 + nseq])
        m["mem"] = np.ascontiguousarray(mem[c * 2:c * 2 + nseq])
        in_maps.append(m)
    res = run_bass_kernel_spmd(nc, in_maps, core_ids=list(range(len(cores))), trace=trace)
    return res


def kernel(**inputs):
    res = run_cores(inputs, list(range(NCORES)))
    out = np.concatenate([np.asarray(r["out"], np.float32) for r in res.results], axis=0)
    return out
```

```python
import numpy as np
import concourse.bass as bass
import concourse.mybir as mybir
from concourse.bass_utils import run_bass_kernel_spmd

F32 = mybir.dt.float32
BF16 = mybir.dt.bfloat16
AF = mybir.ActivationFunctionType
ALU = mybir.AluOpType
AX = mybir.AxisListType

_WRITE_KEYS = ("out", "accum_out", "ap")
_DT_SIZE = {}


def _dsize(dt):
    s = _DT_SIZE.get(dt)
    if s is None:
        s = mybir.dt.size(dt)
        _DT_SIZE[dt] = s
    return s


def _region(ap):
    sp = str(ap.space)
    if sp == "DRAM":
        return None
    pairs = ap.ap
    pstep, pcnt = pairs[0]
    off = ap.offset
    es = _dsize(ap.dtype)
    if pstep == 0:
        p0 = 0
        fo = off
    else:
        p0 = off // pstep
        fo = off - p0 * pstep
    lo = fo
    hi = fo
    for st, cnt in pairs[1:]:
        if cnt <= 0:
            continue
        ext = st * (cnt - 1)
        if ext >= 0:
            hi += ext
        else:
            lo += ext
    return (ap.tensor.name, p0, p0 + pcnt, lo * es, (hi + 1) * es)


def _overlap(a, b):
    return a[1] < b[2] and b[1] < a[2] and a[3] < b[4] and b[3] < a[4]


def _covers(a, b):
    return a[1] <= b[1] and a[2] >= b[2] and a[3] <= b[3] and a[4] >= b[4]


class _Rec:
    __slots__ = ("eng", "name", "kwargs", "deps", "inc", "dma", "dsem", "dval", "seq", "cnt", "pre", "label")

    def __init__(self, eng, name, kwargs, dma):
        self.eng = eng
        self.name = name
        self.kwargs = kwargs
        self.deps = []
        self.inc = False
        self.dma = dma
        self.dsem = None
        self.dval = 0
        self.cnt = 0
        self.pre = None


class _Proxy:
    def __init__(self, sched, key):
        self._s = sched
        self._k = key

    def __getattr__(self, name):
        def f(**kwargs):
            return self._s._add(self._k, name, kwargs)
        return f


class Sched:
    ENGS = ("pe", "act", "dve", "pool", "sp")

    def __init__(self, nc, n_dma_sems=24, same_engine_sync=True):
        self.nc = nc
        self.q = {e: [] for e in self.ENGS}
        self.tr = {}
        self.same = same_engine_sync
        self.n_dma_sems = n_dma_sems
        self.n_dma = 0
        self.n_dma_q = [0, 0]
        self.dma_last = [None] * n_dma_sems
        self.pe = _Proxy(self, "pe")
        self.act = _Proxy(self, "act")
        self.dve = _Proxy(self, "dve")
        self.pool = _Proxy(self, "pool")
        self.n = 0
        self.out_dmas = []
        self.bank_last = {}
        self.label = None
        self.annotate = False
        self.strict = False

    def dma(self, out, in_, queue="sp", is_output=False, **kw):
        kwargs = dict(out=out, in_=in_)
        kwargs.update(kw)
        rec = self._add(queue, "dma_start", kwargs, dma=True)
        if is_output:
            self.out_dmas.append(rec)
        return rec

    def _add(self, key, name, kwargs, dma=False):
        rec = _Rec(key, name, kwargs, dma)
        rec.seq = self.n
        rec.label = self.label
        self.n += 1
        deps = {}
        reads, writes = [], []
        pbanks = set()
        for k, v in kwargs.items():
            if isinstance(v, bass.AP):
                r = _region(v)
                if r is None:
                    continue
                (writes if k in _WRITE_KEYS else reads).append(r)
                if str(v.space) == "PSUM":
                    for bnk in range(r[3] // 2048, (r[4] - 1) // 2048 + 1):
                        pbanks.add((r[0], bnk))
        for bnk in pbanks:
            bl = self.bank_last.setdefault(bnk, {})
            for e2, r2 in bl.items():
                if e2 != key:
                    deps[id(r2)] = r2
            bl[key] = rec
        if name == "matmul" and kwargs.get("start") is False:
            pass
        safe_same = {}
        def _dep(drec, kind, preg, creg):
            deps[id(drec)] = drec
            if drec.eng == key and not drec.dma and self.same == 2:
                ok = not self.strict
                if kind == "raw":
                    big = (preg[4] - preg[3]) >= 512 and (creg[4] - creg[3]) >= 512
                    ok = big and preg[3] == creg[3] and preg[1] == creg[1] and not self.strict
                if not ok:
                    safe_same[id(drec)] = False
                else:
                    safe_same.setdefault(id(drec), True)
        for r in reads:
            t = self.tr.get(r[0])
            if t is None:
                t = self.tr[r[0]] = [[], {}]
            for wr, wrec in t[0]:
                if _overlap(wr, r):
                    _dep(wrec, "raw", wr, r)
        for w in writes:
            t = self.tr.get(w[0])
            if t is None:
                t = self.tr[w[0]] = [[], {}]
            for wr, wrec in t[0]:
                if _overlap(wr, w):
                    _dep(wrec, "waw", wr, w)
            for (rk, rr, _sq), rrec in t[1].items():
                if _overlap(rr, w):
                    _dep(rrec, "war", rr, w)
        for r in reads:
            t = self.tr[r[0]]
            t[1][(key, r, rec.seq if dma else -1)] = rec
        for w in writes:
            t = self.tr[w[0]]
            t[0] = [(wr, wrec) for (wr, wrec) in t[0] if not _covers(w, wr)]
            t[0].append((w, rec))
            for kk in [kk for kk in t[1] if _covers(w, kk[1])]:
                if t[1][kk] is not rec:
                    del t[1][kk]
        for d in deps.values():
            if d is rec:
                continue
            if d.eng == key and not d.dma:
                if key == "pe" or key == "sp":
                    continue
                if not self.same:
                    continue
                if self.same == 2 and safe_same.get(id(d), False):
                    continue
            rec.deps.append(d)
            if not d.dma:
                d.inc = True
        if dma:
            npool = 6
            qi = 1 if key == "pool" else 0
            k = self.n_dma_q[qi]
            self.n_dma_q[qi] += 1
            if qi:
                nsl = npool
                slot = (self.n_dma_sems - npool) + (k % nsl)
            else:
                nsl = self.n_dma_sems - npool
                slot = k % nsl
            rec.dsem = slot
            rec.dval = 16 * (k // nsl + 1)
            rec.pre = self.dma_last[slot]
            self.dma_last[slot] = rec
            self.n_dma += 1
        self.q[key].append(rec)
        return rec

    def emit(self, sems):
        nc = self.nc
        for e in self.ENGS:
            c = 0
            for rec in self.q[e]:
                if rec.inc and not rec.dma:
                    c += 1
                    rec.cnt = c
        self.stats = {e: [len(self.q[e]), 0, 0] for e in self.ENGS}
        engsem = sems

        def replay(key, eng):
            waited = {}
            for rec in self.q[key]:
                need = {}
                deps = list(rec.deps)
                if rec.dma and rec.pre is not None:
                    deps.append(rec.pre)
                for d in deps:
                    if d.dma:
                        sk = ("dma", d.dsem)
                        v = d.dval
                    else:
                        sk = d.eng
                        v = d.cnt
                    if need.get(sk, 0) < v:
                        need[sk] = v
                for sk, v in need.items():
                    if waited.get(sk, 0) >= v:
                        continue
                    waited[sk] = v
                    sem = engsem["dma"][sk[1]] if isinstance(sk, tuple) else engsem[sk]
                    eng.wait_ge(sem, v)
                    self.stats[key][1] += 1
                ins = getattr(eng, rec.name)(**rec.kwargs)
                if self.annotate and rec.label:
                    ins.annotate(rec.label)
                if rec.dma:
                    ins.then_inc(engsem["dma"][rec.dsem], 16)
                elif rec.inc:
                    ins.then_inc(engsem[key], 1)
                    self.stats[key][2] += 1
            if key == "sp":
                for slot, last in enumerate(self.dma_last):
                    if last is not None:
                        eng.wait_ge(engsem["dma"][slot], last.dval)
                for e in ("pe", "act", "dve", "pool"):
                    c = max([r.cnt for r in self.q[e]] + [0])
                    if c:
                        eng.wait_ge(engsem[e], c)

        return replay


def run_sched(nc, S):
    from contextlib import ExitStack
    with ExitStack() as st:
        sems = {}
        for e in ("pe", "act", "dve", "pool"):
            sems[e] = st.enter_context(nc.semaphore("s_" + e))
        sems["dma"] = [st.enter_context(nc.semaphore("s_dma%d" % i)) for i in range(S.n_dma_sems)]
        block = st.enter_context(nc.Block())
        replay = S.emit(sems)

        @block.tensor
        def _(eng):
            replay("pe", eng)

        @block.scalar
        def _(eng):
            replay("act", eng)

        @block.vector
        def _(eng):
            replay("dve", eng)

        @block.gpsimd
        def _(eng):
            replay("pool", eng)

        @block.sync
        def _(eng):
            replay("sp", eng)

D = 1024
SEQ = 2048
DFF = 2816
NIN = 10256
MEMLEN = 256
NCORES = 8
EPS = 1e-6
OFF_AQ, OFF_AK, OFF_AV = 0, 1536, 3072
OFF_MQ, OFF_MK, OFF_MV, OFF_MO = 4608, 5120, 5632, 6144
OFF_MIF, OFF_CQ, OFF_G = 6656, 6672, 7184
NEGM = -30000.0

V_NF1, V_NMIX, V_NF2, V_NFIN, V_NMEM = 0, 8, 16, 24, 32
V_GQ, V_GK = 40, 41
V_CW = 42
V_CB = 82
V_MLG = 90
V_CXQ, V_CXK = 94, 95
V_GB = 96
V_MGB = 120
NV = 136


def _st_chunks(W):
    K, N = W.shape
    return np.ascontiguousarray(W.reshape(K // 128, 128, N // 128, 128).transpose(2, 1, 0, 3))


def _mv_layout(W):
    K, N = W.shape
    return np.ascontiguousarray(W.reshape(K // 128, 128, N).transpose(1, 0, 2))


def _pp(v, n):
    return np.ascontiguousarray(np.asarray(v, np.float32).reshape(n, 128).T)


def host_consts():
    c = {}
    idn = np.eye(128, dtype=np.float32)
    a = np.arange(128)
    A, B = a[:, None], a[None, :]
    tri_le = (A <= B).astype(np.float32)
    tri_ge = (A >= B).astype(np.float32)
    blk = ((A // 64) == (B // 64)).astype(np.float32)
    partner = np.where((a % 64) < 32, a + 32, a - 32)
    psw = np.zeros((128, 128), np.float32)
    psw[partner, a] = 1.0
    sel = np.zeros((128, 64), np.float32)
    sel[64, :] = 1.0
    selp = np.zeros((128, 128), np.float32)
    selp[:, :64] = sel
    c["cmat"] = np.ascontiguousarray(np.stack([idn, tri_le, tri_ge, psw, np.ones((128, 128), np.float32), selp], axis=1))
    c["cbf"] = np.ascontiguousarray(np.stack([idn, np.ones((128, 128), np.float32), blk, tri_le, tri_ge, psw], axis=1))
    mprev = np.where(A - B >= 64, 0.0, NEGM).astype(np.float32)
    mself = np.where(np.abs(A - B) <= 64, 0.0, NEGM).astype(np.float32)
    mnext = np.where(B - A >= 64, 0.0, NEGM).astype(np.float32)
    m01 = np.concatenate([mprev, mself, mnext, mprev, mself, mnext], axis=1)
    m2 = []
    for w in range(4):
        bq = 32 * w + np.arange(32)[None, :]
        blkm = np.where(np.abs(A - bq) <= 64, 0.0, NEGM).astype(np.float32)
        m2.append(np.tile(blkm, (1, 16)))
    c["amask"] = np.ascontiguousarray(np.concatenate([m01] + m2, axis=1))
    inv = 10000.0 ** (-np.arange(0, 64, 2, dtype=np.float64) / 64.0)
    ang = np.arange(SEQ, dtype=np.float64)[:, None] * inv[None, :]
    j = (a % 64) % 32
    cosT = np.cos(ang)[:, j].T.astype(np.float32)
    sinT = np.sin(ang)[:, j].T.astype(np.float32)
    sgn = np.where((a % 64) < 32, -1.0, 1.0).astype(np.float32)[:, None]
    c["rope"] = np.ascontiguousarray(np.stack([cosT, sinT * sgn], axis=1))
    return c


def host_layout(inp):
    L = {}
    f = lambda k: np.asarray(inp[k], np.float32)[0]
    L["w1a"] = _st_chunks(f("w_ffn1_in"))
    L["w1b"] = _st_chunks(f("w_ffn1_out"))
    L["w2a"] = _st_chunks(f("w_ffn2_in"))
    L["w2b"] = _st_chunks(f("w_ffn2_out"))
    win = f("w_in")
    st_cols = np.concatenate([np.arange(OFF_AQ, OFF_AV), np.arange(OFF_MQ, OFF_MV),
                              np.arange(OFF_CQ, OFF_G), np.arange(OFF_G, NIN)])
    L["wst"] = _st_chunks(win[:, st_cols])
    av = []
    for p in range(4):
        cols = np.concatenate([np.arange(OFF_AV + (g * 4 + p) * 128, OFF_AV + (g * 4 + p + 1) * 128) for g in range(3)])
        av.append(_mv_layout(win[:, cols]))
    L["wav"] = np.ascontiguousarray(np.stack(av))
    mvo = []
    for h in range(4):
        cols = np.concatenate([np.arange(OFF_MV + h * 128, OFF_MV + (h + 1) * 128),
                               np.arange(OFF_MO + h * 128, OFF_MO + (h + 1) * 128)])
        mvo.append(_mv_layout(win[:, cols]))
    L["wmvo"] = np.ascontiguousarray(np.stack(mvo))
    L["wmif"] = _mv_layout(win[:, OFF_MIF:OFF_MIF + 16])
    wkv = f("w_mem_kv")
    L["wck"] = _st_chunks(wkv[:, :512])
    L["wcv"] = _mv_layout(wkv[:, 512:])
    L["wbr"] = np.ascontiguousarray(np.stack([_st_chunks(f("w_br_att")), _st_chunks(f("w_br_ml")),
                                              _st_chunks(f("w_br_cx"))]))
    L["wout"] = _st_chunks(f("w_out"))
    vec = np.zeros((128, NV), np.float32)
    vec[:, V_NF1:V_NF1 + 8] = _pp(f("norm_ffn1"), 8)
    vec[:, V_NMIX:V_NMIX + 8] = _pp(f("norm_mix"), 8)
    vec[:, V_NF2:V_NF2 + 8] = _pp(f("norm_ffn2"), 8)
    vec[:, V_NFIN:V_NFIN + 8] = _pp(f("norm_final"), 8)
    vec[:, V_NMEM:V_NMEM + 8] = _pp(f("norm_mem"), 8)
    vec[:, V_GQ] = np.tile(f("att_q_gain"), 2)
    vec[:, V_GK] = np.tile(f("att_k_gain"), 2)
    cw = f("ml_conv_w")
    for ch in range(8):
        vec[:, V_CW + ch * 5:V_CW + ch * 5 + 5] = cw[:, ch * 128:(ch + 1) * 128].T
    vec[:, V_CB:V_CB + 8] = _pp(f("ml_conv_b"), 8)
    vec[:, V_MLG:V_MLG + 4] = _pp(f("ml_out_gain"), 4)
    vec[:, V_CXQ] = f("cx_q_gain")
    vec[:, V_CXK] = f("cx_k_gain")
    vec[:, V_GB:V_GB + 24] = _pp(f("mix_gate_b"), 24)
    vec[:, V_MGB:V_MGB + 16] = np.broadcast_to(f("ml_gate_b")[None, :], (128, 16))
    L["vecs"] = vec
    return L


class Arena:
    def __init__(self, t, nbytes):
        self.t = t
        self.cap = nbytes
        self.top = 0

    def alloc(self, nbytes, align=64):
        off = (self.top + align - 1) // align * align
        self.top = off + nbytes
        assert self.top <= self.cap, ("arena overflow", self.top, self.cap)
        return off

    def view(self, off, dt, shape, p0=0, p1=128):
        es = _dsize(dt)
        n = 1
        for s in shape:
            n *= s
        nb = n * es
        assert off % 4 == 0 and nb % 4 == 0
        ap = self.t[p0:p1, off // 4:(off + nb) // 4]
        if es != 4:
            ap = ap.bitcast(dt)
        if len(shape) > 1:
            names = ["a%d" % i for i in range(len(shape))]
            ap = ap.rearrange("p (%s) -> p %s" % (" ".join(names), " ".join(names)),
                              **{nm: s for nm, s in zip(names, shape)})
        return ap

    def new(self, dt, shape, p0=0, p1=128):
        n = 1
        for s in shape:
            n *= s
        off = self.alloc(n * _dsize(dt))
        return self.view(off, dt, shape, p0, p1)


class Rot:
    def __init__(self, items):
        self.items = list(items)
        self.i = 0

    def next(self):
        v = self.items[self.i % len(self.items)]
        self.i += 1
        return v


def build(nseq=2, stages=("ffn1", "att", "ml", "cx", "ffn2"), same_engine_sync=True):
    from contextlib import ExitStack
    nc = bass.Bass("TRN2", target_bir_lowering=False)

    def dr(name, shape, kind="ExternalInput"):
        return nc.dram_tensor(name, list(shape), F32, kind=kind).ap()

    x_d = dr("x", [nseq, SEQ, D])
    mem_d = dr("mem", [nseq, MEMLEN, D])
    out_d = dr("out", [nseq, SEQ, D], kind="ExternalOutput")
    w1a_d, w1b_d = dr("w1a", [44, 128, 8, 128]), dr("w1b", [8, 128, 22, 128])
    w2a_d, w2b_d = dr("w2a", [44, 128, 8, 128]), dr("w2b", [8, 128, 22, 128])
    wst_d = dr("wst", [60, 128, 8, 128])
    wav_d = dr("wav", [4, 128, 8, 384])
    wmvo_d = dr("wmvo", [4, 128, 8, 256])
    wmif_d = dr("wmif", [128, 8, 16])
    wck_d = dr("wck", [4, 128, 8, 128])
    wcv_d = dr("wcv", [128, 8, 512])
    wbr_d = dr("wbr", [3, 8, 128, 4, 128])
    wout_d = dr("wout", [8, 128, 8, 128])
    vecs_d = dr("vecs", [128, NV])
    cmat_d = dr("cmat", [128, 6, 128])
    cbf_d = dr("cbf", [128, 6, 128])
    amask_d = dr("amask", [128, 2816])
    rope_d = dr("rope", [128, 2, SEQ])

    ARENA_BYTES = 212480
    with ExitStack() as st:
        arena_t = st.enter_context(nc.sbuf_tensor("arena", [128, ARENA_BYTES // 4], F32))
        ps = st.enter_context(nc.psum_tensor("ps", [128, 8, 512], F32))
        A = Arena(arena_t, ARENA_BYTES)
        S = Sched(nc, same_engine_sync=same_engine_sync)
        import os as _os2
        S.annotate = bool(_os2.environ.get('MK_ANNOTATE'))

        def bank(b, n=512, p0=0, p1=128):
            return ps[p0:p1, b, 0:n]

        def banks(b0, nb, p0=0, p1=128):
            return ps[p0:p1, b0:b0 + nb, :].rearrange("p a b -> p (a b)")

        cmat = A.new(F32, [6, 128])
        cbf = A.new(BF16, [6, 128])
        amask = A.new(BF16, [2816])
        vecs = A.new(F32, [NV])
        dvec = A.new(F32, [8])
        S.dma(out=cmat, in_=cmat_d)
        S.dma(out=vecs, in_=vecs_d)
        S.dma(out=cbf, in_=cbf_d, queue="pool")
        S.dma(out=amask, in_=amask_d, queue="pool")
        ident_f = cmat[:, 0, :]
        tri_le_f, tri_ge_f, psw_f, ones_f = cmat[:, 1, :], cmat[:, 2, :], cmat[:, 3, :], cmat[:, 4, :]
        sel_f = cmat[:, 5, 0:64]
        ident_b, ones_b, blk_b, tri_le_b, tri_ge_b, psw_b = (cbf[:, i, :] for i in range(6))
        S.dve.tensor_scalar(out=dvec[:, 0:1], in0=vecs[:, V_GQ:V_GQ + 1], scalar1=0.125, scalar2=None, op0=ALU.mult)
        S.dve.tensor_scalar(out=dvec[:, 1:2], in0=vecs[:, V_CXQ:V_CXQ + 1], scalar1=float(128 ** -0.5), scalar2=None, op0=ALU.mult)

        xT = A.new(F32, [8, SEQ])
        hT = A.new(BF16, [8, SEQ])
        phase_mark = A.top

        def vcol(c, n=1):
            return vecs[:, c:c + n]

        flip = [0]

        def evac(out, in_):
            flip[0] ^= 1
            if flip[0]:
                S.act.activation(out=out, in_=in_, func=AF.Copy)
            else:
                S.dve.tensor_copy(out=out, in_=in_)

        def rms_rstd(tb, sqrot, r1, rstd, pbank, nfeat_inv=1.0 / D):
            for kc in range(8):
                sq = sqrot.next()
                xs_ = xT[:, kc, tb * 512:(tb + 1) * 512]
                if kc % 4 == 3:
                    S.pool.tensor_tensor(out=sq, in0=xs_, in1=xs_, op=ALU.mult)
                else:
                    S.act.activation(out=sq, in_=xs_, func=AF.Square)
                S.pe.matmul(out=bank(pbank), lhsT=ones_b, rhs=sq, start=(kc == 0), stop=(kc == 7))
            S.act.activation(out=r1, in_=bank(pbank), func=AF.Ln, bias=EPS, scale=nfeat_inv)
            S.act.activation(out=rstd, in_=r1, func=AF.Exp, scale=-0.5)

        def make_hT(gcol):
            S.label = 'norm'
            m = A.top
            sqrot = Rot([A.new(BF16, [512]) for _ in range(4)])
            r1 = A.new(F32, [512])
            rstdr = Rot([A.new(F32, [512]) for _ in range(2)])
            for tb in range(4):
                rstd = rstdr.next()
                rms_rstd(tb, sqrot, r1, rstd, 7)
                for kc in range(8):
                    S.dve.scalar_tensor_tensor(out=hT[:, kc, tb * 512:(tb + 1) * 512],
                                               in0=xT[:, kc, tb * 512:(tb + 1) * 512],
                                               scalar=vcol(gcol + kc), in1=rstd, op0=ALU.mult, op1=ALU.mult)
            A.top = m

        def ffn(wa_d, wb_d, gcol):
            make_hT(gcol)
            S.label = 'ffn'
            m = A.top
            act = A.new(BF16, [11, SEQ])
            wrot = Rot([A.new(BF16, [2, 8, 128]) for _ in range(3)])
            w2rot = Rot([A.new(BF16, [11, 128]) for _ in range(2)])
            sgrot = Rot([A.new(BF16, [1024]) for _ in range(2)])
            setflip = 0
            for half in range(2):
                for jj in range(11):
                    j = half * 11 + jj
                    wb = wrot.next()
                    S.dma(out=wb[:, 0], in_=wa_d[j], queue="pool")
                    S.dma(out=wb[:, 1], in_=wa_d[22 + j], queue="pool")
                    for th in range(2):
                        b0 = 4 * setflip
                        setflip ^= 1
                        for kc in range(8):
                            for gu in range(2):
                                for n in range(2):
                                    t0 = th * 1024 + n * 512
                                    S.pe.matmul(out=bank(b0 + gu * 2 + n), lhsT=wb[:, gu, kc, :],
                                                rhs=hT[:, kc, t0:t0 + 512], start=(kc == 0), stop=(kc == 7))
                        sg = sgrot.next()
                        S.act.activation(out=sg, in_=banks(b0, 2), func=AF.Silu)
                        S.dve.tensor_tensor(out=act[:, jj, th * 1024:(th + 1) * 1024], in0=banks(b0 + 2, 2),
                                            in1=sg, op=ALU.mult)
                for d in range(8):
                    w2 = w2rot.next()
                    S.dma(out=w2, in_=wb_d[d, :, half * 11:(half + 1) * 11, :], queue="pool")
                    b0 = 4 * setflip
                    setflip ^= 1
                    for jj in range(11):
                        for tb in range(4):
                            S.pe.matmul(out=bank(b0 + tb), lhsT=w2[:, jj, :], rhs=act[:, jj, tb * 512:(tb + 1) * 512],
                                        start=(jj == 0), stop=(jj == 10))
                    for tb in range(4):
                        xs = xT[:, d, tb * 512:(tb + 1) * 512]
                        S.dve.scalar_tensor_tensor(out=xs, in0=bank(b0 + tb), scalar=0.5, in1=xs,
                                                   op0=ALU.mult, op1=ALU.add)
            A.top = m

        def load_x(b):
            S.label = 'load'
            m = A.top
            xin = Rot([A.new(F32, [D]) for _ in range(4)])
            for tt in range(16):
                xi = xin.next()
                S.dma(out=xi, in_=x_d[b, tt * 128:(tt + 1) * 128, :])
                for hb in range(2):
                    bk = (tt * 2 + hb) % 4
                    for q in range(4):
                        kc = hb * 4 + q
                        S.pe.transpose(out=ps[:, bk, q * 128:(q + 1) * 128], in_=xi[:, kc * 128:(kc + 1) * 128],
                                       identity=ident_f)
                    evac(xT[:, hb * 4:hb * 4 + 4, tt * 128:(tt + 1) * 128],
                         ps[:, bk, :].rearrange("p (a b) -> p a b", a=4))
            A.top = m

        def final_out(b):
            S.label = 'final'
            m = A.top
            sqrot = Rot([A.new(BF16, [512]) for _ in range(3)])
            r1r_ = Rot([A.new(F32, [512]) for _ in range(2)])
            rstdr_ = Rot([A.new(F32, [512]) for _ in range(2)])
            xnr = Rot([A.new(F32, [8, 512]) for _ in range(2)])
            orot = Rot([A.new(F32, [D]) for _ in range(3)])
            for tb in range(4):
                r1, rstd, xn = r1r_.next(), rstdr_.next(), xnr.next()
                rms_rstd(tb, sqrot, r1, rstd, 6 + tb % 2)
                for kc in range(8):
                    S.dve.scalar_tensor_tensor(out=xn[:, kc, :], in0=xT[:, kc, tb * 512:(tb + 1) * 512],
                                               scalar=vcol(V_NFIN + kc), in1=rstd, op0=ALU.mult, op1=ALU.mult)
                for q in range(4):
                    ot = orot.next()
                    for hb in range(2):
                        bk = (q * 2 + hb) % 4
                        for r in range(4):
                            kc = hb * 4 + r
                            S.pe.transpose(out=ps[:, bk, r * 128:(r + 1) * 128], in_=xn[:, kc, q * 128:(q + 1) * 128],
                                           identity=ident_f)
                        evac(ot[:, hb * 512:(hb + 1) * 512], ps[:, bk, :])
                    t0 = tb * 512 + q * 128
                    S.dma(out=out_d[b, t0:t0 + 128, :], in_=ot, is_output=True)
            A.top = m

        P2 = Rot([0, 2])
        P1 = Rot([4, 5, 6, 7])

        def bankb(bk):
            return ps[:, bk, :].bitcast(BF16)

        def branch_contrib(bi, yT):
            S.label = "contrib"
            m = A.top
            cT = A.new(BF16, [8, SEQ])
            wbr = Rot([A.new(BF16, [4, 128]) for _ in range(2)])
            wgr = Rot([A.new(BF16, [8, 128]) for _ in range(2)])
            wor = Rot([A.new(BF16, [8, 128]) for _ in range(2)])
            grot = Rot([A.new(F32, [1024]) for _ in range(2)])
            sf = 0
            for d in range(8):
                wb = wbr.next()
                S.dma(out=wb, in_=wbr_d[bi, d], queue="pool")
                wg = wgr.next()
                S.dma(out=wg, in_=wst_d[36 + bi * 8 + d], queue="pool")
                for th in range(2):
                    b0 = 4 * sf
                    sf ^= 1
                    for kc in range(4):
                        for n in range(2):
                            t0 = th * 1024 + n * 512
                            S.pe.matmul(out=bank(b0 + n), lhsT=wb[:, kc, :], rhs=yT[:, kc, t0:t0 + 512],
                                        start=(kc == 0), stop=(kc == 3))
                    for kc in range(8):
                        for n in range(2):
                            t0 = th * 1024 + n * 512
                            S.pe.matmul(out=bank(b0 + 2 + n), lhsT=wg[:, kc, :], rhs=hT[:, kc, t0:t0 + 512],
                                        start=(kc == 0), stop=(kc == 7))
                    g = grot.next()
                    S.act.activation(out=g, in_=banks(b0 + 2, 2), func=AF.Sigmoid,
                                     bias=vcol(V_GB + bi * 8 + d), scale=1.0)
                    S.dve.tensor_tensor(out=cT[:, d, th * 1024:(th + 1) * 1024], in0=banks(b0, 2), in1=g, op=ALU.mult)
            for d2 in range(8):
                w = wor.next()
                S.dma(out=w, in_=wout_d[d2], queue="pool")
                b0 = 4 * sf
                sf ^= 1
                for kc in range(8):
                    for tb in range(4):
                        S.pe.matmul(out=bank(b0 + tb), lhsT=w[:, kc, :], rhs=cT[:, kc, tb * 512:(tb + 1) * 512],
                                    start=(kc == 0), stop=(kc == 7))
                for tb in range(4):
                    xs = xT[:, d2, tb * 512:(tb + 1) * 512]
                    S.dve.tensor_tensor(out=xs, in0=bank(b0 + tb), in1=xs, op=ALU.add)
            A.top = m

        def cross_branch(b):
            S.label = 'cx'
            m = A.top
            yT = A.new(BF16, [4, SEQ])
            m2 = A.top
            memt = A.new(F32, [2, D])
            memn = A.new(BF16, [2, D])
            junk = A.new(BF16, [D])
            memnT = A.new(BF16, [8, 256])
            ckT = A.new(BF16, [4, 256])
            cv = A.new(BF16, [2, 512])
            ssm = A.new(F32, [4])
            wcv = A.new(BF16, [8, 512])
            wrot = Rot([A.new(BF16, [8, 128]) for _ in range(2)])
            sqr = Rot([A.new(BF16, [512]) for _ in range(2)])
            r1 = A.new(F32, [512])
            rstd = A.new(F32, [512])
            cqr = Rot([A.new(BF16, [512]) for _ in range(2)])
            ptr = Rot([A.new(BF16, [2, 512]) for _ in range(2)])
            rinv = A.new(F32, [512])
            S.dve.memset(ap=ssm, constant=0.0)
            S.dma(out=wcv, in_=wcv_d, queue="pool")
            for kt in range(2):
                S.dma(out=memt[:, kt, :], in_=mem_d[b, kt * 128:(kt + 1) * 128, :])
            for kt in range(2):
                S.act.activation(out=junk, in_=memt[:, kt, :], func=AF.Square, accum_out=ssm[:, kt:kt + 1])
                S.act.activation(out=ssm[:, 2 + kt:3 + kt], in_=ssm[:, kt:kt + 1], func=AF.Ln, bias=EPS, scale=1.0 / D)
                S.act.activation(out=ssm[:, 2 + kt:3 + kt], in_=ssm[:, 2 + kt:3 + kt], func=AF.Exp, scale=-0.5)
                S.dve.tensor_scalar(out=memn[:, kt, :], in0=memt[:, kt, :], scalar1=ssm[:, 2 + kt:3 + kt], scalar2=None,
                                    op0=ALU.mult)
                for hb in range(2):
                    bk = P1.next()
                    for q in range(4):
                        kc = hb * 4 + q
                        S.pe.transpose(out=bankb(bk)[:, q * 128:(q + 1) * 128], in_=memn[:, kt, kc * 128:(kc + 1) * 128],
                                       identity=ident_b)
                    for q in range(4):
                        kc = hb * 4 + q
                        S.act.activation(out=memnT[:, kc, kt * 128:(kt + 1) * 128], in_=bankb(bk)[:, q * 128:(q + 1) * 128],
                                         func=AF.Copy, scale=vcol(V_NMEM + kc))
            for hd in range(4):
                w = wrot.next()
                S.dma(out=w, in_=wck_d[hd], queue="pool")
                braw, bss = P1.next(), P1.next()
                for kc in range(8):
                    S.pe.matmul(out=bank(braw, 256), lhsT=w[:, kc, :], rhs=memnT[:, kc, :], start=(kc == 0), stop=(kc == 7))
                sq = sqr.next()
                S.act.activation(out=sq[:, 0:256], in_=bank(braw, 256), func=AF.Square)
                S.pe.matmul(out=bank(bss, 256), lhsT=ones_b, rhs=sq[:, 0:256], start=True, stop=True)
                S.act.activation(out=r1[:, 0:256], in_=bank(bss, 256), func=AF.Ln, bias=EPS, scale=1.0 / 128)
                S.act.activation(out=rstd[:, 0:256], in_=r1[:, 0:256], func=AF.Exp, scale=-0.5)
                S.dve.scalar_tensor_tensor(out=ckT[:, hd, :], in0=bank(braw, 256), scalar=vcol(V_CXK), in1=rstd[:, 0:256],
                                           op0=ALU.mult, op1=ALU.mult)
            for kt in range(2):
                bk = P1.next()
                for kc in range(8):
                    S.pe.matmul(out=bank(bk), lhsT=memnT[:, kc, kt * 128:(kt + 1) * 128], rhs=wcv[:, kc, :],
                                start=(kc == 0), stop=(kc == 7))
                evac(cv[:, kt, :], bank(bk))
            its = [(hd, tb) for hd in range(4) for tb in range(4)]
            RB, SB_ = Rot([0, 1]), Rot([2, 3])
            r1r = Rot([r1, A.new(F32, [512])])
            rstdr = Rot([rstd, A.new(F32, [512])])
            rinvr = Rot([rinv, A.new(F32, [512])])
            hold = {}

            def s1(i):
                hd, tb = its[i]
                if tb == 0:
                    w = wrot.next()
                    S.dma(out=w, in_=wst_d[32 + hd], queue="pool")
                    hold["w"] = w
                w = hold["w"]
                braw = RB.next()
                for kc in range(8):
                    S.pe.matmul(out=bank(braw), lhsT=w[:, kc, :], rhs=hT[:, kc, tb * 512:(tb + 1) * 512],
                                start=(kc == 0), stop=(kc == 7))
                sq = sqr.next()
                S.act.activation(out=sq, in_=bank(braw), func=AF.Square)
                hold[i] = {"braw": braw, "sq": sq}

            def s2(i):
                h_ = hold[i]
                bss = SB_.next()
                S.pe.matmul(out=bank(bss), lhsT=ones_b, rhs=h_["sq"], start=True, stop=True)
                r1_, rstd_ = r1r.next(), rstdr.next()
                S.act.activation(out=r1_, in_=bank(bss), func=AF.Ln, bias=EPS, scale=1.0 / 128)
                S.act.activation(out=rstd_, in_=r1_, func=AF.Exp, scale=-0.5)
                cq = cqr.next()
                S.dve.scalar_tensor_tensor(out=cq, in0=bank(h_["braw"]), scalar=dvec[:, 1:2], in1=rstd_,
                                           op0=ALU.mult, op1=ALU.mult)
                h_["cq"] = cq

            def s3(i):
                hd, tb = its[i]
                h_ = hold[i]
                for kt in range(2):
                    S.pe.matmul(out=bank(4 + kt), lhsT=ckT[:, hd, kt * 128:(kt + 1) * 128], rhs=h_["cq"], start=True, stop=True)
                pt = ptr.next()
                S.act.activation(out=pt.rearrange("p a b -> p (a b)"), in_=banks(4, 2), func=AF.Exp)
                h_["pt"] = pt

            def s4(i):
                hd, tb = its[i]
                h_ = hold.pop(i)
                pt = h_["pt"]
                for kt in range(2):
                    S.pe.matmul(out=bank(6), lhsT=cv[:, kt, hd * 128:(hd + 1) * 128], rhs=pt[:, kt, :],
                                start=(kt == 0), stop=(kt == 1))
                for kt in range(2):
                    S.pe.matmul(out=bank(7), lhsT=ones_b, rhs=pt[:, kt, :], start=(kt == 0), stop=(kt == 1))
                r1_, rinv_ = r1r.next(), rinvr.next()
                S.act.activation(out=r1_, in_=bank(7), func=AF.Ln)
                S.act.activation(out=rinv_, in_=r1_, func=AF.Exp, scale=-1.0)
                S.dve.tensor_tensor(out=yT[:, hd, tb * 512:(tb + 1) * 512], in0=bank(6), in1=rinv_, op=ALU.mult)

            n_it = len(its)
            for step in range(n_it + 3):
                if step < n_it:
                    s1(step)
                if 0 <= step - 1 < n_it:
                    s2(step - 1)
                if 0 <= step - 2 < n_it:
                    s3(step - 2)
                if 0 <= step - 3 < n_it:
                    s4(step - 3)
            A.top = m2
            branch_contrib(2, yT)
            A.top = m

        def attention_branch(b):
            m = A.top
            yT = A.new(BF16, [4, SEQ])
            m2 = A.top
            kT = A.new(BF16, [3, SEQ])
            qT = A.new(BF16, [3, SEQ])
            Vt = [A.new(BF16, [16, 2, 66]) for _ in range(3)]
            wav = A.new(BF16, [8, 384])
            wrot = Rot([A.new(BF16, [8, 128]) for _ in range(2)])
            ropet = Rot([A.new(F32, [2, 512]) for _ in range(2)])
            sqr = Rot([A.new(BF16, [512]) for _ in range(2)])
            rgr = Rot([A.new(BF16, [512]) for _ in range(2)])
            r1r = Rot([A.new(F32, [512]) for _ in range(2)])
            rstdr = Rot([A.new(F32, [512]) for _ in range(2)])
            t1r = Rot([A.new(F32, [512]) for _ in range(2)])
            t2r = Rot([A.new(F32, [512]) for _ in range(2)])
            ptr = Rot([A.new(BF16, [1024]) for _ in range(2)])
            ytr = Rot([A.new(BF16, [512]) for _ in range(2)])

            def res_view(ap2d, r):
                return ap2d.rearrange("p (i r) -> p r i", r=r)

            def tok_ap(kc, g, tile):
                if g == 0:
                    return hT[:, kc, tile * 128:(tile + 1) * 128]
                if g == 1:
                    c, u = divmod(tile, 4)
                    return res_view(hT[:, kc, :], 4)[:, c, u * 128:(u + 1) * 128]
                return res_view(hT[:, kc, :], 16)[:, tile, :]

            for p in range(4):
                S.label = 'att_v'
                S.dma(out=wav, in_=wav_d[p], queue="pool")
                vfill = []
                for g in range(3):
                    S.dve.memset(ap=Vt[g], constant=1.0)
                    for t2_ in range(8):
                        def vf(g=g, t2_=t2_):
                            lab = S.label
                            S.label = 'att_v'
                            bk = PA.next()
                            for ti in range(2):
                                tile = t2_ * 2 + ti
                                for kc in range(8):
                                    S.pe.matmul(out=ps[:, bk, ti * 128:(ti + 1) * 128], lhsT=tok_ap(kc, g, tile),
                                                rhs=wav[:, kc, g * 128:(g + 1) * 128], start=(kc == 0), stop=(kc == 7))
                            evac(Vt[g][:, t2_ * 2:(t2_ + 1) * 2, :, 0:64],
                                 ps[:, bk, 0:256].rearrange("p (a s d) -> p a s d", a=2, s=2))
                            S.label = lab
                        vfill.append(vf)
                S.label = 'att_qk'
                PA = Rot(list(range(8)))
                its = [(isq, g, tb) for isq in (0, 1) for g in range(3) for tb in range(4)]
                state = {}

                def stage_a(it):
                    isq, g, tb = it
                    if tb == 0:
                        w = wrot.next()
                        S.dma(out=w, in_=wst_d[(0 if isq else 12) + g * 4 + p], queue="pool")
                        state["w"] = w
                    w = state["w"]
                    rt = ropet.next()
                    S.dma(out=rt, in_=rope_d[:, :, tb * 512:(tb + 1) * 512])
                    braw = PA.next()
                    for kc in range(8):
                        S.pe.matmul(out=bank(braw), lhsT=w[:, kc, :], rhs=hT[:, kc, tb * 512:(tb + 1) * 512],
                                    start=(kc == 0), stop=(kc == 7))
                    sq = sqr.next()
                    rg = rgr.next()
                    gcol = dvec[:, 0:1] if isq else vcol(V_GK)
                    S.act.activation(out=rg, in_=bank(braw), func=AF.Copy, scale=gcol)
                    S.act.activation(out=sq, in_=bank(braw), func=AF.Square)
                    return (it, rt, sq, rg)

                def stage_b(st_):
                    (isq, g, tb), rt, sq, rg = st_
                    dstT = qT if isq else kT
                    bss, bsw = PA.next(), PA.next()
                    S.pe.matmul(out=bank(bss), lhsT=blk_b, rhs=sq, start=True, stop=True)
                    S.pe.matmul(out=bank(bsw), lhsT=psw_b, rhs=rg, start=True, stop=True)
                    r1 = r1r.next()
                    rstd = rstdr.next()
                    t2 = t2r.next()
                    S.act.activation(out=r1, in_=bank(bss), func=AF.Ln, bias=EPS, scale=1.0 / 64)
                    S.act.activation(out=rstd, in_=r1, func=AF.Exp, scale=-0.5)
                    t1 = t1r.next()
                    S.pool.tensor_tensor(out=t1, in0=rg, in1=rt[:, 0, :], op=ALU.mult)
                    S.dve.tensor_tensor(out=t2, in0=bank(bsw), in1=rt[:, 1, :], op=ALU.mult)
                    S.dve.tensor_tensor(out=t2, in0=t2, in1=t1, op=ALU.add)
                    S.dve.tensor_tensor(out=dstT[:, g, tb * 512:(tb + 1) * 512], in0=t2, in1=rstd, op=ALU.mult)

                prev = None
                for it in its:
                    cur = stage_a(it)
                    if prev is not None:
                        stage_b(prev)
                        if vfill:
                            vfill.pop(0)()
                    prev = cur
                stage_b(prev)
                while vfill:
                    vfill.pop(0)()
                S.label = 'att_core'
                batches = []
                for sl in range(2):
                    for w_ in range(4):
                        ctx = {"first": True, "OB": None}

                        def mk_pv(ctx):
                            def pv(out_fn, vt_ap, pt_ap):
                                if ctx["OB"] is None:
                                    ctx["OB"] = P1.next()
                                S.pe.matmul(out=out_fn(ctx["OB"]), lhsT=vt_ap, rhs=pt_ap, start=ctx["first"], stop=False,
                                            skip_group_check=True)
                                ctx["first"] = False
                            return pv

                        pv = mk_pv(ctx)
                        pb = 64 * sl
                        kq = (lambda pb: (lambda T, g: T[pb:pb + 64, g, :]))(pb)
                        for g in range(2):
                            for bi in range(2):
                                def qk_fn(g=g, bi=bi, sl=sl, w_=w_, kq=kq):
                                    b2 = P2.next()
                                    pt = ptr.next()
                                    st2 = banks(b2, 2)
                                    S.pe.matmul(out=bank(b2), lhsT=ident_b, rhs=amask[:, 0:512], start=True, stop=False,
                                                skip_group_check=True)
                                    S.pe.matmul(out=bank(b2 + 1, 256), lhsT=ident_b, rhs=amask[:, 512:768], start=True,
                                                stop=False, skip_group_check=True)
                                    blocks = []
                                    for qi in range(2):
                                        if g == 0:
                                            t = 4 * w_ + 2 * bi + qi
                                            q_ap = kq(qT, 0)[:, t * 128:(t + 1) * 128]
                                            ntile = 16
                                            out_fn = (lambda t: (lambda OB: ps[0:65, OB, (t % 4) * 128:(t % 4 + 1) * 128]))(t)
                                        else:
                                            c = 2 * bi + qi
                                            t = w_
                                            q_ap = res_view(kq(qT, 1), 4)[:, c, t * 128:(t + 1) * 128]
                                            ntile = 4
                                            out_fn = (lambda c: (lambda OB: res_view(ps[0:65, OB, :], 4)[:, c, :]))(c)
                                        for ki, kt in enumerate((t - 1, t, t + 1)):
                                            if not (0 <= kt < ntile):
                                                continue
                                            col = (qi * 3 + ki) * 128
                                            if g == 0:
                                                k_ap = kq(kT, 0)[:, kt * 128:(kt + 1) * 128]
                                                v_ap = Vt[0][:, kt, sl, 0:65]
                                            else:
                                                k_ap = res_view(kq(kT, 1), 4)[:, c, kt * 128:(kt + 1) * 128]
                                                v_ap = Vt[1][:, c * 4 + kt, sl, 0:65]
                                            S.pe.matmul(out=st2[:, col:col + 128], lhsT=k_ap, rhs=q_ap, start=False, stop=False,
                                                        skip_group_check=True)
                                            blocks.append((col, v_ap, out_fn))
                                    S.act.activation(out=pt[:, 0:768], in_=st2[:, 0:768], func=AF.Exp)
                                    return (pt, blocks)

                                def pv_fn(res, pv=pv):
                                    pt, blocks = res
                                    for col, v_ap, out_fn in blocks:
                                        pv(out_fn, v_ap, pt[:, col:col + 128])

                                batches.append((qk_fn, pv_fn, None))

                        def qk2_fn(sl=sl, w_=w_, kq=kq):
                            b2 = P2.next()
                            pt = ptr.next()
                            S.pe.matmul(out=bank(b2), lhsT=ident_b, rhs=amask[:, 768 + w_ * 512:768 + (w_ + 1) * 512],
                                        start=True, stop=False, skip_group_check=True)
                            for c in range(16):
                                S.pe.matmul(out=ps[:, b2, c * 32:(c + 1) * 32], lhsT=res_view(kq(kT, 2), 16)[:, c, :],
                                            rhs=res_view(kq(qT, 2), 16)[:, c, w_ * 32:(w_ + 1) * 32], start=False, stop=False,
                                            skip_group_check=True)
                            S.act.activation(out=pt[:, 0:512], in_=bank(b2), func=AF.Exp)
                            return pt

                        def pv2_fn(pt, sl=sl, pv=pv):
                            for c in range(16):
                                pv((lambda c: (lambda OB: res_view(ps[0:65, OB, :], 16)[:, c, :]))(c), Vt[2][:, c, sl, 0:65],
                                   pt[:, c * 32:(c + 1) * 32])

                        def tail_fn(sl=sl, w_=w_, ctx=ctx):
                            OB = ctx["OB"]
                            acc = t2r.next()
                            S.act.activation(out=acc[0:65, :], in_=ps[0:65, OB, :], func=AF.Copy)

                            def tail_b():
                                bl = P1.next()
                                S.pe.matmul(out=ps[0:64, bl, :], lhsT=sel_f[0:65, :], rhs=acc[0:65, :], start=True, stop=True)
                                r1 = r1r.next()
                                rinv = rstdr.next()
                                S.act.activation(out=r1[0:64, :], in_=ps[0:64, bl, :], func=AF.Ln)
                                S.act.activation(out=rinv[0:64, :], in_=r1[0:64, :], func=AF.Exp, scale=-1.0)
                                if sl == 0:
                                    S.dve.tensor_tensor(out=yT[0:64, p, w_ * 512:(w_ + 1) * 512], in0=acc[0:64, :],
                                                        in1=rinv[0:64, :], op=ALU.mult)
                                else:
                                    yt = ytr.next()
                                    S.dve.tensor_tensor(out=yt[0:64, :], in0=acc[0:64, :], in1=rinv[0:64, :], op=ALU.mult)
                                    S.dma(out=yT[64:128, p, w_ * 512:(w_ + 1) * 512], in_=yt[0:64, :])
                            return tail_b

                        batches.append((qk2_fn, pv2_fn, tail_fn))
                prev = None
                pend = None
                for qk_fn, pv_fn, tail_fn in batches:
                    res = qk_fn()
                    if pend is not None:
                        pend()
                        pend = None
                    if prev is not None:
                        prev[0](prev[1])
                        if prev[2] is not None:
                            pend = prev[2]()
                    prev = (pv_fn, res, tail_fn)
                if pend is not None:
                    pend()
                prev[0](prev[1])
                prev[2]()()
            A.top = m2
            branch_contrib(0, yT)
            A.top = m

        LNA = float(-0.5 * np.log(128.0))

        def mlstm_branch(b):
            m = A.top
            yT = A.new(BF16, [4, SEQ])
            m2 = A.top
            G = A.new(F32, [16, 16])
            SP = A.new(F32, [16, 8])
            CUM = A.new(F32, [16, 8])
            EU = A.new(F32, [16, 8])
            EB = A.new(F32, [16, 8])
            EG = A.new(F32, [16, 8])
            wmif = A.new(BF16, [8, 16])
            S.label = 'ml_gates'
            S.dma(out=wmif, in_=wmif_d, queue="pool")
            bk = P1.next()
            for tt in range(16):
                for kc in range(8):
                    S.pe.matmul(out=ps[:, bk, tt * 16:(tt + 1) * 16], lhsT=hT[:, kc, tt * 128:(tt + 1) * 128],
                                rhs=wmif[:, kc, :], start=(kc == 0), stop=(kc == 7))
            S.dve.tensor_tensor(out=G, in0=ps[:, bk, 0:256].rearrange("p (t g) -> p t g", g=16),
                                in1=vecs[:, V_MGB:V_MGB + 16].unsqueeze(1).to_broadcast([128, 16, 16]), op=ALU.add)
            for d_ in range(2):
                S.act.activation(out=SP[:, :, d_ * 4:d_ * 4 + 4], in_=G[:, :, 4 + d_ * 8:8 + d_ * 8], func=AF.Exp, scale=-1.0)
            S.act.activation(out=SP, in_=SP, func=AF.Ln, bias=1.0)
            bkc, bkc2, bkg = P1.next(), P1.next(), P1.next()
            SPf = SP.rearrange("p t g -> p (t g)")
            S.pe.matmul(out=ps[:, bkc, 0:128], lhsT=tri_le_f, rhs=SPf, start=True, stop=True)
            S.pe.matmul(out=ps[:, bkc2, 0:128], lhsT=tri_ge_f, rhs=SPf, start=True, stop=True)
            S.pe.matmul(out=ps[:, bkg, 0:128], lhsT=ones_f, rhs=SPf, start=True, stop=True)
            pcf = ps[:, bkc, 0:128].rearrange("p (t g) -> p t g", g=8)
            pcb = ps[:, bkc2, 0:128].rearrange("p (t g) -> p t g", g=8)
            pg = ps[:, bkg, 0:128].rearrange("p (t g) -> p t g", g=8)
            S.dve.tensor_copy(out=CUM[:, :, 0:4], in_=pcf[:, :, 0:4])
            S.dve.tensor_copy(out=CUM[:, :, 4:8], in_=pcb[:, :, 4:8])
            S.act.activation(out=EB, in_=CUM, func=AF.Exp, scale=-1.0)
            S.act.activation(out=EG, in_=pg, func=AF.Exp, scale=-1.0)
            for d_ in range(2):
                S.dve.tensor_tensor(out=EU[:, :, d_ * 4:d_ * 4 + 4], in0=G[:, :, d_ * 8:d_ * 8 + 4],
                                    in1=CUM[:, :, d_ * 4:d_ * 4 + 4], op=ALU.add)
            S.act.activation(out=EU, in_=EU, func=AF.Exp, bias=LNA)
            qkbuf = [(A.new(BF16, [SEQ]), A.new(BF16, [SEQ])) for _ in range(2)]
            ktok = A.new(BF16, [16, 128])
            vdir = [A.new(BF16, [16, 130]) for _ in range(2)]
            gs = A.new(BF16, [16, 128])
            roff = A.alloc(24960)
            STf = [A.view(roff + d_ * 8320, F32, [16, 130]) for d_ in range(2)]
            STb = [A.view(roff + 16640 + d_ * 4160, BF16, [16, 130]) for d_ in range(2)]
            hmall = A.view(roff, F32, [16, 128])
            ytall = A.view(roff + 8320, BF16, [16, 128])
            cbuf = A.new(BF16, [SEQ + 4])
            dg = A.new(BF16, [10, 128])
            soff = A.alloc(8192)
            smb = A.view(soff, BF16, [2, 16, 128])
            hsq = A.view(soff, F32, [16, 128])
            wq = A.new(BF16, [8, 128])
            wmv = A.new(BF16, [8, 256])
            tmpr = Rot([A.new(F32, [128]) for _ in range(2)])
            SC = A.new(F32, [2, 16])
            sct = A.new(F32, [4, 16])
            stat = A.new(F32, [32])
            PB = Rot(list(range(8)))
            S.dve.memset(ap=cbuf[:, 0:2], constant=0.0)
            S.dve.memset(ap=cbuf[:, SEQ + 2:SEQ + 4], constant=0.0)

            def gen_qk(hd):
                S.label = 'ml_qk'
                qTm, kTm = qkbuf[hd % 2]
                for isk in (0, 1):
                    ch = isk * 4 + hd
                    for j in range(5):
                        S.dve.tensor_scalar(out=dg[:, isk * 5 + j, :], in0=ident_f, scalar1=vcol(V_CW + ch * 5 + j), scalar2=None,
                                            op0=ALU.mult)
                for isk in (0, 1):
                    ch = isk * 4 + hd
                    S.dma(out=wq, in_=wst_d[24 + isk * 4 + hd], queue="pool")
                    for tb in range(4):
                        bk = PB.next()
                        for kc in range(8):
                            S.pe.matmul(out=bank(bk), lhsT=wq[:, kc, :], rhs=hT[:, kc, tb * 512:(tb + 1) * 512],
                                        start=(kc == 0), stop=(kc == 7))
                        evac(cbuf[:, 2 + tb * 512:2 + (tb + 1) * 512], bank(bk))
                    for tb in range(4):
                        bk = PB.next()
                        for j in range(5):
                            S.pe.matmul(out=bank(bk), lhsT=dg[:, isk * 5 + j, :], rhs=cbuf[:, tb * 512 + j:tb * 512 + j + 512],
                                        start=(j == 0), stop=(j == 4))
                        S.act.activation(out=(kTm if isk else qTm)[:, tb * 512:(tb + 1) * 512], in_=bank(bk), func=AF.Silu,
                                         bias=vcol(V_CB + ch), scale=1.0)

            gen_qk(0)
            for hd in range(4):
                qTm, kTm = qkbuf[hd % 2]
                S.label = 'ml_v'
                for t4 in range(4):
                    bk = PB.next()
                    for ti in range(4):
                        tile = t4 * 4 + ti
                        S.pe.transpose(out=bankb(bk)[:, ti * 128:(ti + 1) * 128], in_=kTm[:, tile * 128:(tile + 1) * 128],
                                       identity=ident_b)
                    evac(ktok[:, t4 * 4:(t4 + 1) * 4, :], bankb(bk)[:, 0:512].rearrange("p (a d) -> p a d", a=4))
                S.dma(out=wmv, in_=wmvo_d[hd], queue="pool")
                for t2 in range(8):
                    bk = PB.next()
                    for ti in range(2):
                        tt = t2 * 2 + ti
                        for kc in range(8):
                            S.pe.matmul(out=ps[:, bk, ti * 256:(ti + 1) * 256], lhsT=hT[:, kc, tt * 128:(tt + 1) * 128],
                                        rhs=wmv[:, kc, :], start=(kc == 0), stop=(kc == 7))
                    for ti in range(2):
                        tt = t2 * 2 + ti
                        S.act.activation(out=gs[:, tt, :], in_=ps[:, bk, ti * 256 + 128:(ti + 1) * 256], func=AF.Sigmoid)
                        S.dve.tensor_scalar(out=vdir[0][:, tt, 0:128], in0=ps[:, bk, ti * 256:ti * 256 + 128],
                                            scalar1=EU[:, tt, hd:hd + 1], scalar2=None, op0=ALU.mult)
                        S.act.activation(out=vdir[1][:, tt, 0:128], in_=ps[:, bk, ti * 256:ti * 256 + 128], func=AF.Copy,
                                         scale=EU[:, tt, 4 + hd:5 + hd])
                for d_ in range(2):
                    S.dve.tensor_copy(out=vdir[d_][:, :, 128:129], in_=EU[:, :, d_ * 4 + hd:d_ * 4 + hd + 1])
                S.label = 'ml_rec'
                for d_ in range(2):
                    col = d_ * 4 + hd
                    for c in range(16):
                        if (d_ == 0 and c == 15) or (d_ == 1 and c == 0):
                            continue
                        bU = PB.next()
                        S.pe.matmul(out=ps[:, bU, 0:129], lhsT=ktok[:, c, :], rhs=vdir[d_][:, c, 0:129], start=True, stop=True)
                        S.act.activation(out=STf[d_][:, c, 0:129], in_=ps[:, bU, 0:129], func=AF.Copy,
                                         scale=EG[:, c, col:col + 1])
                for c in range(16):
                    cs = slice(c * 128, (c + 1) * 128)
                    bST = PB.next()
                    S.pe.matmul(out=ps[:, bST, 0:128], lhsT=kTm[:, cs], rhs=qTm[:, cs], start=True, stop=True)
                    S.dve.tensor_tensor(out=smb[:, 0, c, :], in0=ps[:, bST, 0:128], in1=tri_le_b, op=ALU.mult)
                    S.dve.tensor_tensor(out=smb[:, 1, c, :], in0=ps[:, bST, 0:128], in1=tri_ge_b, op=ALU.mult)
                if hd + 1 < 4:
                    gen_qk(hd + 1)
                    S.label = 'ml_rec'
                S.strict = True
                for c in range(1, 15):
                    S.dve.scalar_tensor_tensor(out=STf[0][:, c, 0:129], in0=STf[0][:, c - 1, 0:129], scalar=EG[:, c, hd:hd + 1],
                                               in1=STf[0][:, c, 0:129], op0=ALU.mult, op1=ALU.add)
                    cb = 15 - c
                    S.dve.scalar_tensor_tensor(out=STf[1][:, cb, 0:129], in0=STf[1][:, cb + 1, 0:129],
                                               scalar=EG[:, cb, 4 + hd:5 + hd], in1=STf[1][:, cb, 0:129],
                                               op0=ALU.mult, op1=ALU.add)
                S.strict = False
                S.act.activation(out=STb[0][:, 0:15, 0:129], in_=STf[0][:, 0:15, 0:129], func=AF.Copy)
                S.act.activation(out=STb[1][:, 1:16, 0:129], in_=STf[1][:, 1:16, 0:129], func=AF.Copy)
                bD = PB.next()
                for c in range(16):
                    cs = slice(c * 128, (c + 1) * 128)
                    for d_ in range(2):
                        cp = c - 1 if d_ == 0 else c + 1
                        has = 0 <= cp < 16
                        j = d_ * 16 + c
                        S.pe.matmul(out=ps[:, bD, j:j + 1], lhsT=smb[:, d_, c, :], rhs=vdir[d_][:, c, 128:129],
                                    start=True, stop=(not has))
                        if has:
                            S.pe.matmul(out=ps[:, bD, j:j + 1], lhsT=qTm[:, cs], rhs=STb[d_][:, cp, 128:129],
                                        start=False, stop=True)
                for d_ in range(2):
                    col = d_ * 4 + hd
                    ebv = EB[:, :, col]
                    S.dve.tensor_tensor(out=sct[:, 0, :], in0=ps[:, bD, d_ * 16:(d_ + 1) * 16], in1=ebv, op=ALU.mult)
                    S.dve.scalar_tensor_tensor(out=sct[:, 1, :], in0=sct[:, 0, :], scalar=-1.0, in1=sct[:, 0, :],
                                               op0=ALU.mult, op1=ALU.max)
                    S.dve.tensor_scalar(out=sct[:, 1, :], in0=sct[:, 1, :], scalar1=1.0, scalar2=None, op0=ALU.max)
                    S.dve.reciprocal(out=sct[:, 2, :], in_=sct[:, 1, :])
                    S.dve.tensor_tensor(out=SC[:, d_, :], in0=sct[:, 2, :], in1=ebv, op=ALU.mult)
                for c in range(16):
                    cs = slice(c * 128, (c + 1) * 128)
                    bN = [PB.next(), PB.next()]
                    if bD in bN:
                        bN = [PB.next(), PB.next()]
                    for d_ in range(2):
                        cp = c - 1 if d_ == 0 else c + 1
                        has = 0 <= cp < 16
                        S.pe.matmul(out=ps[:, bN[d_], 0:128], lhsT=smb[:, d_, c, :], rhs=vdir[d_][:, c, 0:128],
                                    start=True, stop=(not has))
                        if has:
                            S.pe.matmul(out=ps[:, bN[d_], 0:128], lhsT=qTm[:, cs], rhs=STb[d_][:, cp, 0:128],
                                        start=False, stop=True)
                    tmp = tmpr.next()
                    S.act.activation(out=tmp, in_=ps[:, bN[0], 0:128], func=AF.Copy, scale=SC[:, 0, c:c + 1])
                    S.dve.scalar_tensor_tensor(out=hmall[:, c, :], in0=ps[:, bN[1], 0:128], scalar=SC[:, 1, c:c + 1],
                                               in1=tmp, op0=ALU.mult, op1=ALU.add)
                S.dve.tensor_tensor(out=hsq, in0=hmall, in1=hmall, op=ALU.mult)
                S.dve.tensor_reduce(out=stat[:, 0:16], in_=hsq, axis=AX.X, op=ALU.add)
                S.act.activation(out=stat[:, 16:32], in_=stat[:, 0:16], func=AF.Ln, bias=EPS, scale=1.0 / 128)
                S.act.activation(out=stat[:, 16:32], in_=stat[:, 16:32], func=AF.Exp, scale=-0.5)
                S.dve.tensor_tensor(out=hmall, in0=hmall, in1=stat[:, 16:32].unsqueeze(2).to_broadcast([128, 16, 128]),
                                    op=ALU.mult)
                S.dve.tensor_tensor(out=ytall, in0=hmall, in1=gs, op=ALU.mult)
                for t4 in range(4):
                    bk = PB.next()
                    for ti in range(4):
                        c = t4 * 4 + ti
                        S.pe.transpose(out=bankb(bk)[:, ti * 128:(ti + 1) * 128], in_=ytall[:, c, :], identity=ident_b)
                    S.act.activation(out=yT[:, hd, t4 * 512:(t4 + 1) * 512], in_=bankb(bk)[:, 0:512], func=AF.Copy,
                                     scale=vcol(V_MLG + hd))
            A.top = m2
            branch_contrib(1, yT)
            A.top = m

        for b in range(nseq):
            load_x(b)
            if "ffn1" in stages:
                ffn(w1a_d, w1b_d, V_NF1)
            if any(s in stages for s in ("att", "ml", "cx")):
                make_hT(V_NMIX)
                if "att" in stages:
                    attention_branch(b)
                if "ml" in stages:
                    mlstm_branch(b)
                if "cx" in stages:
                    cross_branch(b)
            if "ffn2" in stages:
                ffn(w2a_d, w2b_d, V_NF2)
            final_out(b)
        run_sched(nc, S)
        build.last_sched = S
    return nc


_CACHE = {}


def _get_nc(nseq, stages, same=True):
    key = (nseq, tuple(stages), same)
    if key not in _CACHE:
        _CACHE[key] = build(nseq, stages, same)
    return _CACHE[key]


ALL_STAGES = ("ffn1", "att", "ml", "cx", "ffn2")


def run_cores(inputs, cores, nseq=2, stages=ALL_STAGES, same=True, trace=False):
    L = host_layout(inputs)
    L.update(host_consts())
    x = np.asarray(inputs["x"], np.float32)
    mem = np.asarray(inputs["mem"], np.float32)
    nc = _get_nc(nseq, stages, same)
    in_maps = []
    for c in cores:
        m = dict(L)
        m["x"] = np.ascontiguousarray(x[c * 2:c * 2 + nseq])
        m["mem"] = np.ascontiguousarray(mem[c * 2:c * 2 + nseq])
        in_maps.append(m)
    res = run_bass_kernel_spmd(nc, in_maps, core_ids=list(range(len(cores))), trace=trace)
    return res


def kernel(**inputs):
    res = run_cores(inputs, list(range(NCORES)))
    out = np.concatenate([np.asarray(r["out"], np.float32) for r in res.results], axis=0)
    return out
```
